# Optimizing a Trainium2 kernel written in Bass

```python
import math
import jax, jax.numpy as jnp
from jax import lax
import numpy as np

D_MODEL = 1024
BATCH = 8
SEQ = 4096
DEPTH = 2
DEC_BATCH = 32
DEC_SEQ = 16
PAST_LEN = 4096

CHUNK = 64
N_META = 16
Q_BLOCK = 128
EPS = 1e-6
N_AB = (DEPTH + 1) // 2
N_CD = DEPTH // 2
MIX_WIDTH = D_MODEL
S5_WIDTH = D_MODEL // 2
S5_GROUP = 16
G_A = S5_WIDTH // S5_GROUP
P_A = 64
DH_B = 64
H_B = D_MODEL // 256
DV_B = 2 * DH_B
DIFF_QK = H_B * 2 * DH_B
DIFF_V = H_B * DV_B
DH_C = 64
H_C = D_MODEL // 128
FOX_W = H_C * DH_C
FOX_BIAS = 3.0
H_D = D_MODEL // 128
NOPE_D = 64
ROPE_D = 32
V_D = 64
Q_LORA = 3 * D_MODEL // 8
KV_LORA = D_MODEL // 4
ROPE_THETA = 10000.0
D_FF = 256 * ((8 * D_MODEL // 3 + 255) // 256)
AB_IN = S5_WIDTH + 2 * DIFF_QK + DIFF_V
AB_SPLITS = (S5_WIDTH, S5_WIDTH + DIFF_QK, S5_WIDTH + 2 * DIFF_QK)
CD_IN = 3 * FOX_W + H_C + Q_LORA + KV_LORA + ROPE_D
CD_SPLITS = (FOX_W, 2 * FOX_W, 3 * FOX_W, 3 * FOX_W + H_C, 3 * FOX_W + H_C + Q_LORA, 3 * FOX_W + H_C + Q_LORA + KV_LORA)

kernel_name = 'hybrid_streaming_encoder_step'


def _rms(x, g):
    xf = x.astype(jnp.float32)
    y = xf * lax.rsqrt(jnp.mean(xf * xf, axis=-1, keepdims=True) + EPS)
    return (y * g.astype(jnp.float32)).astype(x.dtype)


def _swiglu(h, w_in, w_out):
    gate, up = jnp.split(h @ w_in, 2, axis=-1)
    return (jax.nn.silu(gate) * up) @ w_out


def _rope(x, pos):
    half = x.shape[-1] // 2
    inv = ROPE_THETA ** (-jnp.arange(half, dtype=jnp.float32) / half)
    ang = pos.astype(jnp.float32)[:, None] * inv[None, :]
    cos = jnp.cos(ang)[None, :, None, :]
    sin = jnp.sin(ang)[None, :, None, :]
    xf = x.astype(jnp.float32)
    x1, x2 = xf[..., :half], xf[..., half:]
    return jnp.concatenate([x1 * cos - x2 * sin, x2 * cos + x1 * sin], axis=-1).astype(x.dtype)


def _prompt_chunk(pos):
    return jnp.where(pos < N_META, 0, 1 + (pos - N_META) // CHUNK)


def _frame_chunk(pos):
    return 1 + pos // CHUNK


def _sweep_prompt(fn, *qs):
    n = qs[0].shape[1]
    pad = (-n) % Q_BLOCK
    nb = (n + pad) // Q_BLOCK

    def split(a):
        a = jnp.pad(a, [(0, 0), (0, pad)] + [(0, 0)] * (a.ndim - 2))
        return jnp.moveaxis(a.reshape((a.shape[0], nb, Q_BLOCK) + a.shape[2:]), 1, 0)

    qpos = jnp.arange(nb * Q_BLOCK, dtype=jnp.int32).reshape(nb, Q_BLOCK)
    out = lax.map(lambda args: fn(args[0], *args[1:]), (qpos, *[split(a) for a in qs]))
    out = jnp.moveaxis(out, 0, 1)
    return out.reshape((out.shape[0], nb * Q_BLOCK) + out.shape[3:])[:, :n]


def _masked_softmax_av(s, mask, v):
    p = jax.nn.softmax(jnp.where(mask, s, -jnp.inf), axis=-1)
    return jnp.einsum('bhqk,bkhd->bqhd', p, v.astype(jnp.float32))


def _lin_rec(left, right):
    a1, b1 = left
    a2, b2 = right
    return a1 * a2, a2 * b1 + b2


def _s5(u, a_re, a_im, log_step, b_re, b_im, c_re, c_im, d, glu_w, glu_b, h0):
    bsz, t, _ = u.shape
    f32 = jnp.float32
    uf = u.astype(f32).reshape(bsz, t, G_A, S5_GROUP)
    lam = lax.complex(a_re.astype(f32), a_im.astype(f32))
    lam_bar = jnp.exp(lam * jnp.exp(log_step.astype(f32))[:, None])
    b_bar = ((lam_bar - 1.0) / lam)[..., None] * lax.complex(b_re.astype(f32), b_im.astype(f32))
    bu = jnp.einsum('gpc,btgc->btgp', b_bar, uf.astype(b_bar.dtype))
    if h0 is not None:
        bu = bu.at[:, 0].add(lam_bar * h0)
    _, hs = lax.associative_scan(_lin_rec, (jnp.broadcast_to(lam_bar, bu.shape), bu), axis=1)
    c = lax.complex(c_re.astype(f32), c_im.astype(f32))
    y = jnp.einsum('gcp,btgp->btgc', c, hs).real + d.astype(f32).reshape(G_A, S5_GROUP) * uf
    g = jax.nn.gelu(y.reshape(bsz, t, S5_WIDTH))
    out = g * jax.nn.sigmoid(g @ glu_w.astype(f32) + glu_b.astype(f32))
    return out.astype(u.dtype), hs[:, -1].real, hs[:, -1].imag


def _diff_core(q, k, v, lam, slopes, qpos, qchk, kpos, kchk):
    s = jnp.einsum('bqhmd,bkhmd->bhmqk', q, k, preferred_element_type=jnp.float32) * (DH_B ** -0.5)
    dist = jnp.abs(qpos[:, None] - kpos[None, :]).astype(jnp.float32)
    s = s - slopes[:, None, None, None] * dist
    mask = kchk[None, :] <= qchk[:, None]
    p = jax.nn.softmax(jnp.where(mask, s, -jnp.inf), axis=-1)
    w = p[:, :, 0] - lam * p[:, :, 1]
    return jnp.einsum('bhqk,bkhd->bqhd', w, v.astype(jnp.float32))


def _fox_core(q, k, v, fq, fk, qpos, kpos):
    s = jnp.einsum('bqhd,bkhd->bhqk', q, k, preferred_element_type=jnp.float32) * (DH_C ** -0.5)
    s = s + jnp.swapaxes(fq, 1, 2)[..., :, None] - jnp.swapaxes(fk, 1, 2)[..., None, :]
    return _masked_softmax_av(s, kpos[None, :] <= qpos[:, None], v)


def _mla_core(q, k, v, qchk, kchk):
    s = jnp.einsum('bqhd,bkhd->bhqk', q, k, preferred_element_type=jnp.float32) * ((NOPE_D + ROPE_D) ** -0.5)
    return _masked_softmax_av(s, kchk[None, :] <= qchk[:, None], v)


def _ab_mixer(h, w_in, w_out, a_re, a_im, log_step, b_re, b_im, c_re, c_im, d, glu_w, glu_b,
              q_norm, k_norm, lam_vecs, sub_norm, layer, past):
    bsz, t, _ = h.shape
    f32 = jnp.float32
    u, q, k, v = jnp.split(h @ w_in, AB_SPLITS, axis=-1)
    h0 = None if past is None else lax.complex(past[0].astype(f32), past[1].astype(f32))
    y_s5, s_re, s_im = _s5(u, a_re, a_im, log_step, b_re, b_im, c_re, c_im, d, glu_w, glu_b, h0)
    q = _rms(q.reshape(bsz, t, H_B, 2, DH_B), q_norm)
    k = _rms(k.reshape(bsz, t, H_B, 2, DH_B), k_norm)
    v = v.reshape(bsz, t, H_B, DV_B)
    lv = lam_vecs.astype(f32)
    lam_init = 0.8 - 0.6 * math.exp(-0.3 * layer)
    lam = jnp.exp(jnp.sum(lv[0] * lv[1])) - jnp.exp(jnp.sum(lv[2] * lv[3])) + lam_init
    slopes = jnp.exp2(-8.0 * jnp.arange(1, H_B + 1, dtype=f32) / H_B)
    if past is None:
        kpos = jnp.arange(t, dtype=jnp.int32)
        kchk = _prompt_chunk(kpos)
        o = _sweep_prompt(lambda qp, qb: _diff_core(qb, k, v, lam, slopes, qp, _prompt_chunk(qp), kpos, kchk), q)
    else:
        p_len = past[2].shape[1]
        k_all = jnp.concatenate([past[2].reshape(bsz, p_len, H_B, 2, DH_B).astype(k.dtype), k], axis=1)
        v_all = jnp.concatenate([past[3].astype(v.dtype), v], axis=1)
        kpos = jnp.arange(p_len + t, dtype=jnp.int32)
        qpos = p_len + jnp.arange(t, dtype=jnp.int32)
        o = _diff_core(q, k_all, v_all, lam, slopes, qpos, _frame_chunk(qpos), kpos, _frame_chunk(kpos))
    o = (_rms(o, sub_norm) * (1.0 - lam_init)).astype(h.dtype).reshape(bsz, t, DIFF_V)
    out = jnp.concatenate([y_s5, o], axis=-1) @ w_out
    return out, (s_re, s_im, k.reshape(bsz, t, H_B, 2 * DH_B), v)


def _cd_mixer(h, w_in, w_out, fq_norm, fk_norm, f_bias, q_a_norm, q_b, kv_a_norm, kv_b, mq_norm, mk_norm, past):
    bsz, t, _ = h.shape
    f32 = jnp.float32
    fq, fk, fv, fg, qa, kva, kpe_raw = jnp.split(h @ w_in, CD_SPLITS, axis=-1)
    fq = _rms(fq.reshape(bsz, t, H_C, DH_C), fq_norm)
    fk = _rms(fk.reshape(bsz, t, H_C, DH_C), fk_norm)
    fv = fv.reshape(bsz, t, H_C, DH_C)
    logf = jax.nn.log_sigmoid(fg.astype(f32) + f_bias.astype(f32))
    qh = (_rms(qa, q_a_norm) @ q_b).reshape(bsz, t, H_D, NOPE_D + ROPE_D)
    ckv = _rms(kva, kv_a_norm)
    if past is None:
        qpos = jnp.arange(t, dtype=jnp.int32)
        kpos = qpos
        kchk = _prompt_chunk(kpos)
    else:
        p_len = past[0].shape[1]
        qpos = p_len + jnp.arange(t, dtype=jnp.int32)
        kpos = jnp.arange(p_len + t, dtype=jnp.int32)
        kchk = _frame_chunk(kpos)
    kpe = _rope(kpe_raw[:, :, None, :], qpos)[:, :, 0]
    if past is None:
        fk_all, fv_all, logf_all, ckv_all, kpe_all = fk, fv, logf, ckv, kpe
    else:
        fk_all = jnp.concatenate([past[0].astype(fk.dtype), fk], axis=1)
        fv_all = jnp.concatenate([past[1].astype(fv.dtype), fv], axis=1)
        logf_all = jnp.concatenate([past[2].astype(f32), logf], axis=1)
        ckv_all = jnp.concatenate([past[3].astype(ckv.dtype), ckv], axis=1)
        kpe_all = jnp.concatenate([past[4].astype(kpe.dtype), kpe], axis=1)
    tk = ckv_all.shape[1]
    q_pe = _rope(qh[..., NOPE_D:], qpos)
    qm = _rms(jnp.concatenate([qh[..., :NOPE_D], q_pe], axis=-1), mq_norm)
    kv = (ckv_all @ kv_b).reshape(bsz, tk, H_D, NOPE_D + V_D)
    km = _rms(jnp.concatenate([kv[..., :NOPE_D], jnp.broadcast_to(kpe_all[:, :, None, :], (bsz, tk, H_D, ROPE_D))], axis=-1), mk_norm)
    vm = kv[..., NOPE_D:]
    f_cum = jnp.cumsum(logf_all, axis=1)
    f_q = f_cum[:, tk - t:]
    if past is None:
        o_c = _sweep_prompt(lambda qp, qb, fb: _fox_core(qb, fk_all, fv_all, fb, f_cum, qp, kpos), fq, f_q)
        o_d = _sweep_prompt(lambda qp, qb: _mla_core(qb, km, vm, _prompt_chunk(qp), kchk), qm)
    else:
        o_c = _fox_core(fq, fk_all, fv_all, f_q, f_cum, qpos, kpos)
        o_d = _mla_core(qm, km, vm, _frame_chunk(qpos), kchk)
    o = jnp.concatenate([o_c.reshape(bsz, t, FOX_W), o_d.reshape(bsz, t, H_D * V_D)], axis=-1).astype(h.dtype)
    return o @ w_out, (fk, fv, logf, ckv, kpe)


def _layer(x, ffn_norm_l, ffn_w_in_l, ffn_w_out_l, mix_norm_l, mixer, past):
    x = x + 0.5 * _swiglu(_rms(x, ffn_norm_l[0]), ffn_w_in_l[0], ffn_w_out_l[0])
    m, new_state = mixer(_rms(x, mix_norm_l), past)
    x = x + m
    x = x + 0.5 * _swiglu(_rms(x, ffn_norm_l[1]), ffn_w_in_l[1], ffn_w_out_l[1])
    return x, new_state


def _stack_field(rows, j):
    return jnp.stack([r[j] for r in rows])


def setup_inputs(seed: int = 0) -> dict:
    key = jax.random.key(seed)
    ks = iter(jax.random.split(key, 64))
    f32 = jnp.float32

    def nrm(shape, scale):
        return scale * jax.random.normal(next(ks), shape, f32)

    def gain(shape):
        return 1.0 + nrm(shape, 0.01)

    return {
        'x_prompt': nrm((BATCH, SEQ, D_MODEL), 1.0),
        'x_sample': nrm((DEC_BATCH, DEC_SEQ, D_MODEL), 1.0),
        'state_s5_re': nrm((N_AB, DEC_BATCH, G_A, P_A), 0.1),
        'state_s5_im': nrm((N_AB, DEC_BATCH, G_A, P_A), 0.1),
        'cache_diff_k': nrm((N_AB, DEC_BATCH, PAST_LEN, H_B, 2 * DH_B), 1.0),
        'cache_diff_v': nrm((N_AB, DEC_BATCH, PAST_LEN, H_B, DV_B), 1.0),
        'cache_fox_k': nrm((N_CD, DEC_BATCH, PAST_LEN, H_C, DH_C), 1.0),
        'cache_fox_v': nrm((N_CD, DEC_BATCH, PAST_LEN, H_C, DH_C), 1.0),
        'cache_fox_logf': jax.nn.log_sigmoid(FOX_BIAS + nrm((N_CD, DEC_BATCH, PAST_LEN, H_C), 1.0)),
        'cache_mla_ckv': nrm((N_CD, DEC_BATCH, PAST_LEN, KV_LORA), 1.0),
        'cache_mla_kpe': nrm((N_CD, DEC_BATCH, PAST_LEN, ROPE_D), 1.0),
        'meta_tokens': nrm((N_META, D_MODEL), 1.0),
        'ffn_norm': gain((DEPTH, 2, D_MODEL)),
        'ffn_w_in': nrm((DEPTH, 2, D_MODEL, 2 * D_FF), D_MODEL ** -0.5),
        'ffn_w_out': nrm((DEPTH, 2, D_FF, D_MODEL), D_FF ** -0.5),
        'mix_norm': gain((DEPTH, D_MODEL)),
        'ab_w_in': nrm((N_AB, D_MODEL, AB_IN), D_MODEL ** -0.5),
        'ab_w_out': nrm((N_AB, MIX_WIDTH, D_MODEL), MIX_WIDTH ** -0.5),
        's5_a_re': -0.5 + nrm((N_AB, G_A, P_A), 0.01),
        's5_a_im': math.pi * jnp.arange(P_A, dtype=f32) + nrm((N_AB, G_A, P_A), 0.01),
        's5_log_step': jax.random.uniform(next(ks), (N_AB, G_A), f32, math.log(1e-3), math.log(1e-1)),
        's5_b_re': nrm((N_AB, G_A, P_A, S5_GROUP), (2 * S5_GROUP) ** -0.5),
        's5_b_im': nrm((N_AB, G_A, P_A, S5_GROUP), (2 * S5_GROUP) ** -0.5),
        's5_c_re': nrm((N_AB, G_A, S5_GROUP, P_A), P_A ** -0.5),
        's5_c_im': nrm((N_AB, G_A, S5_GROUP, P_A), P_A ** -0.5),
        's5_d': nrm((N_AB, S5_WIDTH), 1.0),
        's5_glu_w': nrm((N_AB, S5_WIDTH, S5_WIDTH), S5_WIDTH ** -0.5),
        's5_glu_b': nrm((N_AB, S5_WIDTH), 0.02),
        'diff_q_norm': gain((N_AB, DH_B)),
        'diff_k_norm': gain((N_AB, DH_B)),
        'diff_lam': nrm((N_AB, 4, DH_B), 0.1),
        'diff_sub_norm': gain((N_AB, DV_B)),
        'cd_w_in': nrm((N_CD, D_MODEL, CD_IN), D_MODEL ** -0.5),
        'cd_w_out': nrm((N_CD, MIX_WIDTH, D_MODEL), MIX_WIDTH ** -0.5),
        'fox_q_norm': gain((N_CD, DH_C)),
        'fox_k_norm': gain((N_CD, DH_C)),
        'fox_f_bias': FOX_BIAS + nrm((N_CD, H_C), 0.5),
        'mla_q_a_norm': gain((N_CD, Q_LORA)),
        'mla_q_b': nrm((N_CD, Q_LORA, H_D * (NOPE_D + ROPE_D)), Q_LORA ** -0.5),
        'mla_kv_a_norm': gain((N_CD, KV_LORA)),
        'mla_kv_b': nrm((N_CD, KV_LORA, H_D * (NOPE_D + V_D)), KV_LORA ** -0.5),
        'mla_q_norm': gain((N_CD, NOPE_D + ROPE_D)),
        'mla_k_norm': gain((N_CD, NOPE_D + ROPE_D)),
    }


def reference(x_prompt, x_sample, state_s5_re, state_s5_im, cache_diff_k, cache_diff_v, cache_fox_k, cache_fox_v,
              cache_fox_logf, cache_mla_ckv, cache_mla_kpe, meta_tokens, ffn_norm, ffn_w_in, ffn_w_out, mix_norm,
              ab_w_in, ab_w_out, s5_a_re, s5_a_im, s5_log_step, s5_b_re, s5_b_im, s5_c_re, s5_c_im, s5_d,
              s5_glu_w, s5_glu_b, diff_q_norm, diff_k_norm, diff_lam, diff_sub_norm, cd_w_in, cd_w_out,
              fox_q_norm, fox_k_norm, fox_f_bias, mla_q_a_norm, mla_q_b, mla_kv_a_norm, mla_kv_b,
              mla_q_norm, mla_k_norm):
    bsz, _, dm = x_prompt.shape
    meta = jnp.broadcast_to(meta_tokens[None].astype(x_prompt.dtype), (bsz, N_META, dm))
    xp = jnp.concatenate([meta, x_prompt], axis=1)
    xs = x_sample
    ab_new_p, ab_new_s, cd_new_p, cd_new_s = [], [], [], []
    for l in range(DEPTH):
        i = l // 2
        if l % 2 == 0:
            def mixer(h, past, i=i, l=l):
                return _ab_mixer(h, ab_w_in[i], ab_w_out[i], s5_a_re[i], s5_a_im[i], s5_log_step[i],
                                 s5_b_re[i], s5_b_im[i], s5_c_re[i], s5_c_im[i], s5_d[i], s5_glu_w[i], s5_glu_b[i],
                                 diff_q_norm[i], diff_k_norm[i], diff_lam[i], diff_sub_norm[i], l, past)
            past_s = (state_s5_re[i], state_s5_im[i], cache_diff_k[i], cache_diff_v[i])
            rows_p, rows_s = ab_new_p, ab_new_s
        else:
            def mixer(h, past, i=i):
                return _cd_mixer(h, cd_w_in[i], cd_w_out[i], fox_q_norm[i], fox_k_norm[i], fox_f_bias[i],
                                 mla_q_a_norm[i], mla_q_b[i], mla_kv_a_norm[i], mla_kv_b[i],
                                 mla_q_norm[i], mla_k_norm[i], past)
            past_s = (cache_fox_k[i], cache_fox_v[i], cache_fox_logf[i], cache_mla_ckv[i], cache_mla_kpe[i])
            rows_p, rows_s = cd_new_p, cd_new_s
        xp, st_p = _layer(xp, ffn_norm[l], ffn_w_in[l], ffn_w_out[l], mix_norm[l], mixer, None)
        xs, st_s = _layer(xs, ffn_norm[l], ffn_w_in[l], ffn_w_out[l], mix_norm[l], mixer, past_s)
        rows_p.append(st_p)
        rows_s.append(st_s)
    y_prompt = xp[:, N_META:]
    y_sample = xs
    s5_re_p = _stack_field(ab_new_p, 0)
    s5_im_p = _stack_field(ab_new_p, 1)
    diff_k_p = _stack_field(ab_new_p, 2)
    diff_v_p = _stack_field(ab_new_p, 3)
    fox_k_p = _stack_field(cd_new_p, 0)
    fox_v_p = _stack_field(cd_new_p, 1)
    fox_logf_p = _stack_field(cd_new_p, 2)
    mla_ckv_p = _stack_field(cd_new_p, 3)
    mla_kpe_p = _stack_field(cd_new_p, 4)
    s5_re_s = _stack_field(ab_new_s, 0)
    s5_im_s = _stack_field(ab_new_s, 1)
    diff_k_s = _stack_field(ab_new_s, 2)
    diff_v_s = _stack_field(ab_new_s, 3)
    fox_k_s = _stack_field(cd_new_s, 0)
    fox_v_s = _stack_field(cd_new_s, 1)
    fox_logf_s = _stack_field(cd_new_s, 2)
    mla_ckv_s = _stack_field(cd_new_s, 3)
    mla_kpe_s = _stack_field(cd_new_s, 4)
    return (y_prompt, y_sample,
            s5_re_p, s5_im_p, diff_k_p, diff_v_p, fox_k_p, fox_v_p, fox_logf_p, mla_ckv_p, mla_kpe_p,
            s5_re_s, s5_im_s, diff_k_s, diff_v_s, fox_k_s, fox_v_s, fox_logf_s, mla_ckv_s, mla_kpe_s)
```

```python
import math
from contextlib import ExitStack
import numpy as np
import concourse.bass as bass
import concourse.mybir as mybir
from concourse.bass_utils import run_bass_kernel_spmd

F32 = mybir.dt.float32
BF16 = mybir.dt.bfloat16
AF = mybir.ActivationFunctionType
ALU = mybir.AluOpType

NCORES = 8
D = 1024
SEQ = 4096
NMETA = 16
LP = SEQ + NMETA
NS = 4
DSEQ = 16
PAST = 4096
T = LP + NS * DSEQ
TK = PAST + DSEQ
NT = 256
DFF = 2816
EPS = 1e-6
NEG = -30000.0
CDW = 2216
LAM_INIT0 = 0.8 - 0.6 * math.exp(-0.3 * 0)


class Buf:
    __slots__ = ("w", "r")

    def __init__(self):
        self.w = None
        self.r = []


class R:
    __slots__ = ("ap", "b")

    def __init__(self, ap, b):
        self.ap = ap
        self.b = b


class TT:
    def __init__(self, t, b=None):
        self.t = t
        self.b = b if b is not None else Buf()

    def __getitem__(self, idx):
        return R(self.t[idx], self.b)


class Op:
    __slots__ = ("eng", "fn", "deps", "flagged", "count", "is_dma", "sem", "semval")


ENGS = ("tensor", "vector", "scalar", "gpsimd", "sync")


class Prog:
    def __init__(self, nc, st, n_dma_sems=10):
        self.nc = nc
        self.n_dma_sems = n_dma_sems
        self.esem = {e: st.enter_context(nc.semaphore("s_" + e)) for e in ENGS}
        self.dsem = {}
        for q in ("sync", "gpsimd"):
            for k in range(n_dma_sems):
                self.dsem[(q, k)] = st.enter_context(nc.semaphore("d_%s_%d" % (q, k)))
        self.ecount = {e: 0 for e in ENGS}
        self.dma_tot = {k: 0 for k in self.dsem}
        self.dma_last = {}
        self.dma_i = {"sync": 0, "gpsimd": 0}
        self.waited = {e: {} for e in ENGS}
        self.ops = {e: [] for e in ENGS}

    def op(self, eng, fn, reads=(), writes=()):
        o = Op()
        o.eng = eng
        o.fn = fn
        o.flagged = False
        o.count = 0
        o.is_dma = False
        o.sem = None
        o.semval = 0
        deps = {}
        for b in reads:
            if b.w is not None:
                deps[id(b.w)] = b.w
        for b in writes:
            if b.w is not None:
                deps[id(b.w)] = b.w
            for r in b.r:
                deps[id(r)] = r
        dl = []
        for d in deps.values():
            if (not d.is_dma) and d.eng == "tensor" and eng == "tensor":
                continue
            dl.append(d)
        o.deps = dl
        for b in reads:
            b.r.append(o)
        for b in writes:
            b.w = o
            b.r = []
        self.ops[eng].append(o)
        return o

    def dma(self, queue, out, in_, **kw):
        k = self.dma_i[queue] % self.n_dma_sems
        self.dma_i[queue] += 1
        key = (queue, k)
        prev = self.dma_last.get(key)
        oap, iap = out.ap, in_.ap

        def fn(e):
            return e.dma_start(out=oap, in_=iap, **kw)

        o = self.op(queue, fn, [in_.b], [out.b])
        o.is_dma = True
        if prev is not None:
            o.deps.append(prev)
        self.dma_tot[key] += 16
        o.sem = key
        o.semval = self.dma_tot[key]
        o.flagged = True
        self.dma_last[key] = o
        return o

    def mm(self, out, lhsT, rhs, start=True, stop=True):
        o_, l_, r_ = out.ap, lhsT.ap, rhs.ap
        self.op("tensor", lambda e: e.matmul(out=o_, lhsT=l_, rhs=r_, start=start, stop=stop),
                [lhsT.b, rhs.b], [out.b])

    def tr(self, out, in_, ident):
        o_, i_, d_ = out.ap, in_.ap, ident.ap
        self.op("tensor", lambda e: e.transpose(out=o_, in_=i_, identity=d_), [in_.b, ident.b], [out.b])

    def act(self, out, in_, func, bias=None, scale=None, accum=None):
        o_, i_ = out.ap, in_.ap
        rd = [in_.b]
        wr = [out.b]
        kw = {}
        if bias is not None:
            if isinstance(bias, R):
                kw["bias"] = bias.ap
                rd.append(bias.b)
            else:
                kw["bias"] = bias
        if scale is not None:
            if isinstance(scale, R):
                kw["scale"] = scale.ap
                rd.append(scale.b)
            else:
                kw["scale"] = scale
        if accum is not None:
            kw["accum_out"] = accum.ap
            wr.append(accum.b)
        self.op("scalar", lambda e: e.activation(out=o_, in_=i_, func=func, **kw), rd, wr)

    def tt(self, eng, out, in0, in1, op):
        o_, a_, b_ = out.ap, in0.ap, in1.ap
        self.op(eng, lambda e: e.tensor_tensor(out=o_, in0=a_, in1=b_, op=op), [in0.b, in1.b], [out.b])

    def stt(self, out, in0, scalar, in1, op0, op1):
        o_, a_, b_ = out.ap, in0.ap, in1.ap
        rd = [in0.b, in1.b]
        if isinstance(scalar, R):
            s_ = scalar.ap
            rd.append(scalar.b)
        else:
            s_ = scalar
        self.op("vector", lambda e: e.scalar_tensor_tensor(out=o_, in0=a_, scalar=s_, in1=b_, op0=op0, op1=op1),
                rd, [out.b])

    def ts(self, eng, out, in0, s1, s2, op0, op1=None):
        o_, a_ = out.ap, in0.ap
        rd = [in0.b]

        def cv(s):
            if isinstance(s, R):
                rd.append(s.b)
                return s.ap
            return s
        s1_ = cv(s1)
        s2_ = cv(s2)
        if op1 is None:
            self.op(eng, lambda e: e.tensor_scalar(out=o_, in0=a_, scalar1=s1_, scalar2=None, op0=op0), rd, [out.b])
        else:
            self.op(eng, lambda e: e.tensor_scalar(out=o_, in0=a_, scalar1=s1_, scalar2=s2_, op0=op0, op1=op1),
                    rd, [out.b])

    def copy(self, eng, out, in_):
        o_, i_ = out.ap, in_.ap
        if eng == "scalar":
            self.op(eng, lambda e: e.copy(out=o_, in_=i_), [in_.b], [out.b])
        else:
            self.op(eng, lambda e: e.tensor_copy(out=o_, in_=i_), [in_.b], [out.b])

    def recip(self, out, in_):
        o_, i_ = out.ap, in_.ap
        self.op("vector", lambda e: e.reciprocal(out=o_, in_=i_), [in_.b], [out.b])

    def memset(self, eng, out, val):
        o_ = out.ap
        self.op(eng, lambda e: e.memset(o_, val), [], [out.b])

    def scan(self, out, d0, d1, init, op0=ALU.mult, op1=ALU.add):
        o_, a_, b_ = out.ap, d0.ap, d1.ap
        rd = [d0.b, d1.b]
        if isinstance(init, R):
            i_ = init.ap
            rd.append(init.b)
        else:
            i_ = init
        self.op("vector", lambda e: e.tensor_tensor_scan(out=o_, data0=a_, data1=b_, initial=i_, op0=op0, op1=op1),
                rd, [out.b])

    def reduce(self, out, in_, op=ALU.add):
        o_, i_ = out.ap, in_.ap
        self.op("vector", lambda e: e.tensor_reduce(out=o_, in_=i_, axis=mybir.AxisListType.X, op=op), [in_.b], [out.b])

    def emit(self):
        nc = self.nc
        for e in ENGS:
            for o in self.ops[e]:
                for d in o.deps:
                    d.flagged = True
            if self.ops[e]:
                self.ops[e][-1].flagged = True
        for e in ENGS:
            c = self.ecount[e]
            for o in self.ops[e]:
                if o.is_dma:
                    continue
                if o.flagged:
                    c += 1
                    o.count = c
            self.ecount[e] = c
        prog = self
        with nc.Block() as block:
            def make(ename):
                def body(e):
                    waited = prog.waited[ename]
                    for o in prog.ops[ename]:
                        need = {}
                        for d in o.deps:
                            if d.is_dma:
                                s = ("d",) + d.sem
                                v = d.semval
                            else:
                                s = ("e", d.eng)
                                v = d.count
                            if need.get(s, 0) < v:
                                need[s] = v
                        for s, v in need.items():
                            if waited.get(s, 0) < v:
                                waited[s] = v
                                sem = prog.dsem[s[1:]] if s[0] == "d" else prog.esem[s[1]]
                                e.wait_ge(sem, v)
                        ins = o.fn(e)
                        if o.is_dma:
                            ins.then_inc(prog.dsem[o.sem], 16)
                        elif o.flagged:
                            ins.then_inc(prog.esem[ename], 1)
                    for key, tot in prog.dma_tot.items():
                        s = ("d",) + key
                        if tot > 0 and waited.get(s, 0) < tot:
                            waited[s] = tot
                            e.wait_ge(prog.dsem[key], tot)
                    for e2 in ENGS:
                        c2 = prog.ecount[e2]
                        s = ("e", e2)
                        if e2 != ename and c2 > 0 and waited.get(s, 0) < c2:
                            waited[s] = c2
                            e.wait_ge(prog.esem[e2], c2)
                return body
            for ename in ENGS:
                getattr(block, ename)(make(ename))
        self.ops = {e: [] for e in ENGS}
        self.dma_last = {}


def _prm_layout():
    off = {}
    n = 0

    def add(name, w):
        nonlocal n
        off[name] = n
        n += w
    for l in range(2):
        for i in range(2):
            add("ffn_norm%d%d" % (l, i), 8)
        add("mix_norm%d" % l, 8)
    add("a_re", 16); add("a_im", 16); add("lstep", 16)
    add("s5d", 4); add("glub", 4)
    add("gq", 1); add("gk", 1); add("gsub", 1); add("lamv", 4)
    add("fgq", 1); add("fgk", 1); add("fbias", 1)
    add("qan", 3); add("kvan", 2); add("mqn", 1); add("mkn", 1)
    return off, n


PRM_OFF, NPRM = _prm_layout()


def _cols(v, nch):
    return np.ascontiguousarray(np.asarray(v, np.float32).reshape(nch, 128).T)


TILES_F = [(4096, 80)] + [(256 * i, 256) for i in range(16)]
TILES_A = [(4096, 80)] + [(128 * i, 128) for i in range(32)]


def out_rows(c0, n):
    if c0 < 4096:
        return [("p", 16 + c0, 0, n)]
    return [("p", 0, 0, 16), ("s", 0, 16, 64)]


def key_dst(c0, n):
    if c0 < 4096:
        return [(0, c0, 0, n)]
    return [(0, 4096, 0, 16)] + [(1 + s, 4096, 16 + 16 * s, 16) for s in range(NS)]


def build_program():
    nc = bass.Bass("TRN2", target_bir_lowering=False)

    def din(name, shape, dt=F32):
        return TT(nc.dram_tensor(name, list(shape), dt, kind="ExternalInput").ap())

    def dout(name, shape):
        return TT(nc.dram_tensor(name, list(shape), F32, kind="ExternalOutput").ap())

    def dscr(name, shape, dt=BF16):
        return TT(nc.dram_tensor(name, list(shape), dt, kind="Internal").ap())

    xp = din("xp", [SEQ, D]); xs = din("xs", [NS * DSEQ, D]); meta = din("meta", [NMETA, D])
    wi = din("wi", [2, 2, D, 2 * DFF]); wo = din("wo", [2, 2, DFF, D])
    abwi = din("abwi", [D, 2048]); abwo = din("abwo", [D, D])
    cdwi = din("cdwi", [D, CDW]); cdwo = din("cdwo", [D, D])
    gluw = din("gluw", [512, 512]); qb = din("qb", [384, 768]); qbr = din("qbr", [384, 768])
    kvb = din("kvb", [256, 1024])
    prm = din("prm", [128, NPRM]); st0 = din("st0", [128, NS * 2 * 16])
    bre = din("bre", [16, 128, 128]); bim = din("bim", [16, 128, 128])
    cre = din("cre", [16, 128, 128]); cim = din("cim", [16, 128, 128])
    identd = din("ident", [128, 128])
    cdbig = din("cdbig", [128, 4 * 128]); cdsm = din("cdsm", [16, 4 * 16])
    cfbig = din("cfbig", [128, 128]); cfsm = din("cfsm", [16, 16]); cmbig = din("cmbig", [128, 128])
    augk = din("augk", [2, 4, 4, TK]); augq = din("augq", [2, 4, 4, TK])
    cos96 = din("cos96", [96, T]); sin96 = din("sin96", [96, T])
    cos32 = din("cos32", [32, T]); sin32 = din("sin32", [32, T])
    cdk = din("cdk", [NS, PAST, 512]); cdv = din("cdv", [NS, PAST, 512])
    cfk = din("cfk", [NS, PAST, 512]); cfv = din("cfv", [NS, PAST, 512])
    clf = din("clf", [NS, PAST, 8]); cckv = din("cckv", [NS, PAST, 256]); ckpe = din("ckpe", [NS, PAST, 32])

    yp = dout("yp", [SEQ, D]); ys = dout("ys", [NS * DSEQ, D])
    s5p = dout("s5p", [2, 128, 16]); s5s = dout("s5s", [NS, 2, 128, 16])
    O = {}
    for nm, w in (("dk", 512), ("dv", 512), ("fk", 512), ("fv", 512), ("lf", 8), ("ckv", 256), ("kpe", 32)):
        O[nm + "p"] = dout(nm + "p_o", [LP, w])
        O[nm + "s"] = dout(nm + "s_o", [NS * DSEQ, w])

    xres = dscr("xres", [128, 8, T], F32)
    mixin = dscr("mixin", [D, T])
    QD = dscr("QD", [4, 128, T]); KD = dscr("KD", [1 + NS, 4, 128, TK]); VD = dscr("VD", [1 + NS, TK, 512])
    AUGK = dscr("AUGK", [2, 4, 4, TK]); AUGQ = dscr("AUGQ", [2, 4, 4, TK])
    QF = dscr("QF", [4, 128, T]); KF = dscr("KF", [1 + NS, 4, 128, TK]); VF = dscr("VF", [1 + NS, TK, 512])
    FK = dscr("FK", [1 + NS, 8, 3, TK]); FQ = dscr("FQ", [8, 3, T])
    QM = dscr("QM", [8, 96, T]); KM = dscr("KM", [1 + NS, 4, 128, TK]); KP = dscr("KP", [1 + NS, 32, TK])
    VM = dscr("VM", [1 + NS, TK, 512])

    def pcol(name, c=0, n=1, rows=128):
        o = PRM_OFF[name] + c
        return prmt[0:rows, o:o + n]

    with ExitStack() as gst:
        P = Prog(nc, gst)
        ps = [TT(gst.enter_context(nc.psum_tensor("ps%d" % i, [128, 512], F32))) for i in range(8)]
        prmt = TT(gst.enter_context(nc.sbuf_tensor("prmt", [128, NPRM], F32)))
        ident = TT(gst.enter_context(nc.sbuf_tensor("identt", [128, 128], F32)))
        ones = TT(gst.enter_context(nc.sbuf_tensor("ones", [128, 128], F32)))
        blk64 = TT(gst.enter_context(nc.sbuf_tensor("blk64", [128, 128], F32)))
        onesb = TT(gst.enter_context(nc.sbuf_tensor("onesb", [128, 128], BF16)))
        cst = TT(gst.enter_context(nc.sbuf_tensor("cst", [128, 4], F32)))
        ksc_all = TT(gst.enter_context(nc.sbuf_tensor("ksc_all", [128, 1 + NS, 33, 8], F32)))
        esel = TT(gst.enter_context(nc.sbuf_tensor("esel", [128, 64], F32)))

        P.dma("sync", prmt[:, :], prm[:, :])
        P.dma("sync", ident[:, :], identd[:, :])
        P.memset("vector", ones[:, :], 1.0)
        P.memset("vector", onesb[:, :], 1.0)
        P.memset("vector", blk64[:, :], 0.0)
        P.memset("vector", blk64[0:64, 0:64], 1.0)
        P.memset("vector", blk64[64:128, 64:128], 1.0)
        P.memset("vector", cst[:, 0:1], EPS)
        P.memset("vector", cst[:, 1:2], math.pi / 2)
        P.memset("vector", cst[:, 2:3], 0.0)
        P.memset("vector", cst[:, 3:4], 1.0)
        P.memset("vector", esel[:, :], 0.0)
        P.memset("vector", esel[64:65, :], 1.0)
        with ExitStack() as st:
            a32 = TT(st.enter_context(nc.sbuf_tensor("a32", [32, TK], F32)))
            a16 = TT(st.enter_context(nc.sbuf_tensor("a16", [32, TK], BF16)))
            for src, dst in ((augk, AUGK), (augq, AUGQ)):
                P.dma("sync", a32[:, :], R(src.t.rearrange("a h r t -> (a h r) t"), src.b))
                P.copy("vector", a16[:, :], a32[:, :])
                P.dma("sync", R(dst.t.rearrange("a h r t -> (a h r) t"), dst.b), a16[:, :])
            P.emit()
        eps_c = cst[:, 0:1]

        def load_x_first(st_tiles, xT, c0, n):
            xtok = st_tiles["xtok"]
            if c0 < 4096:
                for blk in range(n // 128):
                    P.dma("sync", xtok[:, :], xp[c0 + blk * 128:c0 + (blk + 1) * 128, :])
                    for c in range(8):
                        pt_ = ps[6 + c % 2]
                        P.tr(pt_[:, 0:128], xtok[:, c * 128:(c + 1) * 128], ident[:, :])
                        P.copy("vector" if c % 2 else "scalar", xT[:, c, blk * 128:(blk + 1) * 128], pt_[:, 0:128])
            else:
                P.dma("sync", xtok[0:16, :], meta[:, :])
                P.dma("sync", xtok[16:80, :], xs[:, :])
                for c in range(8):
                    pt_ = ps[6 + c % 2]
                    P.tr(pt_[:, 0:80], xtok[0:80, c * 128:(c + 1) * 128], ident[0:80, 0:80])
                    P.copy("vector" if c % 2 else "scalar", xT[:, c, 0:80], pt_[:, 0:80])

        def rms_h(xT, h, sqc, rstd, gname, n):
            for c in range(8):
                P.act(sqc[c % 2][:, 0:n], xT[:, c, 0:n], AF.Square)
                P.mm(ps[7][:, 0:n], ones[:, :], sqc[c % 2][:, 0:n], c == 0, c == 7)
            P.act(rstd[:, 0:n], ps[7][:, 0:n], AF.Sqrt, bias=eps_c, scale=1.0 / D)
            P.recip(rstd[:, 0:n], rstd[:, 0:n])
            for c in range(8):
                P.stt(h[:, c, 0:n], xT[:, c, 0:n], pcol(gname, c), rstd[:, 0:n], ALU.mult, ALU.mult)

        def store_tok(src, nchunks, c0, n, outs, tokbuf, width, is_y=False):
            nb = (n + 127) // 128
            for blk in range(nb):
                bn = min(128, n - blk * 128)
                for c in range(nchunks):
                    P.tr(ps[6][0:bn, c * 128:(c + 1) * 128] if nchunks <= 4 else ps[6 + c // 4][0:bn, (c % 4) * 128:(c % 4 + 1) * 128],
                         src[:, c, blk * 128:blk * 128 + bn], ident[:, :])
                for q in range((nchunks + 3) // 4):
                    wq = min(4, nchunks - 4 * q) * 128
                    P.copy("vector" if q else "scalar", tokbuf[0:bn, q * 512:q * 512 + wq], ps[6 + q][0:bn, 0:wq])
                for grp, r0, lo, cnt in out_rows(c0 + blk * 128, bn):
                    if is_y and grp == "p":
                        if c0 >= 4096:
                            continue
                        r0 -= 16
                    P.dma("gpsimd", outs[grp][r0:r0 + cnt, 0:width], tokbuf[lo:lo + cnt, 0:width])

        def ffn_phase(l, i, first, last):
            with ExitStack() as st:
                def sb(name, shape, dt=F32):
                    return TT(st.enter_context(nc.sbuf_tensor("%s_f%d%d" % (name, l, i), shape, dt)))
                win = sb("win", [128, 8, 2 * DFF], BF16)
                wout = sb("wout", [128, 22, D], BF16)
                stg = [sb("stg%d" % k, [128, 1024]) for k in range(2)]
                xTs = [sb("xT%d" % k, [128, 8, NT]) for k in range(2)]
                sq8 = sb("sq8", [128, 8, NT])
                rstds = [sb("rstd%d" % k, [128, NT]) for k in range(2)]
                hs = [sb("h%d" % k, [128, 8, NT], BF16) for k in range(2)]
                actb = sb("actb", [128, 22, NT], BF16)
                sg = [sb("sg%d" % k, [128, NT]) for k in range(2)]
                tiles = {}
                if first:
                    tiles["xtok"] = sb("xtok", [128, D])
                if last:
                    ytok = sb("ytok", [128, D])
                k = 0
                engs = ("vector", "gpsimd", "scalar")
                for c in range(8):
                    for n0 in range(0, 2 * DFF, 1024):
                        w = min(1024, 2 * DFF - n0)
                        P.dma("sync", stg[k % 2][:, 0:w], wi[l, i, c * 128:(c + 1) * 128, n0:n0 + w])
                        P.copy(engs[k % 3], win[:, c, n0:n0 + w], stg[k % 2][:, 0:w])
                        k += 1
                for j in range(22):
                    P.dma("sync", stg[k % 2][:, :], wo[l, i, j * 128:(j + 1) * 128, :])
                    P.copy(engs[k % 3], wout[:, j, :], stg[k % 2][:, :])
                    k += 1
                gname = "ffn_norm%d%d" % (l, i)

                def load_x(t):
                    c0, n = TILES_F[t]
                    if first:
                        load_x_first(tiles, xTs[t % 2], c0, n)
                    else:
                        P.dma("sync", xTs[t % 2][:, :, 0:n], xres[:, :, c0:c0 + n])

                def squares(t):
                    c0, n = TILES_F[t]
                    for c in range(8):
                        P.act(sq8[:, c, 0:n], xTs[t % 2][:, c, 0:n], AF.Square)

                def norm_h(t):
                    c0, n = TILES_F[t]
                    xT, h, rstd = xTs[t % 2], hs[t % 2], rstds[t % 2]
                    for c in range(8):
                        P.mm(ps[7][:, 0:n], ones[:, :], sq8[:, c, 0:n], c == 0, c == 7)
                    P.act(rstd[:, 0:n], ps[7][:, 0:n], AF.Sqrt, bias=eps_c, scale=1.0 / D)
                    P.recip(rstd[:, 0:n], rstd[:, 0:n])
                    for c in range(8):
                        P.stt(h[:, c, 0:n], xT[:, c, 0:n], pcol(gname, c), rstd[:, 0:n], ALU.mult, ALU.mult)

                ntile = len(TILES_F)
                pipe = not first
                if pipe:
                    load_x(0)
                    squares(0)
                    norm_h(0)
                for t in range(ntile):
                    c0, n = TILES_F[t]
                    xT, h = xTs[t % 2], hs[t % 2]
                    if pipe:
                        if t + 1 < ntile:
                            load_x(t + 1)
                    else:
                        load_x(t)
                        squares(t)
                        norm_h(t)
                    for j in range(22):
                        pg = ps[(j % 2) * 2]
                        pu = ps[(j % 2) * 2 + 1]
                        for c in range(8):
                            P.mm(pg[:, 0:n], win[:, c, j * 128:(j + 1) * 128], h[:, c, 0:n], c == 0, c == 7)
                        for c in range(8):
                            P.mm(pu[:, 0:n], win[:, c, DFF + j * 128:DFF + (j + 1) * 128], h[:, c, 0:n], c == 0, c == 7)
                        P.act(sg[j % 2][:, 0:n], pg[:, 0:n], AF.Silu)
                        P.tt("vector", actb[:, j, 0:n], sg[j % 2][:, 0:n], pu[:, 0:n], ALU.mult)
                        if pipe and j == 10 and t + 1 < ntile:
                            squares(t + 1)
                    if pipe and t + 1 < ntile:
                        norm_h(t + 1)
                    for m in range(8):
                        po = ps[4 + m % 2]
                        for j in range(22):
                            P.mm(po[:, 0:n], wout[:, j, m * 128:(m + 1) * 128], actb[:, j, 0:n], j == 0, j == 21)
                        P.stt(xT[:, m, 0:n], po[:, 0:n], 0.5, xT[:, m, 0:n], ALU.mult, ALU.add)
                    if last:
                        store_tok(xT, 8, c0, n, {"p": yp, "s": ys}, ytok, D, is_y=True)
                    else:
                        P.dma("gpsimd", xres[:, :, c0:c0 + n], xT[:, :, 0:n])
                P.emit()


        def m0a_phase():
            NA = 128
            import os
            SKIP = os.environ.get("MK_SKIP", "").split(",")
            peng = "vector" if "pool" in SKIP else "gpsimd"
            with ExitStack() as st:
                def sb(name, shape, dt=F32):
                    return TT(st.enter_context(nc.sbuf_tensor(name + "_m0a", shape, dt)))
                wab = sb("wab", [128, 8, 2048], BF16)
                stg = [sb("stg%d" % k, [128, 1024]) for k in range(2)]
                Tor = sb("Tor", [128, 16, NA]); Toi = sb("Toi", [128, 16, NA])
                Tir = sb("Tir", [128, 16, NA]); Tii = sb("Tii", [128, 16, NA])
                rho_t = sb("rho_t", [128, 16, NA])
                Bre = sb("Bre", [128, 16, 128], BF16); Bim = sb("Bim", [128, 16, 128], BF16)
                sp = sb("sp", [128, 16, 16])
                xT = sb("xT", [128, 8, NA]); sqc = [sb("sqc%d" % k, [128, NA]) for k in range(2)]
                rstd = sb("rstd", [128, NA]); h = sb("h", [128, 8, NA], BF16)
                ub = sb("ub", [128, 4, NA], BF16)
                tmp = [sb("tmp%d" % k, [128, NA]) for k in range(8)]
                hst = sb("hst", [128, 2, 16])
                sst = sb("sst", [128, NS * 2 * 16])
                st0t = sb("st0t", [128, NS * 2 * 16])
                kn32 = sb("kn32", [128, 4, NA]); ktok = sb("ktok", [128, 512]); vtok = sb("vtok", [128, 512])
                sqt = sb("sqt", [128, NA]); rs = sb("rs", [128, NA])
                Cre = sb("Cre", [128, 16, 128], BF16); Cimn = sb("Cimn", [128, 16, 128], BF16)
                gluwb = sb("gluwb", [128, 4, 512], BF16)
                u32 = sb("u32", [128, 4, NA]); hrb = sb("hrb", [128, 16, NA], BF16); hib = sb("hib", [128, 16, NA], BF16)
                gsc = [[sb("gsc%d%d" % (a, b), [128, NA]) for b in range(2)] for a in range(2)]
                ptmp = [sb("ptmp%d" % q, [128, NA]) for q in range(4)]
                yv = sb("yv", [128, 4, NA]); g32 = sb("g32", [128, 4, NA]); gb = sb("gb", [128, 4, NA], BF16)
                so = sb("so", [128, 4, NA], BF16); qn = sb("qn", [128, 4, NA], BF16); knb = sb("knb", [128, 4, NA], BF16)
                vtb = sb("vtb", [128, 512], BF16); gt1 = sb("gt1", [128, NA]); gt2 = sb("gt2", [128, NA])
                segctr = [0]
                sqtr = [sb("sqtr%d" % q, [128, NA]) for q in range(2)]; rsr = [sb("rsr%d" % q, [128, NA]) for q in range(2)]
                big = [sb("big%d" % q, [128, 512]) for q in range(4)]
                pbig = [sb("pbig%d" % q, [128, 512]) for q in range(4)]
                gbig = [[sb("gbig%d%d" % (a, b), [128, 512]) for b in range(2)] for a in range(2)]
                sm4 = sb("sm4", [128, 16])
                evac = [[sb("evac%d%d" % (a_, b_), [128, 512]) for b_ in range(2)] for a_ in range(2)]
                engs = ("vector", "gpsimd", "scalar")
                k = 0
                for j in range(16 if "cload" not in SKIP else 0):
                    P.dma("sync", stg[k % 2][:, 0:128], cre[j, :, :])
                    P.copy(engs[k % 3], Cre[:, j, :], stg[k % 2][:, 0:128])
                    k += 1
                    P.dma("sync", stg[k % 2][:, 0:128], cim[j, :, :])
                    P.ts("vector", Cimn[:, j, :], stg[k % 2][:, 0:128], -1.0, None, ALU.mult)
                    k += 1
                for c in range(4 if "cload" not in SKIP else 0):
                    P.dma("sync", stg[k % 2][:, 0:512], gluw[c * 128:(c + 1) * 128, :])
                    P.copy(engs[k % 3], gluwb[:, c, :], stg[k % 2][:, 0:512])
                    k += 1
                for c in range(8):
                    for n0 in range(0, 2048, 1024):
                        P.dma("sync", stg[k % 2][:, :], abwi[c * 128:(c + 1) * 128, n0:n0 + 1024])
                        P.copy(engs[k % 3], wab[:, c, n0:n0 + 1024], stg[k % 2][:, :])
                        k += 1
                for j in range(16):
                    for src, dst in ((bre, Bre), (bim, Bim)):
                        P.dma("sync", stg[k % 2][:, 0:128], src[j, :, :])
                        P.copy(engs[k % 3], dst[:, j, :], stg[k % 2][:, 0:128])
                        k += 1
                P.dma("sync", st0t[:, :], st0[:, :])
                P.memset("vector", hst[:, :, :], 0.0)
                DL, ARD, TH, RHO, C_, S_, T1, T2, LBR, LBI, DEN, KR, KI, T3 = range(14)
                a_re = pcol("a_re", 0, 16); a_im = pcol("a_im", 0, 16)
                def S(i):
                    return sp[:, i, :]
                P.act(S(DL), pcol("lstep", 0, 16), AF.Exp)
                P.tt("vector", S(ARD), a_re, S(DL), ALU.mult)
                P.tt("vector", S(TH), a_im, S(DL), ALU.mult)
                P.act(S(RHO), S(ARD), AF.Exp)
                P.act(S(S_), S(TH), AF.Sin, scale=1.0 / 32)
                P.act(S(C_), S(TH), AF.Sin, scale=1.0 / 32, bias=cst[:, 1:2])
                for _ in range(5):
                    P.tt("vector", S(T1), S(C_), S(C_), ALU.mult)
                    P.tt("vector", S(T2), S(S_), S(S_), ALU.mult)
                    P.stt(S(S_), S(C_), 2.0, S(S_), ALU.mult, ALU.mult)
                    P.tt("vector", S(C_), S(T1), S(T2), ALU.subtract)
                P.tt("vector", S(LBR), S(RHO), S(C_), ALU.mult)
                P.tt("vector", S(LBI), S(RHO), S(S_), ALU.mult)
                P.ts("vector", S(T1), S(LBR), -1.0, None, ALU.add)
                P.tt("vector", S(DEN), a_re, a_re, ALU.mult)
                P.tt("vector", S(T2), a_im, a_im, ALU.mult)
                P.tt("vector", S(DEN), S(DEN), S(T2), ALU.add)
                P.recip(S(DEN), S(DEN))
                P.tt("vector", S(KR), S(T1), a_re, ALU.mult)
                P.tt("vector", S(T2), S(LBI), a_im, ALU.mult)
                P.tt("vector", S(KR), S(KR), S(T2), ALU.add)
                P.tt("vector", S(KR), S(KR), S(DEN), ALU.mult)
                P.tt("vector", S(KI), S(LBI), a_re, ALU.mult)
                P.tt("vector", S(T2), S(T1), a_im, ALU.mult)
                P.tt("vector", S(KI), S(KI), S(T2), ALU.subtract)
                P.tt("vector", S(KI), S(KI), S(DEN), ALU.mult)
                for j in range(16):
                    P.copy("vector", Tor[:, j, 0:1], sp[:, C_, j:j + 1])
                    P.copy("vector", Toi[:, j, 0:1], sp[:, S_, j:j + 1])
                n = 1
                while n < NA:
                    for j in range(16):
                        cr = Tor[:, j, n - 1:n]; ci = Toi[:, j, n - 1:n]
                        a = Tor[:, j, 0:n]; b = Toi[:, j, 0:n]
                        t1 = tmp[(2 * j) % 8][:, 0:n]; t2 = tmp[(2 * j + 1) % 8][:, 0:n]
                        P.ts("vector", t1, b, ci, None, ALU.mult)
                        P.ts("vector", t2, b, cr, None, ALU.mult)
                        P.stt(Tor[:, j, n:2 * n], a, cr, t1, ALU.mult, ALU.subtract)
                        P.stt(Toi[:, j, n:2 * n], a, ci, t2, ALU.mult, ALU.add)
                    n *= 2
                for j in range(16):
                    kr = sp[:, KR, j:j + 1]; ki = sp[:, KI, j:j + 1]
                    t1 = tmp[(2 * j) % 8][:, :]; t2 = tmp[(2 * j + 1) % 8][:, :]
                    P.ts("vector", t1, Toi[:, j, :], ki, None, ALU.mult)
                    P.stt(Tir[:, j, :], Tor[:, j, :], kr, t1, ALU.mult, ALU.add)
                    P.ts("vector", t2, Toi[:, j, :], kr, None, ALU.mult)
                    P.stt(Tii[:, j, :], Tor[:, j, :], ki, t2, ALU.mult, ALU.subtract)
                    P.act(rho_t[:, j, :], Tor[:, j, :], AF.Identity, bias=sp[:, RHO, j:j + 1], scale=0.0)

                def s5_segment(j, pbr, pbi, off, ln, init_re, init_im, fin_re, fin_im):
                    cs = slice(off, off + ln)
                    t = [tmp[q][:, 0:ln] for q in range(8)]
                    P.tt("vector", t[0], Tir[:, j, 0:ln], pbr[:, cs], ALU.mult)
                    P.tt("vector", t[1], Tii[:, j, 0:ln], pbi[:, cs], ALU.mult)
                    P.tt("vector", t[0], t[0], t[1], ALU.subtract)
                    P.tt("vector", t[2], Tir[:, j, 0:ln], pbi[:, cs], ALU.mult)
                    P.tt("vector", t[3], Tii[:, j, 0:ln], pbr[:, cs], ALU.mult)
                    P.tt("vector", t[2], t[2], t[3], ALU.add)
                    gs = gsc[segctr[0] % 2]
                    segctr[0] += 1
                    P.scan(gs[0][:, 0:ln], rho_t[:, j, 0:ln], t[0], init_re)
                    P.scan(gs[1][:, 0:ln], rho_t[:, j, 0:ln], t[2], init_im)
                    pt_ = [ptmp[q][:, 0:ln] for q in range(4)]
                    if "rot" not in SKIP:
                        P.tt(peng, pt_[0], Tor[:, j, 0:ln], gs[0][:, 0:ln], ALU.mult)
                        P.tt(peng, pt_[1], Toi[:, j, 0:ln], gs[1][:, 0:ln], ALU.mult)
                        P.tt(peng, hrb[:, j, cs], pt_[0], pt_[1], ALU.subtract)
                        P.tt(peng, pt_[2], Tor[:, j, 0:ln], gs[1][:, 0:ln], ALU.mult)
                        P.tt(peng, pt_[3], Toi[:, j, 0:ln], gs[0][:, 0:ln], ALU.mult)
                        P.tt(peng, hib[:, j, cs], pt_[2], pt_[3], ALU.add)
                    e = ln - 1
                    gr = gs[0][:, e:e + 1]; gi = gs[1][:, e:e + 1]
                    er = Tor[:, j, e:e + 1]; ei = Toi[:, j, e:e + 1]
                    a1 = tmp[6][:, 0:1]; a2 = tmp[6][:, 1:2]; a3 = tmp[6][:, 2:3]; a4 = tmp[6][:, 3:4]
                    P.tt("vector", a1, er, gr, ALU.mult)
                    P.tt("vector", a2, ei, gi, ALU.mult)
                    P.tt("vector", fin_re, a1, a2, ALU.subtract)
                    P.tt("vector", a3, er, gi, ALU.mult)
                    P.tt("vector", a4, ei, gr, ALU.mult)
                    P.tt("vector", fin_im, a3, a4, ALU.add)

                for (c0, n) in TILES_A:
                    P.dma("sync", xT[:, :, 0:n], xres[:, :, c0:c0 + n])
                    rms_h(xT, h, sqc, rstd, "mix_norm0", n)
                    for cc in range(4):
                        pu = ps[4 + cc % 2]
                        for c in range(8):
                            P.mm(pu[:, 0:n], wab[:, c, cc * 128:(cc + 1) * 128], h[:, c, 0:n], c == 0, c == 7)
                        P.copy("vector", u32[:, cc, 0:n], pu[:, 0:n])
                        P.copy("scalar", ub[:, cc, 0:n], u32[:, cc, 0:n])
                    for q in range(4 if c0 < 4096 else 0):
                        pbr = ps[(q % 2) * 2]; pbi = ps[(q % 2) * 2 + 1]
                        for jj in range(4):
                            j = 4 * q + jj
                            P.mm(pbr[:, jj * 128:(jj + 1) * 128], Bre[:, j, :], ub[:, q, 0:n])
                            P.mm(pbi[:, jj * 128:(jj + 1) * 128], Bim[:, j, :], ub[:, q, 0:n])

                        def Q4(tt_):
                            return R(tt_.t[:, 4 * q:4 * q + 4, :].rearrange("p j t -> p (j t)"), tt_.b)

                        def L4(tt_):
                            return R(tt_.t[:, :].rearrange("p (j t) -> p j t", t=128)[:, :, 127:128].rearrange("p j o -> p (j o)"), tt_.b)

                        def T4(tt_):
                            return R(tt_.t[:, 4 * q:4 * q + 4, 127:128].rearrange("p j o -> p (j o)"), tt_.b)
                        d0, d1, d2, d3 = [big[k][:, :] for k in range(4)]
                        P.tt("vector", d0, Q4(Tir), pbr[:, :], ALU.mult)
                        P.tt("vector", d1, Q4(Tii), pbi[:, :], ALU.mult)
                        P.tt("vector", d0, d0, d1, ALU.subtract)
                        ebr, ebi = evac[q % 2]
                        for src_, dst_ in ((pbr, ebr), (pbi, ebi)):
                            so_, do_ = src_.t[:, :], dst_.t[:, :]
                            P.op("scalar", (lambda e, so_=so_, do_=do_: e.copy(out=do_, in_=so_)), [src_.b, big[0].b], [dst_.b])
                        P.tt("gpsimd", d2, Q4(Tir), ebi[:, :], ALU.mult)
                        P.tt("gpsimd", d3, Q4(Tii), ebr[:, :], ALU.mult)
                        P.tt("gpsimd", d2, d2, d3, ALU.add)
                        gR, gI = gbig[q % 2]
                        for jj in range(4):
                            j = 4 * q + jj
                            cs_ = slice(jj * 128, (jj + 1) * 128)
                            P.scan(gR[:, cs_], rho_t[:, j, :], big[0][:, cs_], hst[:, 0, j:j + 1])
                            P.scan(gI[:, cs_], rho_t[:, j, :], big[2][:, cs_], hst[:, 1, j:j + 1])
                        p0, p1, p2, p3 = [pbig[k][:, :] for k in range(4)]
                        P.tt(peng, p0, Q4(Tor), gR[:, :], ALU.mult)
                        P.tt(peng, p1, Q4(Toi), gI[:, :], ALU.mult)
                        P.tt(peng, Q4(hrb), p0, p1, ALU.subtract)
                        P.tt(peng, p2, Q4(Tor), gI[:, :], ALU.mult)
                        P.tt(peng, p3, Q4(Toi), gR[:, :], ALU.mult)
                        P.tt(peng, Q4(hib), p2, p3, ALU.add)
                        a1, a2, a3, a4 = [sm4[:, 4 * k:4 * k + 4] for k in range(4)]
                        P.tt("vector", a1, T4(Tor), L4(gR), ALU.mult)
                        P.tt("vector", a2, T4(Toi), L4(gI), ALU.mult)
                        P.tt("vector", a3, T4(Tor), L4(gI), ALU.mult)
                        P.tt("vector", a4, T4(Toi), L4(gR), ALU.mult)
                        P.tt("vector", hst[:, 0, 4 * q:4 * q + 4], a1, a2, ALU.subtract)
                        P.tt("vector", hst[:, 1, 4 * q:4 * q + 4], a3, a4, ALU.add)
                    for j in range(16 if c0 >= 4096 else 0):
                        pbr = ps[(j % 2) * 2]; pbi = ps[(j % 2) * 2 + 1]
                        P.mm(pbr[:, 0:n], Bre[:, j, :], ub[:, j // 4, 0:n])
                        P.mm(pbi[:, 0:n], Bim[:, j, :], ub[:, j // 4, 0:n])
                        if c0 < 4096:
                            s5_segment(j, pbr, pbi, 0, n, hst[:, 0, j:j + 1], hst[:, 1, j:j + 1],
                                       hst[:, 0, j:j + 1], hst[:, 1, j:j + 1])
                        else:
                            s5_segment(j, pbr, pbi, 0, 16, cst[:, 2:3], cst[:, 2:3], hst[:, 0, j:j + 1], hst[:, 1, j:j + 1])
                            for s_ in range(NS):
                                o_re = (s_ * 2 + 0) * 16 + j
                                o_im = (s_ * 2 + 1) * 16 + j
                                s5_segment(j, pbr, pbi, 16 + 16 * s_, 16, st0t[:, o_re:o_re + 1], st0t[:, o_im:o_im + 1],
                                           sst[:, o_re:o_re + 1], sst[:, o_im:o_im + 1])
                    for cc in range(4 if "y" not in SKIP else 0):
                        for jj in range(4):
                            j = 4 * cc + jj
                            P.mm(ps[6][:, 0:n], Cre[:, j, :], hrb[:, j, 0:n], jj == 0, False)
                            P.mm(ps[6][:, 0:n], Cimn[:, j, :], hib[:, j, 0:n], False, jj == 3)
                        P.stt(yv[:, cc, 0:n], u32[:, cc, 0:n], pcol("s5d", cc), ps[6][:, 0:n], ALU.mult, ALU.add)
                        P.tt("vector", gt1[:, 0:n], yv[:, cc, 0:n], yv[:, cc, 0:n], ALU.mult)
                        P.ts("vector", gt1[:, 0:n], gt1[:, 0:n], 0.044715, 1.0, ALU.mult, ALU.add)
                        P.tt("vector", gt1[:, 0:n], gt1[:, 0:n], yv[:, cc, 0:n], ALU.mult)
                        P.act(gt2[:, 0:n], gt1[:, 0:n], AF.Sigmoid, scale=2.0 * math.sqrt(2.0 / math.pi))
                        P.tt("vector", g32[:, cc, 0:n], yv[:, cc, 0:n], gt2[:, 0:n], ALU.mult)
                        P.copy("scalar", gb[:, cc, 0:n], g32[:, cc, 0:n])
                    for cc in range(4 if "y" not in SKIP else 0):
                        for c in range(4):
                            P.mm(ps[6][:, 0:n], gluwb[:, c, cc * 128:(cc + 1) * 128], gb[:, c, 0:n], c == 0, c == 3)
                        P.act(gt2[:, 0:n], ps[6][:, 0:n], AF.Sigmoid, bias=pcol("glub", cc))
                        P.tt("vector", so[:, cc, 0:n], g32[:, cc, 0:n], gt2[:, 0:n], ALU.mult)
                    if "y" not in SKIP:
                        P.dma("gpsimd", R(mixin.t[0:512, c0:c0 + n].rearrange("(c p) t -> p c t", p=128), mixin.b), so[:, :, 0:n])
                    for hh in range(4 if "q" not in SKIP else 0):
                        pq = ps[4 + hh % 2]
                        for c in range(8):
                            P.mm(pq[:, 0:n], wab[:, c, 512 + hh * 128:512 + (hh + 1) * 128], h[:, c, 0:n], c == 0, c == 7)
                        sqt_ = sqtr[hh % 2]; rs_ = rsr[hh % 2]; pst = ps[6 + hh % 2]
                        P.act(sqt_[:, 0:n], pq[:, 0:n], AF.Square)
                        P.mm(pst[:, 0:n], blk64[:, :], sqt_[:, 0:n])
                        P.act(rs_[:, 0:n], pst[:, 0:n], AF.Sqrt, bias=eps_c, scale=1.0 / 64)
                        P.recip(rs_[:, 0:n], rs_[:, 0:n])
                        P.stt(qn[:, hh, 0:n], pq[:, 0:n], pcol("gq"), rs_[:, 0:n], ALU.mult, ALU.mult)
                    if "q" not in SKIP:
                        P.dma("gpsimd", R(QD.t[:, :, c0:c0 + n].rearrange("h p t -> p h t"), QD.b), qn[:, :, 0:n])
                    for hh in range(4):
                        pq = ps[4 + hh % 2]
                        for c in range(8):
                            P.mm(pq[:, 0:n], wab[:, c, 1024 + hh * 128:1024 + (hh + 1) * 128], h[:, c, 0:n], c == 0, c == 7)
                        sqt_ = sqtr[hh % 2]; rs_ = rsr[hh % 2]; pst = ps[6 + hh % 2]
                        P.act(sqt_[:, 0:n], pq[:, 0:n], AF.Square)
                        P.mm(pst[:, 0:n], blk64[:, :], sqt_[:, 0:n])
                        P.act(rs_[:, 0:n], pst[:, 0:n], AF.Sqrt, bias=eps_c, scale=1.0 / 64)
                        P.recip(rs_[:, 0:n], rs_[:, 0:n])
                        P.stt(kn32[:, hh, 0:n], pq[:, 0:n], pcol("gk"), rs_[:, 0:n], ALU.mult, ALU.mult)
                    store_tok(kn32, 4, c0, n, {"p": O["dkp"], "s": O["dks"]}, ktok, 512)
                    for hh in range(4 if "kvb" not in SKIP else 0):
                        P.copy("gpsimd", knb[:, hh, 0:n], kn32[:, hh, 0:n])
                    for slot, k0, lo, cnt in (key_dst(c0, n) if "kvs" not in SKIP else []):
                        P.dma("gpsimd", R(KD.t[slot, :, :, k0:k0 + cnt].rearrange("h p t -> p h t"), KD.b), knb[:, :, lo:lo + cnt])
                    pv = ps[6]
                    for c in range(8):
                        P.mm(pv[0:n, 0:512], h[:, c, 0:n], wab[:, c, 1536:2048], c == 0, c == 7)
                    P.copy("vector", vtok[0:n, :], pv[0:n, 0:512])
                    if "vtb" not in SKIP:
                        P.copy("gpsimd", vtb[0:n, :], vtok[0:n, :])
                    for grp, r0, lo, cnt in out_rows(c0, n):
                        P.dma("gpsimd", O["dv" + grp][r0:r0 + cnt, :], vtok[lo:lo + cnt, :])
                    for slot, k0, lo, cnt in (key_dst(c0, n) if "kvs" not in SKIP else []):
                        P.dma("gpsimd", VD[slot, k0:k0 + cnt, :], vtb[lo:lo + cnt, :])
                P.dma("gpsimd", R(s5p.t.rearrange("r p j -> p r j"), s5p.b), hst[:, :, :])
                P.dma("gpsimd", R(s5s.t.rearrange("s r p j -> p (s r) j"), s5s.b),
                      R(sst.t.rearrange("p (a j) -> p a j", j=16), sst.b))
                P.emit()


        def conv_cache(sbk, kc, vc, KDST, VDST, tag):
            ck32s = [sbk("ck32%s%d" % (tag, q), [128, 4, 512]) for q in range(2)]
            kbts = [sbk("kbt%s%d" % (tag, q), [128, 4, 512], BF16) for q in range(2)]
            cv32s = [sbk("cv32%s%d" % (tag, q), [128, 4, 512]) for q in range(2)]
            cvbs = [sbk("cvb%s%d" % (tag, q), [128, 4, 512], BF16) for q in range(2)]
            it_ = 0
            for s_ in range(NS):
                for g4 in range(8):
                    ck32, kbt, cv32, cvb = ck32s[it_ % 2], kbts[it_ % 2], cv32s[it_ % 2], cvbs[it_ % 2]
                    it_ += 1
                    r0 = g4 * 512
                    P.dma("sync", ck32[:, :, :], R(kc.t[s_, r0:r0 + 512, :].rearrange("(b p) w -> p b w", p=128), kc.b))
                    P.dma("sync", cv32[:, :, :], R(vc.t[s_, r0:r0 + 512, :].rearrange("(b p) w -> p b w", p=128), vc.b))
                    for b in range(4):
                        pT = ps[4 + b]
                        for hh in range(4):
                            P.tr(pT[:, hh * 128:(hh + 1) * 128], ck32[:, b, hh * 128:(hh + 1) * 128], ident[:, :])
                        for hh in range(4):
                            P.copy("vector" if b % 2 else "scalar", kbt[:, hh, b * 128:(b + 1) * 128], pT[:, hh * 128:(hh + 1) * 128])
                    P.dma("gpsimd", R(KDST.t[1 + s_, :, :, r0:r0 + 512].rearrange("h p t -> p h t"), KDST.b), kbt[:, :, :])
                    P.copy("gpsimd", R(cvb.t[:, :, :].rearrange("p b w -> p (b w)"), cvb.b), R(cv32.t[:, :, :].rearrange("p b w -> p (b w)"), cv32.b))
                    P.dma("gpsimd", R(VDST.t[1 + s_, r0:r0 + 512, :].rearrange("(b p) w -> p b w", p=128), VDST.b), cvb[:, :, :])

        def attn_blocks(slot, Q):
            if Q == "meta":
                return [(4096, 16, 32, 0, "diag16")]
            if Q == "sample":
                return [(128 * j, 128, j, 0, "full") for j in range(32)] + [(4096, 16, 32, 0, "diag16")]
            bl = [(4096, 16, 32, 0, "full")] + [(128 * j, 128, j, 0, "full") for j in range(4 * Q)]
            bl += [(128 * (4 * Q + r), 128, 4 * Q + r, 128 * r, "diag128") for r in range(4)]
            return bl

        def m0b_phase():
            with ExitStack() as st:
                def sb(name, shape, dt=F32):
                    return TT(st.enter_context(nc.sbuf_tensor(name + "_m0b", shape, dt)))
                conv_cache(sb, cdk, cdv, KD, VD, "d")
                KA = [sb("KA%d" % k, [68, 2, TK], BF16) for k in range(2)]
                QA = [sb("QA%d" % k, [68, 2, LP], BF16) for k in range(2)]
                Vh = [sb("Vh%d" % k, [128, 33, 128], BF16) for k in range(2)]
                ptb = [sb("ptb%d" % k, [128, 512], BF16) for k in range(4)]
                tmpf = [sb("tmpf%d" % k, [128, 128]) for k in range(2)]
                nr = [sb("nr%d" % k, [128, 512]) for k in range(7)]
                ob = sb("ob", [128, 512], BF16)
                cdb = sb("cdb", [128, 512]); cds = sb("cds", [16, 64])
                lw = sb("lw", [128, 8])
                P.dma("sync", cdb[:, :], cdbig[:, :])
                P.dma("sync", cds[:, :], cdsm[:, :])
                P.tt("vector", lw[0:64, 0:1], pcol("lamv", 0, 1, 64), pcol("lamv", 1, 1, 64), ALU.mult)
                P.tt("vector", lw[0:64, 1:2], pcol("lamv", 2, 1, 64), pcol("lamv", 3, 1, 64), ALU.mult)
                P.mm(ps[7][:, 0:2], ones[0:64, :], lw[0:64, 0:2])
                P.act(lw[:, 2:4], ps[7][:, 0:2], AF.Exp)
                P.ts("vector", lw[:, 4:5], lw[:, 2:3], lw[:, 3:4], LAM_INIT0, ALU.subtract, ALU.add)
                P.ts("vector", lw[:, 5:6], lw[:, 4:5], -1.0, None, ALU.mult)
                P.ts("vector", lw[:, 6:7], pcol("gsub"), 1.0 - LAM_INIT0, None, ALU.mult)
                it = 0
                sctr = [0]
                for slot in range(1 + NS):
                    g = 0 if slot == 0 else 1
                    for hh in range(4):
                        ka = KA[it % 2]; qa = QA[it % 2]; vh = Vh[it % 2]
                        it += 1
                        P.dma("sync", ka[0:64, :, :], R(KD.t[slot, hh, :, :].rearrange("(m d) t -> d m t", m=2), KD.b))
                        for m in range(2):
                            P.dma("sync", ka[64:68, m, :], AUGK[g, hh, :, :])
                        if slot == 0:
                            P.dma("sync", qa[0:64, :, :], R(QD.t[hh, :, 0:LP].rearrange("(m d) t -> d m t", m=2), QD.b))
                            for m in range(2):
                                P.dma("sync", qa[64:68, m, :], AUGQ[0, hh, :, :])
                            qtiles = [(512 * Q, 512, Q) for Q in range(8)] + [(4096, 16, "meta")]
                            xcol0 = 0
                        else:
                            xc = LP + 16 * (slot - 1)
                            P.dma("sync", qa[0:64, :, 0:16], R(QD.t[hh, :, xc:xc + 16].rearrange("(m d) t -> d m t", m=2), QD.b))
                            for m in range(2):
                                P.dma("sync", qa[64:68, m, 0:16], AUGQ[1, hh, :, 4096:4112])
                            qtiles = [(0, 16, "sample")]
                            xcol0 = xc
                        P.dma("sync", vh[:, 0:32, :], R(VD.t[slot, 0:4096, hh * 128:(hh + 1) * 128].rearrange("(b p) d -> p b d", p=128), VD.b))
                        P.dma("sync", vh[0:16, 32, :], VD[slot, 4096:4112, hh * 128:(hh + 1) * 128])
                        for (q0, nq, Q) in qtiles:
                            blocks = attn_blocks(slot, Q)
                            nb = len(blocks)
                            if Q == "sample":
                                groups = [blocks[g8 * 8:(g8 + 1) * 8] for g8 in range(4)] + [[blocks[32]]]
                            else:
                                groups = [[blk_] for blk_ in blocks]
                            units = [(gi, grp, m) for gi, grp in enumerate(groups) for m in range(2)]
                            ng = len(groups)
                            ust = {}

                            def qk(ui):
                                gi, grp, m = units[ui]
                                pS_ = ps[sctr[0] % 3]; pt_ = ptb[sctr[0] % 4]; tf = tmpf[sctr[0] % 2]
                                sctr[0] += 1
                                ust[ui] = (pS_, pt_, tf)
                                if len(grp) > 1:
                                    for bb, (k0, nk, jb, qs, kind) in enumerate(grp):
                                        P.mm(pS_[0:nk, bb * 16:(bb + 1) * 16], ka[0:68, m, k0:k0 + nk], qa[0:68, m, q0:q0 + 16])
                                else:
                                    (k0, nk, jb, qs, kind) = grp[0]
                                    P.mm(pS_[0:nk, qs:nq], ka[0:68, m, k0:k0 + nk], qa[0:68, m, q0 + qs:q0 + nq])

                            def rest(ui):
                                gi, grp, m = units[ui]
                                pS_, pt_, tf = ust.pop(ui)
                                first_, last_ = gi == 0, gi == ng - 1
                                if len(grp) > 1:
                                    P.act(pt_[0:128, 0:128], pS_[0:128, 0:128], AF.Exp, scale=0.125)
                                    for bb, (k0, nk, jb, qs, kind) in enumerate(grp):
                                        P.mm(ps[3 + 2 * m][:, 0:16], vh[0:nk, jb, :], pt_[0:nk, bb * 16:(bb + 1) * 16], first_ and bb == 0, False)
                                        P.mm(ps[4 + 2 * m][:, 0:16], onesb[0:nk, :], pt_[0:nk, bb * 16:(bb + 1) * 16], first_ and bb == 0, False)
                                    return
                                (k0, nk, jb, qs, kind) = grp[0]
                                if kind == "diag128":
                                    P.stt(tf[0:128, 0:128], pS_[0:128, qs:qs + 128], 0.125, cdb[:, hh * 128:(hh + 1) * 128], ALU.mult, ALU.add)
                                    P.act(pt_[0:128, qs:qs + 128], tf[0:128, 0:128], AF.Exp)
                                    if qs + 128 < nq:
                                        P.act(pt_[0:128, qs + 128:nq], pS_[0:128, qs + 128:nq], AF.Exp, scale=0.125)
                                elif kind == "diag16":
                                    P.stt(tf[0:16, 0:16], pS_[0:16, 0:16], 0.125, cds[:, hh * 16:(hh + 1) * 16], ALU.mult, ALU.add)
                                    P.act(pt_[0:16, 0:16], tf[0:16, 0:16], AF.Exp)
                                else:
                                    P.act(pt_[0:nk, qs:nq], pS_[0:nk, qs:nq], AF.Exp, scale=0.125)
                                P.mm(ps[3 + 2 * m][:, qs:nq], vh[0:nk, jb, :], pt_[0:nk, qs:nq], first_, last_)
                                P.mm(ps[4 + 2 * m][:, qs:nq], onesb[0:nk, :], pt_[0:nk, qs:nq], first_, last_)
                            qk(0)
                            for ui in range(len(units)):
                                if ui + 1 < len(units):
                                    qk(ui + 1)
                                rest(ui)
                            r0_, A_, r1_, B_, Ot, sq_, rsn = [nr[q][:, 0:nq] for q in range(7)]
                            P.recip(r0_, ps[4][:, 0:nq])
                            P.tt("vector", A_, ps[3][:, 0:nq], r0_, ALU.mult)
                            P.recip(r1_, ps[6][:, 0:nq])
                            P.tt("vector", B_, ps[5][:, 0:nq], r1_, ALU.mult)
                            P.stt(Ot, B_, lw[:, 5:6], A_, ALU.mult, ALU.add)
                            P.act(sq_, Ot, AF.Square)
                            P.mm(ps[7][:, 0:nq], ones[:, :], sq_)
                            P.act(rsn, ps[7][:, 0:nq], AF.Sqrt, bias=eps_c, scale=1.0 / 128)
                            P.recip(rsn, rsn)
                            P.stt(ob[:, 0:nq], Ot, lw[:, 6:7], rsn, ALU.mult, ALU.mult)
                            xc0 = xcol0 + q0
                            P.dma("gpsimd", mixin[512 + hh * 128:512 + (hh + 1) * 128, xc0:xc0 + nq], ob[:, 0:nq])
                P.emit()

        def mixout_phase(wsrc, tag):
            with ExitStack() as st:
                def sb(name, shape, dt=F32):
                    return TT(st.enter_context(nc.sbuf_tensor(name + "_mo" + tag, shape, dt)))
                wmo = sb("wmo", [128, 8, D], BF16)
                stg = [sb("stg%d" % k, [128, 1024]) for k in range(2)]
                xT = sb("xT", [128, 8, NT]); mx = sb("mx", [128, 8, NT], BF16)
                engs = ("vector", "gpsimd", "scalar")
                for c in range(8):
                    P.dma("sync", stg[c % 2][:, :], wsrc[c * 128:(c + 1) * 128, :])
                    P.copy(engs[c % 3], wmo[:, c, :], stg[c % 2][:, :])
                for (c0, n) in TILES_F:
                    P.dma("sync", xT[:, :, 0:n], xres[:, :, c0:c0 + n])
                    P.dma("sync", mx[:, :, 0:n], R(mixin.t[:, c0:c0 + n].rearrange("(c p) t -> p c t", p=128), mixin.b))
                    for m in range(8):
                        po = ps[m % 4]
                        for c in range(8):
                            P.mm(po[:, 0:n], wmo[:, c, m * 128:(m + 1) * 128], mx[:, c, 0:n], c == 0, c == 7)
                        P.tt("vector", xT[:, m, 0:n], xT[:, m, 0:n], po[:, 0:n], ALU.add)
                    P.dma("gpsimd", xres[:, :, c0:c0 + n], xT[:, :, 0:n])
                P.emit()

        def dump_x_phase():
            with ExitStack() as st:
                xT = TT(st.enter_context(nc.sbuf_tensor("xT_dump", [128, 8, NT], F32)))
                ytok = TT(st.enter_context(nc.sbuf_tensor("ytok_dump", [128, D], F32)))
                for (c0, n) in TILES_F:
                    P.dma("sync", xT[:, :, 0:n], xres[:, :, c0:c0 + n])
                    store_tok(xT, 8, c0, n, {"p": yp, "s": ys}, ytok, D, is_y=True)
                P.emit()


        def m1a_phase():
            NA = 128
            with ExitStack() as st:
                def sb(name, shape, dt=F32):
                    return TT(st.enter_context(nc.sbuf_tensor(name + "_m1a", shape, dt)))
                wcd = sb("wcd", [128, 8, CDW], BF16); wkr = sb("wkr", [128, 8, 32], BF16)
                stg = [sb("stg%d" % k, [128, 1024]) for k in range(2)]
                qbb = sb("qbb", [128, 3, 768], BF16); qbrb = sb("qbrb", [128, 3, 768], BF16)
                kvbb = sb("kvbb", [128, 2, 1024], BF16)
                xT = sb("xT", [128, 8, NA]); sqc = [sb("sqc%d" % k, [128, NA]) for k in range(2)]
                rstd = sb("rstd", [128, NA]); h = sb("h", [128, 8, NA], BF16)
                sqt = sb("sqt", [128, NA]); rs = sb("rs", [128, NA])
                qnf = sb("qnf", [128, 4, NA], BF16); kn32 = sb("kn32", [128, 4, NA]); knb = sb("knb", [128, 4, NA], BF16)
                ktok = sb("ktok", [128, 512]); vtok = sb("vtok", [128, 512]); vtb = sb("vtb", [128, 512], BF16)
                ones8 = sb("ones8", [8, 512]); lf32 = sb("lf32", [8, 512]); Fc = sb("Fc", [8, 512]); fe = sb("fe", [8, 512])
                fst = sb("fst", [8, 8])
                x8 = sb("x8", [8, 512]); r1 = sb("r1", [8, 512])
                sp_hi = [sb("sphi%d" % r, [8, 512], BF16) for r in range(3)]
                sn_hi = [sb("snhi%d" % r, [8, 512], BF16) for r in range(3)]
                lftok = sb("lftok", [128, 8]); lc32 = sb("lc32", [128, 4, 8])
                qa32 = sb("qa32", [128, 3, NA]); qan = sb("qan", [128, 3, NA], BF16)
                cos_t = sb("cos_t", [96, NA]); sin_t = sb("sin_t", [96, NA]); cos3 = sb("cos3", [32, NA]); sin3 = sb("sin3", [32, NA])
                qt1 = sb("qt1", [96, NA]); qt2 = sb("qt2", [96, NA]); qm = sb("qm", [96, NA], BF16)
                gmq = sb("gmq", [96, 1])
                ck32 = sb("ck32", [128, 2, NA]); ckn32 = sb("ckn32", [128, 2, NA]); ckvb = sb("ckvb", [128, 2, NA], BF16)
                cktok = sb("cktok", [128, 256])
                kpe32 = sb("kpe32", [32, NA]); kpeb = sb("kpeb", [32, NA], BF16); kptok = sb("kptok", [128, 32])
                knm = sb("knm", [128, 4, NA], BF16); ssq = sb("ssq", [128, NA]); spe = sb("spe", [128, NA]); sqp = sb("sqp", [32, NA])
                kscf = sb("kscf", [128, NA]); ksct = sb("ksct", [128, 8])
                kp_sets = [dict(knm=sb("knm_%d" % q, [128, 4, NA], BF16), sqk=sb("sqk_%d" % q, [128, 512]), ss8=sb("ss8_%d" % q, [128, 9]),
                                sqp2=sb("sqp2_%d" % q, [128, 32]), ksct=sb("ksct_%d" % q, [128, 8]), vtb=sb("vtbk_%d" % q, [128, 512], BF16),
                                ckvb=sb("ckvbp_%d" % q, [128, 2, NA], BF16), kpe32=sb("kpe32p_%d" % q, [32, NA]), kpeb=sb("kpebp_%d" % q, [32, NA], BF16))
                           for q in range(2)]
                kpctr = [0]
                sqt2 = [sb("sqt2%d" % k, [128, NA]) for k in range(2)]; rs2 = [sb("rs2%d" % k, [128, NA]) for k in range(2)]
                qt1s = [sb("qt1s%d" % k, [96, NA]) for k in range(2)]; qt2s = [sb("qt2s%d" % k, [96, NA]) for k in range(2)]
                qms = [sb("qms%d" % k, [96, NA], BF16) for k in range(2)]
                cc32 = sb("cc32", [128, 4, 256]); cp32 = sb("cp32", [128, 4, 32])
                engs = ("vector", "gpsimd", "scalar")
                k = 0
                for c in range(8):
                    for n0 in range(0, CDW, 1024):
                        w = min(1024, CDW - n0)
                        P.dma("sync", stg[k % 2][:, 0:w], cdwi[c * 128:(c + 1) * 128, n0:n0 + w])
                        P.copy(engs[k % 3], wcd[:, c, n0:n0 + w], stg[k % 2][:, 0:w])
                        k += 1
                    P.copy("vector", wkr[:, c, 0:16], wcd[:, c, 2200:2216])
                    P.copy("vector", wkr[:, c, 16:32], wcd[:, c, 2184:2200])
                for c in range(3):
                    for src, dst in ((qb, qbb), (qbr, qbrb)):
                        P.dma("sync", stg[k % 2][:, 0:768], src[c * 128:(c + 1) * 128, :])
                        P.copy(engs[k % 3], dst[:, c, :], stg[k % 2][:, 0:768])
                        k += 1
                for c in range(2):
                    P.dma("sync", stg[k % 2][:, :], kvb[c * 128:(c + 1) * 128, :])
                    P.copy(engs[k % 3], kvbb[:, c, :], stg[k % 2][:, :])
                    k += 1
                P.memset("vector", ones8[:, :], 1.0)
                P.memset("vector", fst[:, :], 0.0)
                P.ts("vector", fst[:, 5:6], pcol("fbias", 0, 1, 8), -1.0, None, ALU.mult)
                P.tt("vector", gmq[:, :], pcol("mqn", 0, 1, 96), pcol("mkn", 0, 1, 96), ALU.mult)

                def f_split(n_, kdst, qdst):
                    P.ts("vector", x8[:, 0:n_], Fc[:, 0:n_], 8.0, None, ALU.mult)
                    P.copy("vector", sp_hi[0][:, 0:n_], x8[:, 0:n_])
                    P.tt("vector", r1[:, 0:n_], x8[:, 0:n_], sp_hi[0][:, 0:n_], ALU.subtract)
                    P.copy("vector", sp_hi[1][:, 0:n_], r1[:, 0:n_])
                    P.tt("vector", r1[:, 0:n_], r1[:, 0:n_], sp_hi[1][:, 0:n_], ALU.subtract)
                    P.copy("vector", sp_hi[2][:, 0:n_], r1[:, 0:n_])
                    for r in range(3):
                        P.ts("vector", sn_hi[r][:, 0:n_], sp_hi[r][:, 0:n_], -1.0, None, ALU.mult)
                    for (slot, k0, lo, cnt) in kdst:
                        for r in range(3):
                            P.dma("gpsimd", FK[slot, :, r, k0:k0 + cnt], sn_hi[r][:, lo:lo + cnt])
                    for (x0, lo, cnt) in qdst:
                        for r in range(3):
                            P.dma("gpsimd", FQ[:, r, x0:x0 + cnt], sp_hi[r][:, lo:lo + cnt])

                def mla_kprep(ckvb_, kpe_tok, n_, slot_list, blk):
                    S_ = kp_sets[kpctr[0] % 2]
                    kpctr[0] += 1
                    knm, sqk, ss8, sqp2, ksct, vtb = S_["knm"], S_["sqk"], S_["ss8"], S_["sqp2"], S_["ksct"], S_["vtb"]
                    for cc in range(4):
                        pa = ps[cc % 2]
                        for c in range(2):
                            P.mm(pa[:, 0:n_], kvbb[:, c, cc * 128:(cc + 1) * 128], ckvb_[:, c, 0:n_], c == 0, c == 1)
                        P.copy("vector", knm[:, cc, 0:n_], pa[:, 0:n_])
                    for c in range(2):
                        P.mm(ps[2][0:n_, 0:512], ckvb_[:, c, 0:n_], kvbb[:, c, 0:512], c == 0, c == 1)
                    P.act(sqk[0:n_, :], ps[2][0:n_, 0:512], AF.Square)
                    P.reduce(ss8[0:n_, 0:8], R(sqk.t[0:n_, :].rearrange("p (h d) -> p h d", d=64), sqk.b))
                    P.act(sqp2[0:n_, :], kpe_tok, AF.Square)
                    P.reduce(ss8[0:n_, 8:9], sqp2[0:n_, :])
                    P.ts("vector", ss8[0:n_, 0:8], ss8[0:n_, 0:8], ss8[0:n_, 8:9], None, ALU.add)
                    P.act(ss8[0:n_, 0:8], ss8[0:n_, 0:8], AF.Sqrt, bias=cst[0:n_, 0:1], scale=1.0 / 96)
                    P.recip(ss8[0:n_, 0:8], ss8[0:n_, 0:8])
                    P.ts("vector", ksct[0:n_, :], ss8[0:n_, 0:8], 96.0 ** -0.5, None, ALU.mult)
                    for c in range(2):
                        P.mm(ps[3][0:n_, 0:512], ckvb_[:, c, 0:n_], kvbb[:, c, 512:1024], c == 0, c == 1)
                    P.copy("vector", vtb[0:n_, :], ps[3][0:n_, 0:512])
                    for (slot, k0, lo, cnt) in slot_list:
                        P.dma("gpsimd", R(KM.t[slot, :, :, k0:k0 + cnt].rearrange("h p t -> p h t"), KM.b), knm[:, :, lo:lo + cnt])
                        P.dma("gpsimd", VM[slot, k0:k0 + cnt, :], vtb[lo:lo + cnt, :])
                        P.dma("gpsimd", ksc_all[0:cnt, slot, blk if k0 < 4096 else 32, :], ksct[lo:lo + cnt, :])

                for s_ in range(NS):
                    for g4 in range(8):
                        r0 = g4 * 512
                        P.dma("sync", lc32[:, :, :], R(clf.t[s_, r0:r0 + 512, :].rearrange("(b p) w -> p b w", p=128), clf.b))
                        for b in range(4):
                            P.tr(ps[5][0:8, b * 128:(b + 1) * 128], lc32[:, b, :], ident[:, :])
                        P.copy("vector", lf32[:, :], ps[5][0:8, 0:512])
                        P.scan(Fc[:, :], ones8[:, :], lf32[:, :], fst[:, 1 + s_:2 + s_])
                        P.copy("vector", fst[:, 1 + s_:2 + s_], Fc[:, 511:512])
                        f_split(512, [(1 + s_, r0, 0, 512)], [])
                        P.dma("sync", cc32[:, :, :], R(cckv.t[s_, r0:r0 + 512, :].rearrange("(b p) w -> p b w", p=128), cckv.b))
                        P.dma("sync", cp32[:, :, :], R(ckpe.t[s_, r0:r0 + 512, :].rearrange("(b p) w -> p b w", p=128), ckpe.b))
                        for b in range(4):
                            S2 = kp_sets[kpctr[0] % 2]
                            ckvb_p, kpe32_p, kpeb_p = S2["ckvb"], S2["kpe32"], S2["kpeb"]
                            for c in range(2):
                                P.tr(ps[6][:, c * 128:(c + 1) * 128], cc32[:, b, c * 128:(c + 1) * 128], ident[:, :])
                                P.copy("vector", ckvb_p[:, c, :], ps[6][:, c * 128:(c + 1) * 128])
                            P.tr(ps[7][0:32, 0:128], cp32[:, b, :], ident[:, :])
                            P.copy("vector", kpe32_p[:, :], ps[7][0:32, 0:128])
                            P.copy("gpsimd", kpeb_p[:, :], kpe32_p[:, :])
                            P.dma("gpsimd", KP[1 + s_, :, r0 + b * 128:r0 + (b + 1) * 128], kpeb_p[:, :])
                            mla_kprep(ckvb_p, cp32[:, b, :], 128, [(1 + s_, r0 + b * 128, 0, 128)], g4 * 4 + b)

                for (c0, n) in TILES_A:
                    P.dma("sync", xT[:, :, 0:n], xres[:, :, c0:c0 + n])
                    P.dma("sync", cos_t[:, 0:n], cos96[:, c0:c0 + n]); P.dma("sync", sin_t[:, 0:n], sin96[:, c0:c0 + n])
                    P.dma("sync", cos3[:, 0:n], cos32[:, c0:c0 + n]); P.dma("sync", sin3[:, 0:n], sin32[:, c0:c0 + n])
                    rms_h(xT, h, sqc, rstd, "mix_norm1", n)
                    for which in range(2):
                        for hh in range(4):
                            pq = ps[4 + hh % 2]
                            for c in range(8):
                                P.mm(pq[:, 0:n], wcd[:, c, which * 512 + hh * 128:which * 512 + (hh + 1) * 128], h[:, c, 0:n], c == 0, c == 7)
                            sqt_ = sqt2[hh % 2]; rs_ = rs2[hh % 2]; pst = ps[6 + hh % 2]
                            P.act(sqt_[:, 0:n], pq[:, 0:n], AF.Square)
                            P.mm(pst[:, 0:n], blk64[:, :], sqt_[:, 0:n])
                            P.act(rs_[:, 0:n], pst[:, 0:n], AF.Sqrt, bias=eps_c, scale=1.0 / 64)
                            P.recip(rs_[:, 0:n], rs_[:, 0:n])
                            if which == 0:
                                P.stt(qnf[:, hh, 0:n], pq[:, 0:n], pcol("fgq"), rs_[:, 0:n], ALU.mult, ALU.mult)
                            else:
                                P.stt(kn32[:, hh, 0:n], pq[:, 0:n], pcol("fgk"), rs_[:, 0:n], ALU.mult, ALU.mult)
                                P.copy("gpsimd", knb[:, hh, 0:n], kn32[:, hh, 0:n])
                    P.dma("gpsimd", R(QF.t[:, :, c0:c0 + n].rearrange("h p t -> p h t"), QF.b), qnf[:, :, 0:n])
                    store_tok(kn32, 4, c0, n, {"p": O["fkp"], "s": O["fks"]}, ktok, 512)
                    for slot, k0, lo, cnt in key_dst(c0, n):
                        P.dma("gpsimd", R(KF.t[slot, :, :, k0:k0 + cnt].rearrange("h p t -> p h t"), KF.b), knb[:, :, lo:lo + cnt])
                    for c in range(8):
                        P.mm(ps[6][0:n, 0:512], h[:, c, 0:n], wcd[:, c, 1024:1536], c == 0, c == 7)
                    P.copy("vector", vtok[0:n, :], ps[6][0:n, 0:512])
                    P.copy("gpsimd", vtb[0:n, :], vtok[0:n, :])
                    for grp, r0, lo, cnt in out_rows(c0, n):
                        P.dma("gpsimd", O["fv" + grp][r0:r0 + cnt, :], vtok[lo:lo + cnt, :])
                    for slot, k0, lo, cnt in key_dst(c0, n):
                        P.dma("gpsimd", VF[slot, k0:k0 + cnt, :], vtb[lo:lo + cnt, :])
                    for c in range(8):
                        P.mm(ps[4][0:8, 0:n], wcd[:, c, 1536:1544], h[:, c, 0:n], c == 0, c == 7)
                    P.act(fe[:, 0:n], ps[4][0:8, 0:n], AF.Exp, bias=fst[:, 5:6], scale=-1.0)
                    P.act(fe[:, 0:n], fe[:, 0:n], AF.Ln, bias=cst[0:8, 3:4])
                    P.ts("vector", lf32[:, 0:n], fe[:, 0:n], -1.0, None, ALU.mult)
                    P.tr(ps[5][0:n, 0:8], lf32[:, 0:n], ident[0:8, 0:8])
                    P.copy("vector", lftok[0:n, :], ps[5][0:n, 0:8])
                    for grp, r0, lo, cnt in out_rows(c0, n):
                        P.dma("gpsimd", O["lf" + grp][r0:r0 + cnt, :], lftok[lo:lo + cnt, :])
                    if c0 < 4096:
                        P.scan(Fc[:, 0:n], ones8[:, 0:n], lf32[:, 0:n], fst[:, 0:1])
                        P.copy("vector", fst[:, 0:1], Fc[:, n - 1:n])
                        f_split(n, [(0, c0, 0, n)], [(c0, 0, n)])
                    else:
                        P.scan(Fc[:, 0:16], ones8[:, 0:16], lf32[:, 0:16], cst[0:8, 2:3])
                        P.copy("vector", fst[:, 0:1], Fc[:, 15:16])
                        for s_ in range(NS):
                            o_ = 16 + 16 * s_
                            P.scan(Fc[:, o_:o_ + 16], ones8[:, 0:16], lf32[:, o_:o_ + 16], fst[:, 1 + s_:2 + s_])
                        f_split(80, key_dst(c0, n), [(c0, 0, n)])
                    for cc in range(3):
                        pq = ps[4 + cc % 2]
                        for c in range(8):
                            P.mm(pq[:, 0:n], wcd[:, c, 1544 + cc * 128:1544 + (cc + 1) * 128], h[:, c, 0:n], c == 0, c == 7)
                        P.copy("vector", qa32[:, cc, 0:n], pq[:, 0:n])
                        P.act(sqt[:, 0:n], qa32[:, cc, 0:n], AF.Square)
                        P.mm(ps[7][:, 0:n], ones[:, :], sqt[:, 0:n], cc == 0, cc == 2)
                    P.act(rs[:, 0:n], ps[7][:, 0:n], AF.Sqrt, bias=eps_c, scale=1.0 / 384)
                    P.recip(rs[:, 0:n], rs[:, 0:n])
                    for cc in range(3):
                        P.stt(qan[:, cc, 0:n], qa32[:, cc, 0:n], pcol("qan", cc), rs[:, 0:n], ALU.mult, ALU.mult)
                    for hh in range(8):
                        pa = ps[(hh % 2) * 2]; pb = ps[(hh % 2) * 2 + 1]
                        for c in range(3):
                            P.mm(pa[0:96, 0:n], qbb[:, c, hh * 96:(hh + 1) * 96], qan[:, c, 0:n], c == 0, c == 2)
                        for c in range(3):
                            P.mm(pb[0:96, 0:n], qbrb[:, c, hh * 96:(hh + 1) * 96], qan[:, c, 0:n], c == 0, c == 2)
                        qt1_ = qt1s[hh % 2]; qt2_ = qt2s[hh % 2]; qm_ = qms[hh % 2]; pst = ps[6 + hh % 2]
                        P.tt("vector", qt1_[:, 0:n], pa[0:96, 0:n], cos_t[:, 0:n], ALU.mult)
                        P.tt("vector", qt2_[:, 0:n], pb[0:96, 0:n], sin_t[:, 0:n], ALU.mult)
                        P.tt("vector", qt1_[:, 0:n], qt1_[:, 0:n], qt2_[:, 0:n], ALU.add)
                        P.act(qt2_[:, 0:n], qt1_[:, 0:n], AF.Square)
                        P.mm(pst[0:96, 0:n], ones[0:96, 0:96], qt2_[:, 0:n])
                        P.act(qt2_[:, 0:n], pst[0:96, 0:n], AF.Sqrt, bias=cst[0:96, 0:1], scale=1.0 / 96)
                        P.recip(qt2_[:, 0:n], qt2_[:, 0:n])
                        P.stt(qm_[:, 0:n], qt1_[:, 0:n], gmq[:, :], qt2_[:, 0:n], ALU.mult, ALU.mult)
                        P.dma("gpsimd", QM[hh, :, c0:c0 + n], qm_[:, 0:n])
                    for cc in range(2):
                        pq = ps[4 + cc % 2]
                        for c in range(8):
                            P.mm(pq[:, 0:n], wcd[:, c, 1928 + cc * 128:1928 + (cc + 1) * 128], h[:, c, 0:n], c == 0, c == 7)
                        P.copy("vector", ck32[:, cc, 0:n], pq[:, 0:n])
                        P.act(sqt[:, 0:n], ck32[:, cc, 0:n], AF.Square)
                        P.mm(ps[7][:, 0:n], ones[:, :], sqt[:, 0:n], cc == 0, cc == 1)
                    P.act(rs[:, 0:n], ps[7][:, 0:n], AF.Sqrt, bias=eps_c, scale=1.0 / 256)
                    P.recip(rs[:, 0:n], rs[:, 0:n])
                    for cc in range(2):
                        P.stt(ckn32[:, cc, 0:n], ck32[:, cc, 0:n], pcol("kvan", cc), rs[:, 0:n], ALU.mult, ALU.mult)
                        P.copy("gpsimd", ckvb[:, cc, 0:n], ckn32[:, cc, 0:n])
                    store_tok(ckn32, 2, c0, n, {"p": O["ckvp"], "s": O["ckvs"]}, cktok, 256)
                    for c in range(8):
                        P.mm(ps[0][0:32, 0:n], wcd[:, c, 2184:2216], h[:, c, 0:n], c == 0, c == 7)
                    for c in range(8):
                        P.mm(ps[1][0:32, 0:n], wkr[:, c, :], h[:, c, 0:n], c == 0, c == 7)
                    P.tt("vector", kpe32[:, 0:n], ps[0][0:32, 0:n], cos3[:, 0:n], ALU.mult)
                    P.tt("vector", sqp[:, 0:n], ps[1][0:32, 0:n], sin3[:, 0:n], ALU.mult)
                    P.tt("vector", kpe32[:, 0:n], kpe32[:, 0:n], sqp[:, 0:n], ALU.add)
                    P.copy("gpsimd", kpeb[:, 0:n], kpe32[:, 0:n])
                    P.tr(ps[5][0:n, 0:32], kpe32[:, 0:n], ident[0:32, 0:32])
                    P.copy("vector", kptok[0:n, :], ps[5][0:n, 0:32])
                    for grp, r0, lo, cnt in out_rows(c0, n):
                        P.dma("gpsimd", O["kpe" + grp][r0:r0 + cnt, :], kptok[lo:lo + cnt, :])
                    for slot, k0, lo, cnt in key_dst(c0, n):
                        P.dma("gpsimd", KP[slot, :, k0:k0 + cnt], kpeb[:, lo:lo + cnt])
                    mla_kprep(ckvb, kptok[0:n, :], n, key_dst(c0, n), c0 // 128)
                P.emit()


        def m1b_phase():
            with ExitStack() as st:
                def sb(name, shape, dt=F32):
                    return TT(st.enter_context(nc.sbuf_tensor(name + "_m1b", shape, dt)))
                conv_cache(sb, cfk, cfv, KF, VF, "f")
                KAs = {kd: [sb("KA%s%d" % (kd, k), [96, TK], BF16) for k in range(2)] for kd in ("f", "m")}
                QAs = {kd: [sb("QA%s%d" % (kd, k), [96, LP], BF16) for k in range(2)] for kd in ("f", "m")}
                Vh = [sb("Vh%d" % k, [128, 33 * 65], BF16) for k in range(2)]
                ptb = [sb("ptb%d" % k, [128, 512], BF16) for k in range(4)]
                tmpf = [sb("tmpf%d" % k, [128, 128]) for k in range(2)]
                os_ = sb("os", [65, 512]); rr = sb("rr", [64, 512]); ob = sb("ob", [64, 512], BF16)
                cfb = sb("cfb", [128, 128]); cfs = sb("cfs", [16, 16]); cmb = sb("cmb", [128, 128])
                P.dma("sync", cfb[:, :], cfbig[:, :]); P.dma("sync", cfs[:, :], cfsm[:, :]); P.dma("sync", cmb[:, :], cmbig[:, :])
                for k in range(2):
                    P.memset("vector", KAs["f"][k][64:96, :], 1.0)
                    P.memset("vector", QAs["f"][k][64:96, :], 1.0)
                    P.memset("vector", Vh[k][:, :], 1.0)
                it = 0
                sctr = [0]
                for kd in ("f", "m"):
                    RW = 70 if kd == "f" else 96
                    KSRC = KF if kd == "f" else KM
                    VSRC = VF if kd == "f" else VM
                    for slot in range(1 + NS):
                        for hh in range(8):
                            ka = KAs[kd][it % 2]; qa = QAs[kd][it % 2]; vh = Vh[it % 2]
                            vh3 = TT(vh.t[:, :].rearrange("p (b d) -> p b d", d=65), vh.b)
                            it += 1
                            hp, hl = hh // 2, 64 * (hh % 2)
                            P.dma("sync", ka[0:64, :], KSRC[slot, hp, hl:hl + 64, :])
                            if slot == 0:
                                nq_all, xcol0 = LP, 0
                                qtiles = [(512 * Q, 512, Q) for Q in range(8)] + [(4096, 16, "meta")]
                            else:
                                nq_all, xcol0 = 16, LP + 16 * (slot - 1)
                                qtiles = [(0, 16, "sample")]
                            if kd == "f":
                                P.dma("sync", ka[64:67, :], FK[slot, hh, :, :])
                                P.dma("sync", qa[0:64, 0:nq_all], QF[hp, hl:hl + 64, xcol0:xcol0 + nq_all])
                                P.dma("sync", qa[67:70, 0:nq_all], FQ[hh, :, xcol0:xcol0 + nq_all])
                            else:
                                P.dma("sync", ka[64:96, :], KP[slot, :, :])
                                P.dma("sync", qa[0:96, 0:nq_all], QM[hh, :, xcol0:xcol0 + nq_all])
                            P.dma("sync", vh3[:, 0:32, 0:64], R(VSRC.t[slot, 0:4096, hh * 64:(hh + 1) * 64].rearrange("(b p) d -> p b d", p=128), VSRC.b))
                            P.dma("sync", vh3[0:16, 32, 0:64], VSRC[slot, 4096:4112, hh * 64:(hh + 1) * 64])
                            for qi, (q0, nq, Q) in enumerate(qtiles):
                                blocks = attn_blocks(slot, Q)
                                if kd == "m":
                                    blocks = [(a, b, c, d, "full" if e == "diag16" else e) for (a, b, c, d, e) in blocks]
                                nb = len(blocks)
                                pO = ps[4 + qi % 2]
                                if Q == "sample" and kd == "f":
                                    groups = [blocks[g8 * 8:(g8 + 1) * 8] for g8 in range(4)] + [[blocks[32]]]
                                else:
                                    groups = [[blk_] for blk_ in blocks]
                                ng = len(groups)
                                ust = {}

                                def qk(gi):
                                    grp = groups[gi]
                                    pS_ = ps[sctr[0] % 4]; pt_ = ptb[sctr[0] % 4]; tf = tmpf[sctr[0] % 2]
                                    sctr[0] += 1
                                    ust[gi] = (pS_, pt_, tf)
                                    if len(grp) > 1:
                                        for bb, (k0, nk, jb, qs, kind) in enumerate(grp):
                                            P.mm(pS_[0:nk, bb * 16:(bb + 1) * 16], ka[0:RW, k0:k0 + nk], qa[0:RW, q0:q0 + 16])
                                    else:
                                        (k0, nk, jb, qs, kind) = grp[0]
                                        P.mm(pS_[0:nk, qs:nq], ka[0:RW, k0:k0 + nk], qa[0:RW, q0 + qs:q0 + nq])

                                def rest(gi):
                                    grp = groups[gi]
                                    pS_, pt_, tf = ust.pop(gi)
                                    first_, last_ = gi == 0, gi == ng - 1
                                    if len(grp) > 1:
                                        P.act(pt_[0:128, 0:128], pS_[0:128, 0:128], AF.Exp, scale=0.125)
                                        for bb, (k0, nk, jb, qs, kind) in enumerate(grp):
                                            P.mm(pO[0:65, 0:16], vh3[0:nk, jb, :], pt_[0:nk, bb * 16:(bb + 1) * 16], first_ and bb == 0, False)
                                        return
                                    (k0, nk, jb, qs, kind) = grp[0]
                                    if kd == "f":
                                        sc = 0.125
                                    else:
                                        sc = ksc_all[0:nk, slot, jb, hh:hh + 1]
                                    if kind == "diag128":
                                        P.stt(tf[0:128, 0:128], pS_[0:128, qs:qs + 128], sc, (cfb if kd == "f" else cmb)[:, :], ALU.mult, ALU.add)
                                        P.act(pt_[0:128, qs:qs + 128], tf[0:128, 0:128], AF.Exp)
                                        if qs + 128 < nq:
                                            P.act(pt_[0:128, qs + 128:nq], pS_[0:128, qs + 128:nq], AF.Exp, scale=sc)
                                    elif kind == "diag16":
                                        P.stt(tf[0:16, 0:16], pS_[0:16, 0:16], sc, cfs[:, :], ALU.mult, ALU.add)
                                        P.act(pt_[0:16, 0:16], tf[0:16, 0:16], AF.Exp)
                                    else:
                                        P.act(pt_[0:nk, qs:nq], pS_[0:nk, qs:nq], AF.Exp, scale=sc)
                                    P.mm(pO[0:65, qs:nq], vh3[0:nk, jb, :], pt_[0:nk, qs:nq], first_, last_)
                                qk(0)
                                if ng > 1:
                                    qk(1)
                                for gi in range(ng):
                                    if gi + 2 < ng:
                                        qk(gi + 2)
                                    rest(gi)
                                P.copy("vector", os_[:, 0:nq], pO[0:65, 0:nq])
                                P.mm(ps[6][0:64, 0:nq], esel[0:65, :], os_[0:65, 0:nq])
                                P.recip(rr[:, 0:nq], ps[6][0:64, 0:nq])
                                P.tt("vector", ob[:, 0:nq], os_[0:64, 0:nq], rr[:, 0:nq], ALU.mult)
                                row0 = (0 if kd == "f" else 512) + 64 * hh
                                xc0 = xcol0 + q0
                                P.dma("gpsimd", mixin[row0:row0 + 64, xc0:xc0 + nq], ob[:, 0:nq])
                P.emit()

        import os
        stop = os.environ.get("MK_STOP", "")
        ffn_phase(0, 0, True, False)
        m0a_phase()
        if stop == "M0A0":
            return nc
        if stop == "M0A":
            dump_x_phase()
            return nc
        m0b_phase()
        if stop == "M0B":
            dump_x_phase()
            return nc
        mixout_phase(abwo, "0")
        if stop == "L0":
            dump_x_phase()
            return nc
        ffn_phase(0, 1, False, False)
        ffn_phase(1, 0, False, False)
        m1a_phase()
        if stop == "M1A":
            ffn_phase(1, 1, False, True)
            return nc
        m1b_phase()
        mixout_phase(cdwo, "1")
        ffn_phase(1, 1, False, True)
    return nc


def _host_consts():
    c = {}
    c["ident"] = np.eye(128, dtype=np.float32)
    slopes = np.exp2(-8.0 * np.arange(1, 5, dtype=np.float32) / 4).astype(np.float32)
    kl = np.arange(128)[:, None]
    ql = np.arange(128)[None, :]
    cd = np.zeros((128, 4, 128), np.float32)
    for h in range(4):
        cd[:, h, :] = np.where((kl // 64) <= (ql // 64), -2.0 * slopes[h] * np.maximum(kl - ql, 0), NEG)
    c["cdbig"] = cd.reshape(128, 512)
    k16 = np.arange(16)[:, None]
    q16 = np.arange(16)[None, :]
    cs = np.zeros((16, 4, 16), np.float32)
    for h in range(4):
        cs[:, h, :] = -2.0 * slopes[h] * np.maximum(k16 - q16, 0)
    c["cdsm"] = cs.reshape(16, 64)
    c["cfbig"] = np.where(kl <= ql, 0.0, NEG).astype(np.float32)
    c["cfsm"] = np.where(k16 <= q16, 0.0, NEG).astype(np.float32)
    c["cmbig"] = np.where((kl // 64) <= (ql // 64), 0.0, NEG).astype(np.float32)
    pos_p = np.concatenate([16 + np.arange(SEQ), np.arange(NMETA)]).astype(np.float64)
    pos_s = np.arange(TK).astype(np.float64)
    augk = np.zeros((2, 4, 4, TK), np.float32)
    augq = np.zeros((2, 4, 4, TK), np.float32)
    for g, pos in enumerate((pos_p, pos_s)):
        a = np.floor(pos / 64)
        b = pos - 64 * a
        for h in range(4):
            augk[g, h, 0] = 8 * slopes[h] * 64 * a
            augk[g, h, 1] = 8 * slopes[h] * b
            augk[g, h, 2] = 1
            augk[g, h, 3] = 1
            augq[g, h, 0] = 1
            augq[g, h, 1] = 1
            augq[g, h, 2] = -8 * slopes[h] * 64 * a
            augq[g, h, 3] = -8 * slopes[h] * b
    c["augk"] = augk
    c["augq"] = augq
    posx = np.concatenate([16 + np.arange(SEQ), np.arange(NMETA), np.tile(PAST + np.arange(DSEQ), NS)]).astype(np.float32)
    inv = (np.float32(10000.0) ** (-np.arange(16, dtype=np.float32) / np.float32(16))).astype(np.float32)
    ang = (posx[None, :] * inv[:, None]).astype(np.float32)
    cs_, sn_ = np.cos(ang).astype(np.float32), np.sin(ang).astype(np.float32)
    c["cos32"] = np.concatenate([cs_, cs_], 0)
    c["sin32"] = np.concatenate([-sn_, sn_], 0)
    c["cos96"] = np.concatenate([np.ones((64, T), np.float32), cs_, cs_], 0)
    c["sin96"] = np.concatenate([np.zeros((64, T), np.float32), -sn_, sn_], 0)
    return c


def _pack_prm(inp):
    prm = np.zeros((128, NPRM), np.float32)

    def put(name, arr):
        arr = np.asarray(arr, np.float32)
        prm[:arr.shape[0], PRM_OFF[name]:PRM_OFF[name] + arr.shape[1]] = arr
    for l in range(2):
        for i in range(2):
            put("ffn_norm%d%d" % (l, i), _cols(inp["ffn_norm"][l, i], 8))
        put("mix_norm%d" % l, _cols(inp["mix_norm"][l], 8))
    put("a_re", _cols(inp["s5_a_re"][0].reshape(-1), 16))
    put("a_im", _cols(inp["s5_a_im"][0].reshape(-1), 16))
    put("lstep", _cols(np.repeat(inp["s5_log_step"][0], 64), 16))
    put("s5d", _cols(inp["s5_d"][0], 4))
    put("glub", _cols(inp["s5_glu_b"][0], 4))
    put("gq", np.tile(inp["diff_q_norm"][0], 2)[:, None])
    put("gk", np.tile(inp["diff_k_norm"][0], 2)[:, None])
    put("gsub", inp["diff_sub_norm"][0][:, None])
    put("lamv", np.ascontiguousarray(inp["diff_lam"][0].T))
    put("fgq", np.tile(inp["fox_q_norm"][0], 2)[:, None])
    put("fgk", np.tile(inp["fox_k_norm"][0], 2)[:, None])
    put("fbias", inp["fox_f_bias"][0][:, None])
    put("qan", _cols(inp["mla_q_a_norm"][0], 3))
    put("kvan", _cols(inp["mla_kv_a_norm"][0], 2))
    put("mqn", inp["mla_q_norm"][0][:, None])
    put("mkn", inp["mla_k_norm"][0][:, None])
    return prm


def _s5_mats(inp):
    b_re, b_im = np.asarray(inp["s5_b_re"][0]), np.asarray(inp["s5_b_im"][0])
    c_re, c_im = np.asarray(inp["s5_c_re"][0]), np.asarray(inp["s5_c_im"][0])
    out = {k: np.zeros((16, 128, 128), np.float32) for k in ("bre", "bim", "cre", "cim")}
    for j in range(16):
        for gl in range(2):
            g = 2 * j + gl
            f0 = 32 * (j % 4) + 16 * gl
            out["bre"][j, f0:f0 + 16, 64 * gl:64 * gl + 64] = b_re[g].T
            out["bim"][j, f0:f0 + 16, 64 * gl:64 * gl + 64] = b_im[g].T
            out["cre"][j, 64 * gl:64 * gl + 64, f0:f0 + 16] = c_re[g].T
            out["cim"][j, 64 * gl:64 * gl + 64, f0:f0 + 16] = c_im[g].T
    return out


_NC_CACHE = {}


def kernel(**inp):
    inp = {k: np.asarray(v) for k, v in inp.items()}
    if "nc" not in _NC_CACHE:
        _NC_CACHE["nc"] = build_program()
    nc = _NC_CACHE["nc"]
    consts = _host_consts()
    prm = _pack_prm(inp)
    s5m = _s5_mats(inp)
    qb = np.ascontiguousarray(inp["mla_q_b"][0])
    qbr = qb.reshape(384, 8, 96).copy()
    x1 = qbr[:, :, 64:80].copy()
    x2 = qbr[:, :, 80:96].copy()
    qbr[:, :, 64:80] = x2
    qbr[:, :, 80:96] = x1
    qbr = np.ascontiguousarray(qbr.reshape(384, 768))
    kvb = inp["mla_kv_b"][0].reshape(256, 8, 128)
    kvb = np.ascontiguousarray(np.concatenate([kvb[:, :, :64].reshape(256, 512), kvb[:, :, 64:].reshape(256, 512)], 1))
    shared = dict(
        meta=inp["meta_tokens"], wi=inp["ffn_w_in"], wo=inp["ffn_w_out"],
        abwi=inp["ab_w_in"][0], abwo=inp["ab_w_out"][0], cdwi=inp["cd_w_in"][0], cdwo=inp["cd_w_out"][0],
        gluw=inp["s5_glu_w"][0], qb=qb, qbr=qbr, kvb=kvb, prm=prm, **s5m, **consts)
    shared = {k: np.ascontiguousarray(v, dtype=np.float32) for k, v in shared.items()}
    in_maps = []
    for c in range(NCORES):
        sl = slice(NS * c, NS * (c + 1))
        st = np.stack([inp["state_s5_re"][0, sl], inp["state_s5_im"][0, sl]], 1)
        st = st.reshape(NS, 2, 16, 128).transpose(3, 0, 1, 2).reshape(128, NS * 2 * 16)
        m = dict(shared)
        m.update(
            xp=inp["x_prompt"][c], xs=inp["x_sample"][sl].reshape(NS * DSEQ, D), st0=np.ascontiguousarray(st),
            cdk=inp["cache_diff_k"][0, sl].reshape(NS, PAST, 512), cdv=inp["cache_diff_v"][0, sl].reshape(NS, PAST, 512),
            cfk=inp["cache_fox_k"][0, sl].reshape(NS, PAST, 512), cfv=inp["cache_fox_v"][0, sl].reshape(NS, PAST, 512),
            clf=inp["cache_fox_logf"][0, sl], cckv=inp["cache_mla_ckv"][0, sl], ckpe=inp["cache_mla_kpe"][0, sl])
        in_maps.append({k: np.ascontiguousarray(v, dtype=np.float32) for k, v in m.items()})
    res = run_bass_kernel_spmd(nc, in_maps, core_ids=list(range(NCORES)))
    rs = res.results

    def cat(name, shape):
        return np.stack([np.asarray(r[name]) for r in rs]).reshape(shape).astype(np.float32)

    def st_out(a):
        return np.ascontiguousarray(np.swapaxes(a, -1, -2)).reshape(a.shape[:-2] + (32, 64))
    y_prompt = cat("yp", (8, SEQ, D))
    y_sample = cat("ys", (32, DSEQ, D))
    s5pp = cat("s5p", (8, 2, 128, 16))
    s5ss = cat("s5s", (32, 2, 128, 16))
    outs = [y_prompt, y_sample,
            st_out(s5pp[:, 0])[None], st_out(s5pp[:, 1])[None],
            cat("dkp_o", (1, 8, LP, 4, 128)), cat("dvp_o", (1, 8, LP, 4, 128)),
            cat("fkp_o", (1, 8, LP, 8, 64)), cat("fvp_o", (1, 8, LP, 8, 64)),
            cat("lfp_o", (1, 8, LP, 8)), cat("ckvp_o", (1, 8, LP, 256)), cat("kpep_o", (1, 8, LP, 32)),
            st_out(s5ss[:, 0])[None], st_out(s5ss[:, 1])[None],
            cat("dks_o", (1, 32, DSEQ, 4, 128)), cat("dvs_o", (1, 32, DSEQ, 4, 128)),
            cat("fks_o", (1, 32, DSEQ, 8, 64)), cat("fvs_o", (1, 32, DSEQ, 8, 64)),
            cat("lfs_o", (1, 32, DSEQ, 8)), cat("ckvs_o", (1, 32, DSEQ, 256)), cat("kpes_o", (1, 32, DSEQ, 32))]
    return tuple(outs)
```

```python
import math
from contextlib import ExitStack
import numpy as np
import concourse.bass as bass
import concourse.mybir as mybir
from concourse.bass_utils import run_bass_kernel_spmd

F32 = mybir.dt.float32
BF16 = mybir.dt.bfloat16
AF = mybir.ActivationFunctionType
ALU = mybir.AluOpType

NCORES = 8
D = 1024
SEQ = 4096
NMETA = 16
LP = SEQ + NMETA
NS = 4
DSEQ = 16
PAST = 4096
T = LP + NS * DSEQ
TK = PAST + DSEQ
NT = 256
DFF = 2816
EPS = 1e-6
NEG = -30000.0
CDW = 2216
LAM_INIT0 = 0.8 - 0.6 * math.exp(-0.3 * 0)


class Buf:
    __slots__ = ("w", "r")

    def __init__(self):
        self.w = None
        self.r = []


class R:
    __slots__ = ("ap", "b")

    def __init__(self, ap, b):
        self.ap = ap
        self.b = b


class TT:
    def __init__(self, t, b=None):
        self.t = t
        self.b = b if b is not None else Buf()

    def __getitem__(self, idx):
        return R(self.t[idx], self.b)


class Op:
    __slots__ = ("eng", "fn", "deps", "flagged", "count", "is_dma", "sem", "semval")


ENGS = ("tensor", "vector", "scalar", "gpsimd", "sync")


class Prog:
    def __init__(self, nc, st, n_dma_sems=10):
        self.nc = nc
        self.n_dma_sems = n_dma_sems
        self.esem = {e: st.enter_context(nc.semaphore("s_" + e)) for e in ENGS}
        self.dsem = {}
        for q in ("sync", "gpsimd"):
            for k in range(n_dma_sems):
                self.dsem[(q, k)] = st.enter_context(nc.semaphore("d_%s_%d" % (q, k)))
        self.ecount = {e: 0 for e in ENGS}
        self.dma_tot = {k: 0 for k in self.dsem}
        self.dma_last = {}
        self.dma_i = {"sync": 0, "gpsimd": 0}
        self.waited = {e: {} for e in ENGS}
        self.ops = {e: [] for e in ENGS}

    def op(self, eng, fn, reads=(), writes=()):
        o = Op()
        o.eng = eng
        o.fn = fn
        o.flagged = False
        o.count = 0
        o.is_dma = False
        o.sem = None
        o.semval = 0
        deps = {}
        for b in reads:
            if b.w is not None:
                deps[id(b.w)] = b.w
        for b in writes:
            if b.w is not None:
                deps[id(b.w)] = b.w
            for r in b.r:
                deps[id(r)] = r
        dl = []
        for d in deps.values():
            if (not d.is_dma) and d.eng == "tensor" and eng == "tensor":
                continue
            dl.append(d)
        o.deps = dl
        for b in reads:
            b.r.append(o)
        for b in writes:
            b.w = o
            b.r = []
        self.ops[eng].append(o)
        return o

    def dma(self, queue, out, in_, **kw):
        k = self.dma_i[queue] % self.n_dma_sems
        self.dma_i[queue] += 1
        key = (queue, k)
        prev = self.dma_last.get(key)
        oap, iap = out.ap, in_.ap

        def fn(e):
            return e.dma_start(out=oap, in_=iap, **kw)

        o = self.op(queue, fn, [in_.b], [out.b])
        o.is_dma = True
        if prev is not None:
            o.deps.append(prev)
        self.dma_tot[key] += 16
        o.sem = key
        o.semval = self.dma_tot[key]
        o.flagged = True
        self.dma_last[key] = o
        return o

    def mm(self, out, lhsT, rhs, start=True, stop=True):
        o_, l_, r_ = out.ap, lhsT.ap, rhs.ap
        self.op("tensor", lambda e: e.matmul(out=o_, lhsT=l_, rhs=r_, start=start, stop=stop),
                [lhsT.b, rhs.b], [out.b])

    def tr(self, out, in_, ident):
        o_, i_, d_ = out.ap, in_.ap, ident.ap
        self.op("tensor", lambda e: e.transpose(out=o_, in_=i_, identity=d_), [in_.b, ident.b], [out.b])

    def act(self, out, in_, func, bias=None, scale=None, accum=None):
        o_, i_ = out.ap, in_.ap
        rd = [in_.b]
        wr = [out.b]
        kw = {}
        if bias is not None:
            if isinstance(bias, R):
                kw["bias"] = bias.ap
                rd.append(bias.b)
            else:
                kw["bias"] = bias
        if scale is not None:
            if isinstance(scale, R):
                kw["scale"] = scale.ap
                rd.append(scale.b)
            else:
                kw["scale"] = scale
        if accum is not None:
            kw["accum_out"] = accum.ap
            wr.append(accum.b)
        self.op("scalar", lambda e: e.activation(out=o_, in_=i_, func=func, **kw), rd, wr)

    def tt(self, eng, out, in0, in1, op):
        o_, a_, b_ = out.ap, in0.ap, in1.ap
        self.op(eng, lambda e: e.tensor_tensor(out=o_, in0=a_, in1=b_, op=op), [in0.b, in1.b], [out.b])

    def stt(self, out, in0, scalar, in1, op0, op1):
        o_, a_, b_ = out.ap, in0.ap, in1.ap
        rd = [in0.b, in1.b]
        if isinstance(scalar, R):
            s_ = scalar.ap
            rd.append(scalar.b)
        else:
            s_ = scalar
        self.op("vector", lambda e: e.scalar_tensor_tensor(out=o_, in0=a_, scalar=s_, in1=b_, op0=op0, op1=op1),
                rd, [out.b])

    def ts(self, eng, out, in0, s1, s2, op0, op1=None):
        o_, a_ = out.ap, in0.ap
        rd = [in0.b]

        def cv(s):
            if isinstance(s, R):
                rd.append(s.b)
                return s.ap
            return s
        s1_ = cv(s1)
        s2_ = cv(s2)
        if op1 is None:
            self.op(eng, lambda e: e.tensor_scalar(out=o_, in0=a_, scalar1=s1_, scalar2=None, op0=op0), rd, [out.b])
        else:
            self.op(eng, lambda e: e.tensor_scalar(out=o_, in0=a_, scalar1=s1_, scalar2=s2_, op0=op0, op1=op1),
                    rd, [out.b])

    def copy(self, eng, out, in_):
        o_, i_ = out.ap, in_.ap
        if eng == "scalar":
            self.op(eng, lambda e: e.copy(out=o_, in_=i_), [in_.b], [out.b])
        else:
            self.op(eng, lambda e: e.tensor_copy(out=o_, in_=i_), [in_.b], [out.b])

    def recip(self, out, in_):
        o_, i_ = out.ap, in_.ap
        self.op("vector", lambda e: e.reciprocal(out=o_, in_=i_), [in_.b], [out.b])

    def memset(self, eng, out, val):
        o_ = out.ap
        self.op(eng, lambda e: e.memset(o_, val), [], [out.b])

    def scan(self, out, d0, d1, init, op0=ALU.mult, op1=ALU.add):
        o_, a_, b_ = out.ap, d0.ap, d1.ap
        rd = [d0.b, d1.b]
        if isinstance(init, R):
            i_ = init.ap
            rd.append(init.b)
        else:
            i_ = init
        self.op("vector", lambda e: e.tensor_tensor_scan(out=o_, data0=a_, data1=b_, initial=i_, op0=op0, op1=op1),
                rd, [out.b])

    def reduce(self, out, in_, op=ALU.add):
        o_, i_ = out.ap, in_.ap
        self.op("vector", lambda e: e.tensor_reduce(out=o_, in_=i_, axis=mybir.AxisListType.X, op=op), [in_.b], [out.b])

    def emit(self):
        nc = self.nc
        for e in ENGS:
            for o in self.ops[e]:
                for d in o.deps:
                    d.flagged = True
            if self.ops[e]:
                self.ops[e][-1].flagged = True
        for e in ENGS:
            c = self.ecount[e]
            for o in self.ops[e]:
                if o.is_dma:
                    continue
                if o.flagged:
                    c += 1
                    o.count = c
            self.ecount[e] = c
        prog = self
        with nc.Block() as block:
            def make(ename):
                def body(e):
                    waited = prog.waited[ename]
                    for o in prog.ops[ename]:
                        need = {}
                        for d in o.deps:
                            if d.is_dma:
                                s = ("d",) + d.sem
                                v = d.semval
                            else:
                                s = ("e", d.eng)
                                v = d.count
                            if need.get(s, 0) < v:
                                need[s] = v
                        for s, v in need.items():
                            if waited.get(s, 0) < v:
                                waited[s] = v
                                sem = prog.dsem[s[1:]] if s[0] == "d" else prog.esem[s[1]]
                                e.wait_ge(sem, v)
                        ins = o.fn(e)
                        if o.is_dma:
                            ins.then_inc(prog.dsem[o.sem], 16)
                        elif o.flagged:
                            ins.then_inc(prog.esem[ename], 1)
                    for key, tot in prog.dma_tot.items():
                        s = ("d",) + key
                        if tot > 0 and waited.get(s, 0) < tot:
                            waited[s] = tot
                            e.wait_ge(prog.dsem[key], tot)
                    for e2 in ENGS:
                        c2 = prog.ecount[e2]
                        s = ("e", e2)
                        if e2 != ename and c2 > 0 and waited.get(s, 0) < c2:
                            waited[s] = c2
                            e.wait_ge(prog.esem[e2], c2)
                return body
            for ename in ENGS:
                getattr(block, ename)(make(ename))
        self.ops = {e: [] for e in ENGS}
        self.dma_last = {}


def _prm_layout():
    off = {}
    n = 0

    def add(name, w):
        nonlocal n
        off[name] = n
        n += w
    for l in range(2):
        for i in range(2):
            add("ffn_norm%d%d" % (l, i), 8)
        add("mix_norm%d" % l, 8)
    add("a_re", 16); add("a_im", 16); add("lstep", 16)
    add("s5d", 4); add("glub", 4)
    add("gq", 1); add("gk", 1); add("gsub", 1); add("lamv", 4)
    add("fgq", 1); add("fgk", 1); add("fbias", 1)
    add("qan", 3); add("kvan", 2); add("mqn", 1); add("mkn", 1)
    return off, n


PRM_OFF, NPRM = _prm_layout()


def _cols(v, nch):
    return np.ascontiguousarray(np.asarray(v, np.float32).reshape(nch, 128).T)


TILES_F = [(4096, 80)] + [(256 * i, 256) for i in range(16)]
TILES_A = [(4096, 80)] + [(128 * i, 128) for i in range(32)]


def out_rows(c0, n):
    if c0 < 4096:
        return [("p", 16 + c0, 0, n)]
    return [("p", 0, 0, 16), ("s", 0, 16, 64)]


def key_dst(c0, n):
    if c0 < 4096:
        return [(0, c0, 0, n)]
    return [(0, 4096, 0, 16)] + [(1 + s, 4096, 16 + 16 * s, 16) for s in range(NS)]


def build_program():
    nc = bass.Bass("TRN2", target_bir_lowering=False)

    def din(name, shape, dt=F32):
        return TT(nc.dram_tensor(name, list(shape), dt, kind="ExternalInput").ap())

    def dout(name, shape):
        return TT(nc.dram_tensor(name, list(shape), F32, kind="ExternalOutput").ap())

    def dscr(name, shape, dt=BF16):
        return TT(nc.dram_tensor(name, list(shape), dt, kind="Internal").ap())

    xp = din("xp", [SEQ, D]); xs = din("xs", [NS * DSEQ, D]); meta = din("meta", [NMETA, D])
    wi = din("wi", [2, 2, D, 2 * DFF]); wo = din("wo", [2, 2, DFF, D])
    abwi = din("abwi", [D, 2048]); abwo = din("abwo", [D, D])
    cdwi = din("cdwi", [D, CDW]); cdwo = din("cdwo", [D, D])
    gluw = din("gluw", [512, 512]); qb = din("qb", [384, 768]); qbr = din("qbr", [384, 768])
    kvb = din("kvb", [256, 1024])
    prm = din("prm", [128, NPRM]); st0 = din("st0", [128, NS * 2 * 16])
    bre = din("bre", [16, 128, 128]); bim = din("bim", [16, 128, 128])
    cre = din("cre", [16, 128, 128]); cim = din("cim", [16, 128, 128])
    identd = din("ident", [128, 128])
    cdbig = din("cdbig", [128, 4 * 128]); cdsm = din("cdsm", [16, 4 * 16])
    cfbig = din("cfbig", [128, 128]); cfsm = din("cfsm", [16, 16]); cmbig = din("cmbig", [128, 128])
    augk = din("augk", [2, 4, 4, TK]); augq = din("augq", [2, 4, 4, TK])
    cos96 = din("cos96", [96, T]); sin96 = din("sin96", [96, T])
    cos32 = din("cos32", [32, T]); sin32 = din("sin32", [32, T])
    cdk = din("cdk", [NS, PAST, 512]); cdv = din("cdv", [NS, PAST, 512])
    cfk = din("cfk", [NS, PAST, 512]); cfv = din("cfv", [NS, PAST, 512])
    clf = din("clf", [NS, PAST, 8]); cckv = din("cckv", [NS, PAST, 256]); ckpe = din("ckpe", [NS, PAST, 32])

    yp = dout("yp", [SEQ, D]); ys = dout("ys", [NS * DSEQ, D])
    s5p = dout("s5p", [2, 128, 16]); s5s = dout("s5s", [NS, 2, 128, 16])
    O = {}
    for nm, w in (("dk", 512), ("dv", 512), ("fk", 512), ("fv", 512), ("lf", 8), ("ckv", 256), ("kpe", 32)):
        O[nm + "p"] = dout(nm + "p_o", [LP, w])
        O[nm + "s"] = dout(nm + "s_o", [NS * DSEQ, w])

    xres = dscr("xres", [128, 8, T], F32)
    mixin = dscr("mixin", [D, T])
    QD = dscr("QD", [4, 128, T]); KD = dscr("KD", [1 + NS, 4, 128, TK]); VD = dscr("VD", [1 + NS, TK, 512])
    AUGK = dscr("AUGK", [2, 4, 4, TK]); AUGQ = dscr("AUGQ", [2, 4, 4, TK])
    QF = dscr("QF", [4, 128, T]); KF = dscr("KF", [1 + NS, 4, 128, TK]); VF = dscr("VF", [1 + NS, TK, 512])
    FK = dscr("FK", [1 + NS, 8, 3, TK]); FQ = dscr("FQ", [8, 3, T])
    QM = dscr("QM", [8, 96, T]); KM = dscr("KM", [1 + NS, 4, 128, TK]); KP = dscr("KP", [1 + NS, 32, TK])
    VM = dscr("VM", [1 + NS, TK, 512])

    def pcol(name, c=0, n=1, rows=128):
        o = PRM_OFF[name] + c
        return prmt[0:rows, o:o + n]

    with ExitStack() as gst:
        P = Prog(nc, gst)
        ps = [TT(gst.enter_context(nc.psum_tensor("ps%d" % i, [128, 512], F32))) for i in range(8)]
        prmt = TT(gst.enter_context(nc.sbuf_tensor("prmt", [128, NPRM], F32)))
        ident = TT(gst.enter_context(nc.sbuf_tensor("identt", [128, 128], F32)))
        ones = TT(gst.enter_context(nc.sbuf_tensor("ones", [128, 128], F32)))
        blk64 = TT(gst.enter_context(nc.sbuf_tensor("blk64", [128, 128], F32)))
        onesb = TT(gst.enter_context(nc.sbuf_tensor("onesb", [128, 128], BF16)))
        cst = TT(gst.enter_context(nc.sbuf_tensor("cst", [128, 4], F32)))
        ksc_all = TT(gst.enter_context(nc.sbuf_tensor("ksc_all", [128, 1 + NS, 33, 8], F32)))
        esel = TT(gst.enter_context(nc.sbuf_tensor("esel", [128, 64], F32)))

        P.dma("sync", prmt[:, :], prm[:, :])
        P.dma("sync", ident[:, :], identd[:, :])
        P.memset("vector", ones[:, :], 1.0)
        P.memset("vector", onesb[:, :], 1.0)
        P.memset("vector", blk64[:, :], 0.0)
        P.memset("vector", blk64[0:64, 0:64], 1.0)
        P.memset("vector", blk64[64:128, 64:128], 1.0)
        P.memset("vector", cst[:, 0:1], EPS)
        P.memset("vector", cst[:, 1:2], math.pi / 2)
        P.memset("vector", cst[:, 2:3], 0.0)
        P.memset("vector", cst[:, 3:4], 1.0)
        P.memset("vector", esel[:, :], 0.0)
        P.memset("vector", esel[64:65, :], 1.0)
        with ExitStack() as st:
            a32 = TT(st.enter_context(nc.sbuf_tensor("a32", [32, TK], F32)))
            a16 = TT(st.enter_context(nc.sbuf_tensor("a16", [32, TK], BF16)))
            for src, dst in ((augk, AUGK), (augq, AUGQ)):
                P.dma("sync", a32[:, :], R(src.t.rearrange("a h r t -> (a h r) t"), src.b))
                P.copy("vector", a16[:, :], a32[:, :])
                P.dma("sync", R(dst.t.rearrange("a h r t -> (a h r) t"), dst.b), a16[:, :])
            P.emit()
        eps_c = cst[:, 0:1]

        def load_x_first(st_tiles, xT, c0, n):
            xtok = st_tiles["xtok"]
            if c0 < 4096:
                for blk in range(n // 128):
                    P.dma("sync", xtok[:, :], xp[c0 + blk * 128:c0 + (blk + 1) * 128, :])
                    for c in range(8):
                        pt_ = ps[6 + c % 2]
                        P.tr(pt_[:, 0:128], xtok[:, c * 128:(c + 1) * 128], ident[:, :])
                        P.copy("vector" if c % 2 else "scalar", xT[:, c, blk * 128:(blk + 1) * 128], pt_[:, 0:128])
            else:
                P.dma("sync", xtok[0:16, :], meta[:, :])
                P.dma("sync", xtok[16:80, :], xs[:, :])
                for c in range(8):
                    pt_ = ps[6 + c % 2]
                    P.tr(pt_[:, 0:80], xtok[0:80, c * 128:(c + 1) * 128], ident[0:80, 0:80])
                    P.copy("vector" if c % 2 else "scalar", xT[:, c, 0:80], pt_[:, 0:80])

        def rms_h(xT, h, sqc, rstd, gname, n):
            for c in range(8):
                P.act(sqc[c % 2][:, 0:n], xT[:, c, 0:n], AF.Square)
                P.mm(ps[7][:, 0:n], ones[:, :], sqc[c % 2][:, 0:n], c == 0, c == 7)
            P.act(rstd[:, 0:n], ps[7][:, 0:n], AF.Sqrt, bias=eps_c, scale=1.0 / D)
            P.recip(rstd[:, 0:n], rstd[:, 0:n])
            for c in range(8):
                P.stt(h[:, c, 0:n], xT[:, c, 0:n], pcol(gname, c), rstd[:, 0:n], ALU.mult, ALU.mult)

        def store_tok(src, nchunks, c0, n, outs, tokbuf, width, is_y=False):
            nb = (n + 127) // 128
            for blk in range(nb):
                bn = min(128, n - blk * 128)
                for c in range(nchunks):
                    P.tr(ps[6][0:bn, c * 128:(c + 1) * 128] if nchunks <= 4 else ps[6 + c // 4][0:bn, (c % 4) * 128:(c % 4 + 1) * 128],
                         src[:, c, blk * 128:blk * 128 + bn], ident[:, :])
                for q in range((nchunks + 3) // 4):
                    wq = min(4, nchunks - 4 * q) * 128
                    P.copy("vector" if q else "scalar", tokbuf[0:bn, q * 512:q * 512 + wq], ps[6 + q][0:bn, 0:wq])
                for grp, r0, lo, cnt in out_rows(c0 + blk * 128, bn):
                    if is_y and grp == "p":
                        if c0 >= 4096:
                            continue
                        r0 -= 16
                    P.dma("gpsimd", outs[grp][r0:r0 + cnt, 0:width], tokbuf[lo:lo + cnt, 0:width])

        def ffn_phase(l, i, first, last):
            with ExitStack() as st:
                def sb(name, shape, dt=F32):
                    return TT(st.enter_context(nc.sbuf_tensor("%s_f%d%d" % (name, l, i), shape, dt)))
                win = sb("win", [128, 8, 2 * DFF], BF16)
                wout = sb("wout", [128, 22, D], BF16)
                stg = [sb("stg%d" % k, [128, 1024]) for k in range(2)]
                xTs = [sb("xT%d" % k, [128, 8, NT]) for k in range(2)]
                sq8 = sb("sq8", [128, 8, NT])
                rstds = [sb("rstd%d" % k, [128, NT]) for k in range(2)]
                hs = [sb("h%d" % k, [128, 8, NT], BF16) for k in range(2)]
                actb = sb("actb", [128, 22, NT], BF16)
                sg = [sb("sg%d" % k, [128, NT]) for k in range(2)]
                tiles = {}
                if first:
                    tiles["xtok"] = sb("xtok", [128, D])
                if last:
                    ytok = sb("ytok", [128, D])
                k = 0
                engs = ("vector", "gpsimd", "scalar")
                for c in range(8):
                    for n0 in range(0, 2 * DFF, 1024):
                        w = min(1024, 2 * DFF - n0)
                        P.dma("sync", stg[k % 2][:, 0:w], wi[l, i, c * 128:(c + 1) * 128, n0:n0 + w])
                        P.copy(engs[k % 3], win[:, c, n0:n0 + w], stg[k % 2][:, 0:w])
                        k += 1
                for j in range(22):
                    P.dma("sync", stg[k % 2][:, :], wo[l, i, j * 128:(j + 1) * 128, :])
                    P.copy(engs[k % 3], wout[:, j, :], stg[k % 2][:, :])
                    k += 1
                gname = "ffn_norm%d%d" % (l, i)

                def load_x(t):
                    c0, n = TILES_F[t]
                    if first:
                        load_x_first(tiles, xTs[t % 2], c0, n)
                    else:
                        P.dma("sync", xTs[t % 2][:, :, 0:n], xres[:, :, c0:c0 + n])

                def squares(t):
                    c0, n = TILES_F[t]
                    for c in range(8):
                        P.act(sq8[:, c, 0:n], xTs[t % 2][:, c, 0:n], AF.Square)

                def norm_h(t):
                    c0, n = TILES_F[t]
                    xT, h, rstd = xTs[t % 2], hs[t % 2], rstds[t % 2]
                    for c in range(8):
                        P.mm(ps[7][:, 0:n], ones[:, :], sq8[:, c, 0:n], c == 0, c == 7)
                    P.act(rstd[:, 0:n], ps[7][:, 0:n], AF.Sqrt, bias=eps_c, scale=1.0 / D)
                    P.recip(rstd[:, 0:n], rstd[:, 0:n])
                    for c in range(8):
                        P.stt(h[:, c, 0:n], xT[:, c, 0:n], pcol(gname, c), rstd[:, 0:n], ALU.mult, ALU.mult)

                ntile = len(TILES_F)
                pipe = not first
                if pipe:
                    load_x(0)
                    squares(0)
                    norm_h(0)
                for t in range(ntile):
                    c0, n = TILES_F[t]
                    xT, h = xTs[t % 2], hs[t % 2]
                    if pipe:
                        if t + 1 < ntile:
                            load_x(t + 1)
                    else:
                        load_x(t)
                        squares(t)
                        norm_h(t)
                    for j in range(22):
                        pg = ps[(j % 2) * 2]
                        pu = ps[(j % 2) * 2 + 1]
                        for c in range(8):
                            P.mm(pg[:, 0:n], win[:, c, j * 128:(j + 1) * 128], h[:, c, 0:n], c == 0, c == 7)
                        for c in range(8):
                            P.mm(pu[:, 0:n], win[:, c, DFF + j * 128:DFF + (j + 1) * 128], h[:, c, 0:n], c == 0, c == 7)
                        P.act(sg[j % 2][:, 0:n], pg[:, 0:n], AF.Silu)
                        P.tt("vector", actb[:, j, 0:n], sg[j % 2][:, 0:n], pu[:, 0:n], ALU.mult)
                        if pipe and j == 10 and t + 1 < ntile:
                            squares(t + 1)
                    if pipe and t + 1 < ntile:
                        norm_h(t + 1)
                    for m in range(8):
                        po = ps[4 + m % 2]
                        for j in range(22):
                            P.mm(po[:, 0:n], wout[:, j, m * 128:(m + 1) * 128], actb[:, j, 0:n], j == 0, j == 21)
                        P.stt(xT[:, m, 0:n], po[:, 0:n], 0.5, xT[:, m, 0:n], ALU.mult, ALU.add)
                    if last:
                        store_tok(xT, 8, c0, n, {"p": yp, "s": ys}, ytok, D, is_y=True)
                    else:
                        P.dma("gpsimd", xres[:, :, c0:c0 + n], xT[:, :, 0:n])
                P.emit()


        def m0a_phase():
            NA = 128
            import os
            SKIP = os.environ.get("MK_SKIP", "").split(",")
            peng = "vector" if "pool" in SKIP else "gpsimd"
            with ExitStack() as st:
                def sb(name, shape, dt=F32):
                    return TT(st.enter_context(nc.sbuf_tensor(name + "_m0a", shape, dt)))
                wab = sb("wab", [128, 8, 2048], BF16)
                stg = [sb("stg%d" % k, [128, 1024]) for k in range(2)]
                Tor = sb("Tor", [128, 16, NA]); Toi = sb("Toi", [128, 16, NA])
                Tir = sb("Tir", [128, 16, NA]); Tii = sb("Tii", [128, 16, NA])
                rho_t = sb("rho_t", [128, 16, NA])
                Bre = sb("Bre", [128, 16, 128], BF16); Bim = sb("Bim", [128, 16, 128], BF16)
                sp = sb("sp", [128, 16, 16])
                xT = sb("xT", [128, 8, NA]); sqc = [sb("sqc%d" % k, [128, NA]) for k in range(2)]
                rstd = sb("rstd", [128, NA]); h = sb("h", [128, 8, NA], BF16)
                ub = sb("ub", [128, 4, NA], BF16)
                tmp = [sb("tmp%d" % k, [128, NA]) for k in range(8)]
                hst = sb("hst", [128, 2, 16])
                sst = sb("sst", [128, NS * 2 * 16])
                st0t = sb("st0t", [128, NS * 2 * 16])
                kn32 = sb("kn32", [128, 4, NA]); ktok = sb("ktok", [128, 512]); vtok = sb("vtok", [128, 512])
                sqt = sb("sqt", [128, NA]); rs = sb("rs", [128, NA])
                Cre = sb("Cre", [128, 16, 128], BF16); Cimn = sb("Cimn", [128, 16, 128], BF16)
                gluwb = sb("gluwb", [128, 4, 512], BF16)
                u32 = sb("u32", [128, 4, NA]); hrb = sb("hrb", [128, 16, NA], BF16); hib = sb("hib", [128, 16, NA], BF16)
                gsc = [[sb("gsc%d%d" % (a, b), [128, NA]) for b in range(2)] for a in range(2)]
                ptmp = [sb("ptmp%d" % q, [128, NA]) for q in range(4)]
                yv = sb("yv", [128, 4, NA]); g32 = sb("g32", [128, 4, NA]); gb = sb("gb", [128, 4, NA], BF16)
                so = sb("so", [128, 4, NA], BF16); qn = sb("qn", [128, 4, NA], BF16); knb = sb("knb", [128, 4, NA], BF16)
                vtb = sb("vtb", [128, 512], BF16); gt1 = sb("gt1", [128, NA]); gt2 = sb("gt2", [128, NA])
                segctr = [0]
                sqtr = [sb("sqtr%d" % q, [128, NA]) for q in range(2)]; rsr = [sb("rsr%d" % q, [128, NA]) for q in range(2)]
                big = [sb("big%d" % q, [128, 512]) for q in range(4)]
                pbig = [sb("pbig%d" % q, [128, 512]) for q in range(4)]
                gbig = [[sb("gbig%d%d" % (a, b), [128, 512]) for b in range(2)] for a in range(2)]
                sm4 = sb("sm4", [128, 16])
                engs = ("vector", "gpsimd", "scalar")
                k = 0
                for j in range(16 if "cload" not in SKIP else 0):
                    P.dma("sync", stg[k % 2][:, 0:128], cre[j, :, :])
                    P.copy(engs[k % 3], Cre[:, j, :], stg[k % 2][:, 0:128])
                    k += 1
                    P.dma("sync", stg[k % 2][:, 0:128], cim[j, :, :])
                    P.ts("vector", Cimn[:, j, :], stg[k % 2][:, 0:128], -1.0, None, ALU.mult)
                    k += 1
                for c in range(4 if "cload" not in SKIP else 0):
                    P.dma("sync", stg[k % 2][:, 0:512], gluw[c * 128:(c + 1) * 128, :])
                    P.copy(engs[k % 3], gluwb[:, c, :], stg[k % 2][:, 0:512])
                    k += 1
                for c in range(8):
                    for n0 in range(0, 2048, 1024):
                        P.dma("sync", stg[k % 2][:, :], abwi[c * 128:(c + 1) * 128, n0:n0 + 1024])
                        P.copy(engs[k % 3], wab[:, c, n0:n0 + 1024], stg[k % 2][:, :])
                        k += 1
                for j in range(16):
                    for src, dst in ((bre, Bre), (bim, Bim)):
                        P.dma("sync", stg[k % 2][:, 0:128], src[j, :, :])
                        P.copy(engs[k % 3], dst[:, j, :], stg[k % 2][:, 0:128])
                        k += 1
                P.dma("sync", st0t[:, :], st0[:, :])
                P.memset("vector", hst[:, :, :], 0.0)
                DL, ARD, TH, RHO, C_, S_, T1, T2, LBR, LBI, DEN, KR, KI, T3 = range(14)
                a_re = pcol("a_re", 0, 16); a_im = pcol("a_im", 0, 16)
                def S(i):
                    return sp[:, i, :]
                P.act(S(DL), pcol("lstep", 0, 16), AF.Exp)
                P.tt("vector", S(ARD), a_re, S(DL), ALU.mult)
                P.tt("vector", S(TH), a_im, S(DL), ALU.mult)
                P.act(S(RHO), S(ARD), AF.Exp)
                P.act(S(S_), S(TH), AF.Sin, scale=1.0 / 32)
                P.act(S(C_), S(TH), AF.Sin, scale=1.0 / 32, bias=cst[:, 1:2])
                for _ in range(5):
                    P.tt("vector", S(T1), S(C_), S(C_), ALU.mult)
                    P.tt("vector", S(T2), S(S_), S(S_), ALU.mult)
                    P.stt(S(S_), S(C_), 2.0, S(S_), ALU.mult, ALU.mult)
                    P.tt("vector", S(C_), S(T1), S(T2), ALU.subtract)
                P.tt("vector", S(LBR), S(RHO), S(C_), ALU.mult)
                P.tt("vector", S(LBI), S(RHO), S(S_), ALU.mult)
                P.ts("vector", S(T1), S(LBR), -1.0, None, ALU.add)
                P.tt("vector", S(DEN), a_re, a_re, ALU.mult)
                P.tt("vector", S(T2), a_im, a_im, ALU.mult)
                P.tt("vector", S(DEN), S(DEN), S(T2), ALU.add)
                P.recip(S(DEN), S(DEN))
                P.tt("vector", S(KR), S(T1), a_re, ALU.mult)
                P.tt("vector", S(T2), S(LBI), a_im, ALU.mult)
                P.tt("vector", S(KR), S(KR), S(T2), ALU.add)
                P.tt("vector", S(KR), S(KR), S(DEN), ALU.mult)
                P.tt("vector", S(KI), S(LBI), a_re, ALU.mult)
                P.tt("vector", S(T2), S(T1), a_im, ALU.mult)
                P.tt("vector", S(KI), S(KI), S(T2), ALU.subtract)
                P.tt("vector", S(KI), S(KI), S(DEN), ALU.mult)
                for j in range(16):
                    P.copy("vector", Tor[:, j, 0:1], sp[:, C_, j:j + 1])
                    P.copy("vector", Toi[:, j, 0:1], sp[:, S_, j:j + 1])
                n = 1
                while n < NA:
                    for j in range(16):
                        cr = Tor[:, j, n - 1:n]; ci = Toi[:, j, n - 1:n]
                        a = Tor[:, j, 0:n]; b = Toi[:, j, 0:n]
                        t1 = tmp[(2 * j) % 8][:, 0:n]; t2 = tmp[(2 * j + 1) % 8][:, 0:n]
                        P.ts("vector", t1, b, ci, None, ALU.mult)
                        P.ts("vector", t2, b, cr, None, ALU.mult)
                        P.stt(Tor[:, j, n:2 * n], a, cr, t1, ALU.mult, ALU.subtract)
                        P.stt(Toi[:, j, n:2 * n], a, ci, t2, ALU.mult, ALU.add)
                    n *= 2
                for j in range(16):
                    kr = sp[:, KR, j:j + 1]; ki = sp[:, KI, j:j + 1]
                    t1 = tmp[(2 * j) % 8][:, :]; t2 = tmp[(2 * j + 1) % 8][:, :]
                    P.ts("vector", t1, Toi[:, j, :], ki, None, ALU.mult)
                    P.stt(Tir[:, j, :], Tor[:, j, :], kr, t1, ALU.mult, ALU.add)
                    P.ts("vector", t2, Toi[:, j, :], kr, None, ALU.mult)
                    P.stt(Tii[:, j, :], Tor[:, j, :], ki, t2, ALU.mult, ALU.subtract)
                    P.act(rho_t[:, j, :], Tor[:, j, :], AF.Identity, bias=sp[:, RHO, j:j + 1], scale=0.0)

                def s5_segment(j, pbr, pbi, off, ln, init_re, init_im, fin_re, fin_im):
                    cs = slice(off, off + ln)
                    t = [tmp[q][:, 0:ln] for q in range(8)]
                    P.tt("vector", t[0], Tir[:, j, 0:ln], pbr[:, cs], ALU.mult)
                    P.tt("vector", t[1], Tii[:, j, 0:ln], pbi[:, cs], ALU.mult)
                    P.tt("vector", t[0], t[0], t[1], ALU.subtract)
                    P.tt("vector", t[2], Tir[:, j, 0:ln], pbi[:, cs], ALU.mult)
                    P.tt("vector", t[3], Tii[:, j, 0:ln], pbr[:, cs], ALU.mult)
                    P.tt("vector", t[2], t[2], t[3], ALU.add)
                    gs = gsc[segctr[0] % 2]
                    segctr[0] += 1
                    P.scan(gs[0][:, 0:ln], rho_t[:, j, 0:ln], t[0], init_re)
                    P.scan(gs[1][:, 0:ln], rho_t[:, j, 0:ln], t[2], init_im)
                    pt_ = [ptmp[q][:, 0:ln] for q in range(4)]
                    if "rot" not in SKIP:
                        P.tt(peng, pt_[0], Tor[:, j, 0:ln], gs[0][:, 0:ln], ALU.mult)
                        P.tt(peng, pt_[1], Toi[:, j, 0:ln], gs[1][:, 0:ln], ALU.mult)
                        P.tt(peng, hrb[:, j, cs], pt_[0], pt_[1], ALU.subtract)
                        P.tt(peng, pt_[2], Tor[:, j, 0:ln], gs[1][:, 0:ln], ALU.mult)
                        P.tt(peng, pt_[3], Toi[:, j, 0:ln], gs[0][:, 0:ln], ALU.mult)
                        P.tt(peng, hib[:, j, cs], pt_[2], pt_[3], ALU.add)
                    e = ln - 1
                    gr = gs[0][:, e:e + 1]; gi = gs[1][:, e:e + 1]
                    er = Tor[:, j, e:e + 1]; ei = Toi[:, j, e:e + 1]
                    a1 = tmp[6][:, 0:1]; a2 = tmp[6][:, 1:2]; a3 = tmp[6][:, 2:3]; a4 = tmp[6][:, 3:4]
                    P.tt("vector", a1, er, gr, ALU.mult)
                    P.tt("vector", a2, ei, gi, ALU.mult)
                    P.tt("vector", fin_re, a1, a2, ALU.subtract)
                    P.tt("vector", a3, er, gi, ALU.mult)
                    P.tt("vector", a4, ei, gr, ALU.mult)
                    P.tt("vector", fin_im, a3, a4, ALU.add)

                for (c0, n) in TILES_A:
                    P.dma("sync", xT[:, :, 0:n], xres[:, :, c0:c0 + n])
                    rms_h(xT, h, sqc, rstd, "mix_norm0", n)
                    for cc in range(4):
                        pu = ps[4 + cc % 2]
                        for c in range(8):
                            P.mm(pu[:, 0:n], wab[:, c, cc * 128:(cc + 1) * 128], h[:, c, 0:n], c == 0, c == 7)
                        P.copy("vector", u32[:, cc, 0:n], pu[:, 0:n])
                        P.copy("scalar", ub[:, cc, 0:n], u32[:, cc, 0:n])
                    for q in range(4 if c0 < 4096 else 0):
                        pbr = ps[(q % 2) * 2]; pbi = ps[(q % 2) * 2 + 1]
                        for jj in range(4):
                            j = 4 * q + jj
                            P.mm(pbr[:, jj * 128:(jj + 1) * 128], Bre[:, j, :], ub[:, q, 0:n])
                            P.mm(pbi[:, jj * 128:(jj + 1) * 128], Bim[:, j, :], ub[:, q, 0:n])

                        def Q4(tt_):
                            return R(tt_.t[:, 4 * q:4 * q + 4, :].rearrange("p j t -> p (j t)"), tt_.b)

                        def L4(tt_):
                            return R(tt_.t[:, :].rearrange("p (j t) -> p j t", t=128)[:, :, 127:128].rearrange("p j o -> p (j o)"), tt_.b)

                        def T4(tt_):
                            return R(tt_.t[:, 4 * q:4 * q + 4, 127:128].rearrange("p j o -> p (j o)"), tt_.b)
                        d0, d1, d2, d3 = [big[k][:, :] for k in range(4)]
                        P.tt("vector", d0, Q4(Tir), pbr[:, :], ALU.mult)
                        P.tt("vector", d1, Q4(Tii), pbi[:, :], ALU.mult)
                        P.tt("vector", d0, d0, d1, ALU.subtract)
                        P.tt("vector", d2, Q4(Tir), pbi[:, :], ALU.mult)
                        P.tt("vector", d3, Q4(Tii), pbr[:, :], ALU.mult)
                        P.tt("vector", d2, d2, d3, ALU.add)
                        gR, gI = gbig[q % 2]
                        for jj in range(4):
                            j = 4 * q + jj
                            cs_ = slice(jj * 128, (jj + 1) * 128)
                            P.scan(gR[:, cs_], rho_t[:, j, :], big[0][:, cs_], hst[:, 0, j:j + 1])
                            P.scan(gI[:, cs_], rho_t[:, j, :], big[2][:, cs_], hst[:, 1, j:j + 1])
                        p0, p1, p2, p3 = [pbig[k][:, :] for k in range(4)]
                        P.tt(peng, p0, Q4(Tor), gR[:, :], ALU.mult)
                        P.tt(peng, p1, Q4(Toi), gI[:, :], ALU.mult)
                        P.tt(peng, Q4(hrb), p0, p1, ALU.subtract)
                        P.tt(peng, p2, Q4(Tor), gI[:, :], ALU.mult)
                        P.tt(peng, p3, Q4(Toi), gR[:, :], ALU.mult)
                        P.tt(peng, Q4(hib), p2, p3, ALU.add)
                        a1, a2, a3, a4 = [sm4[:, 4 * k:4 * k + 4] for k in range(4)]
                        P.tt("vector", a1, T4(Tor), L4(gR), ALU.mult)
                        P.tt("vector", a2, T4(Toi), L4(gI), ALU.mult)
                        P.tt("vector", a3, T4(Tor), L4(gI), ALU.mult)
                        P.tt("vector", a4, T4(Toi), L4(gR), ALU.mult)
                        P.tt("vector", hst[:, 0, 4 * q:4 * q + 4], a1, a2, ALU.subtract)
                        P.tt("vector", hst[:, 1, 4 * q:4 * q + 4], a3, a4, ALU.add)
                    for j in range(16 if c0 >= 4096 else 0):
                        pbr = ps[(j % 2) * 2]; pbi = ps[(j % 2) * 2 + 1]
                        P.mm(pbr[:, 0:n], Bre[:, j, :], ub[:, j // 4, 0:n])
                        P.mm(pbi[:, 0:n], Bim[:, j, :], ub[:, j // 4, 0:n])
                        if c0 < 4096:
                            s5_segment(j, pbr, pbi, 0, n, hst[:, 0, j:j + 1], hst[:, 1, j:j + 1],
                                       hst[:, 0, j:j + 1], hst[:, 1, j:j + 1])
                        else:
                            s5_segment(j, pbr, pbi, 0, 16, cst[:, 2:3], cst[:, 2:3], hst[:, 0, j:j + 1], hst[:, 1, j:j + 1])
                            for s_ in range(NS):
                                o_re = (s_ * 2 + 0) * 16 + j
                                o_im = (s_ * 2 + 1) * 16 + j
                                s5_segment(j, pbr, pbi, 16 + 16 * s_, 16, st0t[:, o_re:o_re + 1], st0t[:, o_im:o_im + 1],
                                           sst[:, o_re:o_re + 1], sst[:, o_im:o_im + 1])
                    for cc in range(4 if "y" not in SKIP else 0):
                        for jj in range(4):
                            j = 4 * cc + jj
                            P.mm(ps[6][:, 0:n], Cre[:, j, :], hrb[:, j, 0:n], jj == 0, False)
                            P.mm(ps[6][:, 0:n], Cimn[:, j, :], hib[:, j, 0:n], False, jj == 3)
                        P.stt(yv[:, cc, 0:n], u32[:, cc, 0:n], pcol("s5d", cc), ps[6][:, 0:n], ALU.mult, ALU.add)
                        P.tt("vector", gt1[:, 0:n], yv[:, cc, 0:n], yv[:, cc, 0:n], ALU.mult)
                        P.ts("vector", gt1[:, 0:n], gt1[:, 0:n], 0.044715, 1.0, ALU.mult, ALU.add)
                        P.tt("vector", gt1[:, 0:n], gt1[:, 0:n], yv[:, cc, 0:n], ALU.mult)
                        P.act(gt2[:, 0:n], gt1[:, 0:n], AF.Sigmoid, scale=2.0 * math.sqrt(2.0 / math.pi))
                        P.tt("vector", g32[:, cc, 0:n], yv[:, cc, 0:n], gt2[:, 0:n], ALU.mult)
                        P.copy("scalar", gb[:, cc, 0:n], g32[:, cc, 0:n])
                    for cc in range(4 if "y" not in SKIP else 0):
                        for c in range(4):
                            P.mm(ps[6][:, 0:n], gluwb[:, c, cc * 128:(cc + 1) * 128], gb[:, c, 0:n], c == 0, c == 3)
                        P.act(gt2[:, 0:n], ps[6][:, 0:n], AF.Sigmoid, bias=pcol("glub", cc))
                        P.tt("vector", so[:, cc, 0:n], g32[:, cc, 0:n], gt2[:, 0:n], ALU.mult)
                    if "y" not in SKIP:
                        P.dma("gpsimd", R(mixin.t[0:512, c0:c0 + n].rearrange("(c p) t -> p c t", p=128), mixin.b), so[:, :, 0:n])
                    for hh in range(4 if "q" not in SKIP else 0):
                        pq = ps[4 + hh % 2]
                        for c in range(8):
                            P.mm(pq[:, 0:n], wab[:, c, 512 + hh * 128:512 + (hh + 1) * 128], h[:, c, 0:n], c == 0, c == 7)
                        sqt_ = sqtr[hh % 2]; rs_ = rsr[hh % 2]; pst = ps[6 + hh % 2]
                        P.act(sqt_[:, 0:n], pq[:, 0:n], AF.Square)
                        P.mm(pst[:, 0:n], blk64[:, :], sqt_[:, 0:n])
                        P.act(rs_[:, 0:n], pst[:, 0:n], AF.Sqrt, bias=eps_c, scale=1.0 / 64)
                        P.recip(rs_[:, 0:n], rs_[:, 0:n])
                        P.stt(qn[:, hh, 0:n], pq[:, 0:n], pcol("gq"), rs_[:, 0:n], ALU.mult, ALU.mult)
                    if "q" not in SKIP:
                        P.dma("gpsimd", R(QD.t[:, :, c0:c0 + n].rearrange("h p t -> p h t"), QD.b), qn[:, :, 0:n])
                    for hh in range(4):
                        pq = ps[4 + hh % 2]
                        for c in range(8):
                            P.mm(pq[:, 0:n], wab[:, c, 1024 + hh * 128:1024 + (hh + 1) * 128], h[:, c, 0:n], c == 0, c == 7)
                        sqt_ = sqtr[hh % 2]; rs_ = rsr[hh % 2]; pst = ps[6 + hh % 2]
                        P.act(sqt_[:, 0:n], pq[:, 0:n], AF.Square)
                        P.mm(pst[:, 0:n], blk64[:, :], sqt_[:, 0:n])
                        P.act(rs_[:, 0:n], pst[:, 0:n], AF.Sqrt, bias=eps_c, scale=1.0 / 64)
                        P.recip(rs_[:, 0:n], rs_[:, 0:n])
                        P.stt(kn32[:, hh, 0:n], pq[:, 0:n], pcol("gk"), rs_[:, 0:n], ALU.mult, ALU.mult)
                    store_tok(kn32, 4, c0, n, {"p": O["dkp"], "s": O["dks"]}, ktok, 512)
                    for hh in range(4 if "kvb" not in SKIP else 0):
                        P.copy("gpsimd", knb[:, hh, 0:n], kn32[:, hh, 0:n])
                    for slot, k0, lo, cnt in (key_dst(c0, n) if "kvs" not in SKIP else []):
                        P.dma("gpsimd", R(KD.t[slot, :, :, k0:k0 + cnt].rearrange("h p t -> p h t"), KD.b), knb[:, :, lo:lo + cnt])
                    pv = ps[6]
                    for c in range(8):
                        P.mm(pv[0:n, 0:512], h[:, c, 0:n], wab[:, c, 1536:2048], c == 0, c == 7)
                    P.copy("vector", vtok[0:n, :], pv[0:n, 0:512])
                    if "vtb" not in SKIP:
                        P.copy("gpsimd", vtb[0:n, :], vtok[0:n, :])
                    for grp, r0, lo, cnt in out_rows(c0, n):
                        P.dma("gpsimd", O["dv" + grp][r0:r0 + cnt, :], vtok[lo:lo + cnt, :])
                    for slot, k0, lo, cnt in (key_dst(c0, n) if "kvs" not in SKIP else []):
                        P.dma("gpsimd", VD[slot, k0:k0 + cnt, :], vtb[lo:lo + cnt, :])
                P.dma("gpsimd", R(s5p.t.rearrange("r p j -> p r j"), s5p.b), hst[:, :, :])
                P.dma("gpsimd", R(s5s.t.rearrange("s r p j -> p (s r) j"), s5s.b),
                      R(sst.t.rearrange("p (a j) -> p a j", j=16), sst.b))
                P.emit()


        def conv_cache(sbk, kc, vc, KDST, VDST, tag):
            ck32s = [sbk("ck32%s%d" % (tag, q), [128, 4, 512]) for q in range(2)]
            kbts = [sbk("kbt%s%d" % (tag, q), [128, 4, 512], BF16) for q in range(2)]
            cv32s = [sbk("cv32%s%d" % (tag, q), [128, 4, 512]) for q in range(2)]
            cvbs = [sbk("cvb%s%d" % (tag, q), [128, 4, 512], BF16) for q in range(2)]
            it_ = 0
            for s_ in range(NS):
                for g4 in range(8):
                    ck32, kbt, cv32, cvb = ck32s[it_ % 2], kbts[it_ % 2], cv32s[it_ % 2], cvbs[it_ % 2]
                    it_ += 1
                    r0 = g4 * 512
                    P.dma("sync", ck32[:, :, :], R(kc.t[s_, r0:r0 + 512, :].rearrange("(b p) w -> p b w", p=128), kc.b))
                    P.dma("sync", cv32[:, :, :], R(vc.t[s_, r0:r0 + 512, :].rearrange("(b p) w -> p b w", p=128), vc.b))
                    for b in range(4):
                        pT = ps[4 + b]
                        for hh in range(4):
                            P.tr(pT[:, hh * 128:(hh + 1) * 128], ck32[:, b, hh * 128:(hh + 1) * 128], ident[:, :])
                        for hh in range(4):
                            P.copy("vector" if b % 2 else "scalar", kbt[:, hh, b * 128:(b + 1) * 128], pT[:, hh * 128:(hh + 1) * 128])
                    P.dma("gpsimd", R(KDST.t[1 + s_, :, :, r0:r0 + 512].rearrange("h p t -> p h t"), KDST.b), kbt[:, :, :])
                    P.copy("gpsimd", R(cvb.t[:, :, :].rearrange("p b w -> p (b w)"), cvb.b), R(cv32.t[:, :, :].rearrange("p b w -> p (b w)"), cv32.b))
                    P.dma("gpsimd", R(VDST.t[1 + s_, r0:r0 + 512, :].rearrange("(b p) w -> p b w", p=128), VDST.b), cvb[:, :, :])

        def attn_blocks(slot, Q):
            if Q == "meta":
                return [(4096, 16, 32, 0, "diag16")]
            if Q == "sample":
                return [(128 * j, 128, j, 0, "full") for j in range(32)] + [(4096, 16, 32, 0, "diag16")]
            bl = [(4096, 16, 32, 0, "full")] + [(128 * j, 128, j, 0, "full") for j in range(4 * Q)]
            bl += [(128 * (4 * Q + r), 128, 4 * Q + r, 128 * r, "diag128") for r in range(4)]
            return bl

        def m0b_phase():
            with ExitStack() as st:
                def sb(name, shape, dt=F32):
                    return TT(st.enter_context(nc.sbuf_tensor(name + "_m0b", shape, dt)))
                conv_cache(sb, cdk, cdv, KD, VD, "d")
                KA = [sb("KA%d" % k, [68, 2, TK], BF16) for k in range(2)]
                QA = [sb("QA%d" % k, [68, 2, LP], BF16) for k in range(2)]
                Vh = [sb("Vh%d" % k, [128, 33, 128], BF16) for k in range(2)]
                ptb = [sb("ptb%d" % k, [128, 512], BF16) for k in range(4)]
                tmpf = [sb("tmpf%d" % k, [128, 128]) for k in range(2)]
                nr = [sb("nr%d" % k, [128, 512]) for k in range(7)]
                ob = sb("ob", [128, 512], BF16)
                cdb = sb("cdb", [128, 512]); cds = sb("cds", [16, 64])
                lw = sb("lw", [128, 8])
                P.dma("sync", cdb[:, :], cdbig[:, :])
                P.dma("sync", cds[:, :], cdsm[:, :])
                P.tt("vector", lw[0:64, 0:1], pcol("lamv", 0, 1, 64), pcol("lamv", 1, 1, 64), ALU.mult)
                P.tt("vector", lw[0:64, 1:2], pcol("lamv", 2, 1, 64), pcol("lamv", 3, 1, 64), ALU.mult)
                P.mm(ps[7][:, 0:2], ones[0:64, :], lw[0:64, 0:2])
                P.act(lw[:, 2:4], ps[7][:, 0:2], AF.Exp)
                P.ts("vector", lw[:, 4:5], lw[:, 2:3], lw[:, 3:4], LAM_INIT0, ALU.subtract, ALU.add)
                P.ts("vector", lw[:, 5:6], lw[:, 4:5], -1.0, None, ALU.mult)
                P.ts("vector", lw[:, 6:7], pcol("gsub"), 1.0 - LAM_INIT0, None, ALU.mult)
                it = 0
                sctr = [0]
                for slot in range(1 + NS):
                    g = 0 if slot == 0 else 1
                    for hh in range(4):
                        ka = KA[it % 2]; qa = QA[it % 2]; vh = Vh[it % 2]
                        it += 1
                        P.dma("sync", ka[0:64, :, :], R(KD.t[slot, hh, :, :].rearrange("(m d) t -> d m t", m=2), KD.b))
                        for m in range(2):
                            P.dma("sync", ka[64:68, m, :], AUGK[g, hh, :, :])
                        if slot == 0:
                            P.dma("sync", qa[0:64, :, :], R(QD.t[hh, :, 0:LP].rearrange("(m d) t -> d m t", m=2), QD.b))
                            for m in range(2):
                                P.dma("sync", qa[64:68, m, :], AUGQ[0, hh, :, :])
                            qtiles = [(512 * Q, 512, Q) for Q in range(8)] + [(4096, 16, "meta")]
                            xcol0 = 0
                        else:
                            xc = LP + 16 * (slot - 1)
                            P.dma("sync", qa[0:64, :, 0:16], R(QD.t[hh, :, xc:xc + 16].rearrange("(m d) t -> d m t", m=2), QD.b))
                            for m in range(2):
                                P.dma("sync", qa[64:68, m, 0:16], AUGQ[1, hh, :, 4096:4112])
                            qtiles = [(0, 16, "sample")]
                            xcol0 = xc
                        P.dma("sync", vh[:, 0:32, :], R(VD.t[slot, 0:4096, hh * 128:(hh + 1) * 128].rearrange("(b p) d -> p b d", p=128), VD.b))
                        P.dma("sync", vh[0:16, 32, :], VD[slot, 4096:4112, hh * 128:(hh + 1) * 128])
                        for (q0, nq, Q) in qtiles:
                            blocks = attn_blocks(slot, Q)
                            nb = len(blocks)
                            if Q == "sample":
                                groups = [blocks[g8 * 8:(g8 + 1) * 8] for g8 in range(4)] + [[blocks[32]]]
                            else:
                                groups = [[blk_] for blk_ in blocks]
                            units = [(gi, grp, m) for gi, grp in enumerate(groups) for m in range(2)]
                            ng = len(groups)
                            ust = {}

                            def qk(ui):
                                gi, grp, m = units[ui]
                                pS_ = (ps[0], ps[1], ps[2], ps[7])[sctr[0] % 4]; pt_ = ptb[sctr[0] % 4]; tf = tmpf[sctr[0] % 2]
                                sctr[0] += 1
                                ust[ui] = (pS_, pt_, tf)
                                if len(grp) > 1:
                                    for bb, (k0, nk, jb, qs, kind) in enumerate(grp):
                                        P.mm(pS_[0:nk, bb * 16:(bb + 1) * 16], ka[0:68, m, k0:k0 + nk], qa[0:68, m, q0:q0 + 16])
                                else:
                                    (k0, nk, jb, qs, kind) = grp[0]
                                    P.mm(pS_[0:nk, qs:nq], ka[0:68, m, k0:k0 + nk], qa[0:68, m, q0 + qs:q0 + nq])

                            def rest(ui):
                                gi, grp, m = units[ui]
                                pS_, pt_, tf = ust.pop(ui)
                                first_, last_ = gi == 0, gi == ng - 1
                                if len(grp) > 1:
                                    P.act(pt_[0:128, 0:128], pS_[0:128, 0:128], AF.Exp, scale=0.125)
                                    for bb, (k0, nk, jb, qs, kind) in enumerate(grp):
                                        P.mm(ps[3 + 2 * m][:, 0:16], vh[0:nk, jb, :], pt_[0:nk, bb * 16:(bb + 1) * 16], first_ and bb == 0, False)
                                        P.mm(ps[4 + 2 * m][:, 0:16], onesb[0:nk, :], pt_[0:nk, bb * 16:(bb + 1) * 16], first_ and bb == 0, False)
                                    return
                                (k0, nk, jb, qs, kind) = grp[0]
                                if kind == "diag128":
                                    P.stt(tf[0:128, 0:128], pS_[0:128, qs:qs + 128], 0.125, cdb[:, hh * 128:(hh + 1) * 128], ALU.mult, ALU.add)
                                    P.act(pt_[0:128, qs:qs + 128], tf[0:128, 0:128], AF.Exp)
                                    if qs + 128 < nq:
                                        P.act(pt_[0:128, qs + 128:nq], pS_[0:128, qs + 128:nq], AF.Exp, scale=0.125)
                                elif kind == "diag16":
                                    P.stt(tf[0:16, 0:16], pS_[0:16, 0:16], 0.125, cds[:, hh * 16:(hh + 1) * 16], ALU.mult, ALU.add)
                                    P.act(pt_[0:16, 0:16], tf[0:16, 0:16], AF.Exp)
                                else:
                                    P.act(pt_[0:nk, qs:nq], pS_[0:nk, qs:nq], AF.Exp, scale=0.125)
                                P.mm(ps[3 + 2 * m][:, qs:nq], vh[0:nk, jb, :], pt_[0:nk, qs:nq], first_, last_)
                                P.mm(ps[4 + 2 * m][:, qs:nq], onesb[0:nk, :], pt_[0:nk, qs:nq], first_, last_)
                            qk(0)
                            if len(units) > 1:
                                qk(1)
                            for ui in range(len(units)):
                                if ui + 2 < len(units):
                                    qk(ui + 2)
                                rest(ui)
                            r0_, A_, r1_, B_, Ot, sq_, rsn = [nr[q][:, 0:nq] for q in range(7)]
                            P.recip(r0_, ps[4][:, 0:nq])
                            P.tt("vector", A_, ps[3][:, 0:nq], r0_, ALU.mult)
                            P.recip(r1_, ps[6][:, 0:nq])
                            P.tt("vector", B_, ps[5][:, 0:nq], r1_, ALU.mult)
                            P.stt(Ot, B_, lw[:, 5:6], A_, ALU.mult, ALU.add)
                            P.act(sq_, Ot, AF.Square)
                            P.mm(ps[7][:, 0:nq], ones[:, :], sq_)
                            P.act(rsn, ps[7][:, 0:nq], AF.Sqrt, bias=eps_c, scale=1.0 / 128)
                            P.recip(rsn, rsn)
                            P.stt(ob[:, 0:nq], Ot, lw[:, 6:7], rsn, ALU.mult, ALU.mult)
                            xc0 = xcol0 + q0
                            P.dma("gpsimd", mixin[512 + hh * 128:512 + (hh + 1) * 128, xc0:xc0 + nq], ob[:, 0:nq])
                P.emit()

        def mixout_phase(wsrc, tag):
            with ExitStack() as st:
                def sb(name, shape, dt=F32):
                    return TT(st.enter_context(nc.sbuf_tensor(name + "_mo" + tag, shape, dt)))
                wmo = sb("wmo", [128, 8, D], BF16)
                stg = [sb("stg%d" % k, [128, 1024]) for k in range(2)]
                xT = sb("xT", [128, 8, NT]); mx = sb("mx", [128, 8, NT], BF16)
                engs = ("vector", "gpsimd", "scalar")
                for c in range(8):
                    P.dma("sync", stg[c % 2][:, :], wsrc[c * 128:(c + 1) * 128, :])
                    P.copy(engs[c % 3], wmo[:, c, :], stg[c % 2][:, :])
                for (c0, n) in TILES_F:
                    P.dma("sync", xT[:, :, 0:n], xres[:, :, c0:c0 + n])
                    P.dma("sync", mx[:, :, 0:n], R(mixin.t[:, c0:c0 + n].rearrange("(c p) t -> p c t", p=128), mixin.b))
                    for m in range(8):
                        po = ps[m % 4]
                        for c in range(8):
                            P.mm(po[:, 0:n], wmo[:, c, m * 128:(m + 1) * 128], mx[:, c, 0:n], c == 0, c == 7)
                        P.tt("vector", xT[:, m, 0:n], xT[:, m, 0:n], po[:, 0:n], ALU.add)
                    P.dma("gpsimd", xres[:, :, c0:c0 + n], xT[:, :, 0:n])
                P.emit()

        def dump_x_phase():
            with ExitStack() as st:
                xT = TT(st.enter_context(nc.sbuf_tensor("xT_dump", [128, 8, NT], F32)))
                ytok = TT(st.enter_context(nc.sbuf_tensor("ytok_dump", [128, D], F32)))
                for (c0, n) in TILES_F:
                    P.dma("sync", xT[:, :, 0:n], xres[:, :, c0:c0 + n])
                    store_tok(xT, 8, c0, n, {"p": yp, "s": ys}, ytok, D, is_y=True)
                P.emit()


        def m1a_phase():
            NA = 128
            with ExitStack() as st:
                def sb(name, shape, dt=F32):
                    return TT(st.enter_context(nc.sbuf_tensor(name + "_m1a", shape, dt)))
                wcd = sb("wcd", [128, 8, CDW], BF16); wkr = sb("wkr", [128, 8, 32], BF16)
                stg = [sb("stg%d" % k, [128, 1024]) for k in range(2)]
                qbb = sb("qbb", [128, 3, 768], BF16); qbrb = sb("qbrb", [128, 3, 768], BF16)
                kvbb = sb("kvbb", [128, 2, 1024], BF16)
                xT = sb("xT", [128, 8, NA]); sqc = [sb("sqc%d" % k, [128, NA]) for k in range(2)]
                rstd = sb("rstd", [128, NA]); h = sb("h", [128, 8, NA], BF16)
                sqt = sb("sqt", [128, NA]); rs = sb("rs", [128, NA])
                qnf = sb("qnf", [128, 4, NA], BF16); kn32 = sb("kn32", [128, 4, NA]); knb = sb("knb", [128, 4, NA], BF16)
                ktok = sb("ktok", [128, 512]); vtok = sb("vtok", [128, 512]); vtb = sb("vtb", [128, 512], BF16)
                ones8 = sb("ones8", [8, 512]); lf32 = sb("lf32", [8, 512]); Fc = sb("Fc", [8, 512]); fe = sb("fe", [8, 512])
                fst = sb("fst", [8, 8])
                x8 = sb("x8", [8, 512]); r1 = sb("r1", [8, 512])
                sp_hi = [sb("sphi%d" % r, [8, 512], BF16) for r in range(3)]
                sn_hi = [sb("snhi%d" % r, [8, 512], BF16) for r in range(3)]
                lftok = sb("lftok", [128, 8]); lc32 = sb("lc32", [128, 4, 8])
                qa32 = sb("qa32", [128, 3, NA]); qan = sb("qan", [128, 3, NA], BF16)
                cos_t = sb("cos_t", [96, NA]); sin_t = sb("sin_t", [96, NA]); cos3 = sb("cos3", [32, NA]); sin3 = sb("sin3", [32, NA])
                qt1 = sb("qt1", [96, NA]); qt2 = sb("qt2", [96, NA]); qm = sb("qm", [96, NA], BF16)
                gmq = sb("gmq", [96, 1])
                ck32 = sb("ck32", [128, 2, NA]); ckn32 = sb("ckn32", [128, 2, NA]); ckvb = sb("ckvb", [128, 2, NA], BF16)
                cktok = sb("cktok", [128, 256])
                kpe32 = sb("kpe32", [32, NA]); kpeb = sb("kpeb", [32, NA], BF16); kptok = sb("kptok", [128, 32])
                knm = sb("knm", [128, 4, NA], BF16); ssq = sb("ssq", [128, NA]); spe = sb("spe", [128, NA]); sqp = sb("sqp", [32, NA])
                kscf = sb("kscf", [128, NA]); ksct = sb("ksct", [128, 8])
                kp_sets = [dict(knm=sb("knm_%d" % q, [128, 4, NA], BF16), sqk=sb("sqk_%d" % q, [128, 512]), ss8=sb("ss8_%d" % q, [128, 9]),
                                sqp2=sb("sqp2_%d" % q, [128, 32]), ksct=sb("ksct_%d" % q, [128, 8]), vtb=sb("vtbk_%d" % q, [128, 512], BF16),
                                ckvb=sb("ckvbp_%d" % q, [128, 2, NA], BF16), kpe32=sb("kpe32p_%d" % q, [32, NA]), kpeb=sb("kpebp_%d" % q, [32, NA], BF16))
                           for q in range(2)]
                kpctr = [0]
                sqt2 = [sb("sqt2%d" % k, [128, NA]) for k in range(2)]; rs2 = [sb("rs2%d" % k, [128, NA]) for k in range(2)]
                qt1s = [sb("qt1s%d" % k, [96, NA]) for k in range(2)]; qt2s = [sb("qt2s%d" % k, [96, NA]) for k in range(2)]
                qms = [sb("qms%d" % k, [96, NA], BF16) for k in range(2)]
                cc32 = sb("cc32", [128, 4, 256]); cp32 = sb("cp32", [128, 4, 32])
                engs = ("vector", "gpsimd", "scalar")
                k = 0
                for c in range(8):
                    for n0 in range(0, CDW, 1024):
                        w = min(1024, CDW - n0)
                        P.dma("sync", stg[k % 2][:, 0:w], cdwi[c * 128:(c + 1) * 128, n0:n0 + w])
                        P.copy(engs[k % 3], wcd[:, c, n0:n0 + w], stg[k % 2][:, 0:w])
                        k += 1
                    P.copy("vector", wkr[:, c, 0:16], wcd[:, c, 2200:2216])
                    P.copy("vector", wkr[:, c, 16:32], wcd[:, c, 2184:2200])
                for c in range(3):
                    for src, dst in ((qb, qbb), (qbr, qbrb)):
                        P.dma("sync", stg[k % 2][:, 0:768], src[c * 128:(c + 1) * 128, :])
                        P.copy(engs[k % 3], dst[:, c, :], stg[k % 2][:, 0:768])
                        k += 1
                for c in range(2):
                    P.dma("sync", stg[k % 2][:, :], kvb[c * 128:(c + 1) * 128, :])
                    P.copy(engs[k % 3], kvbb[:, c, :], stg[k % 2][:, :])
                    k += 1
                P.memset("vector", ones8[:, :], 1.0)
                P.memset("vector", fst[:, :], 0.0)
                P.ts("vector", fst[:, 5:6], pcol("fbias", 0, 1, 8), -1.0, None, ALU.mult)
                P.tt("vector", gmq[:, :], pcol("mqn", 0, 1, 96), pcol("mkn", 0, 1, 96), ALU.mult)

                def f_split(n_, kdst, qdst):
                    P.ts("vector", x8[:, 0:n_], Fc[:, 0:n_], 8.0, None, ALU.mult)
                    P.copy("vector", sp_hi[0][:, 0:n_], x8[:, 0:n_])
                    P.tt("vector", r1[:, 0:n_], x8[:, 0:n_], sp_hi[0][:, 0:n_], ALU.subtract)
                    P.copy("vector", sp_hi[1][:, 0:n_], r1[:, 0:n_])
                    P.tt("vector", r1[:, 0:n_], r1[:, 0:n_], sp_hi[1][:, 0:n_], ALU.subtract)
                    P.copy("vector", sp_hi[2][:, 0:n_], r1[:, 0:n_])
                    for r in range(3):
                        P.ts("vector", sn_hi[r][:, 0:n_], sp_hi[r][:, 0:n_], -1.0, None, ALU.mult)
                    for (slot, k0, lo, cnt) in kdst:
                        for r in range(3):
                            P.dma("gpsimd", FK[slot, :, r, k0:k0 + cnt], sn_hi[r][:, lo:lo + cnt])
                    for (x0, lo, cnt) in qdst:
                        for r in range(3):
                            P.dma("gpsimd", FQ[:, r, x0:x0 + cnt], sp_hi[r][:, lo:lo + cnt])

                def mla_kprep(ckvb_, kpe_tok, n_, slot_list, blk):
                    S_ = kp_sets[kpctr[0] % 2]
                    kpctr[0] += 1
                    knm, sqk, ss8, sqp2, ksct, vtb = S_["knm"], S_["sqk"], S_["ss8"], S_["sqp2"], S_["ksct"], S_["vtb"]
                    for cc in range(4):
                        pa = ps[cc % 2]
                        for c in range(2):
                            P.mm(pa[:, 0:n_], kvbb[:, c, cc * 128:(cc + 1) * 128], ckvb_[:, c, 0:n_], c == 0, c == 1)
                        P.copy("vector", knm[:, cc, 0:n_], pa[:, 0:n_])
                    for c in range(2):
                        P.mm(ps[2][0:n_, 0:512], ckvb_[:, c, 0:n_], kvbb[:, c, 0:512], c == 0, c == 1)
                    P.act(sqk[0:n_, :], ps[2][0:n_, 0:512], AF.Square)
                    P.reduce(ss8[0:n_, 0:8], R(sqk.t[0:n_, :].rearrange("p (h d) -> p h d", d=64), sqk.b))
                    P.act(sqp2[0:n_, :], kpe_tok, AF.Square)
                    P.reduce(ss8[0:n_, 8:9], sqp2[0:n_, :])
                    P.ts("vector", ss8[0:n_, 0:8], ss8[0:n_, 0:8], ss8[0:n_, 8:9], None, ALU.add)
                    P.act(ss8[0:n_, 0:8], ss8[0:n_, 0:8], AF.Sqrt, bias=cst[0:n_, 0:1], scale=1.0 / 96)
                    P.recip(ss8[0:n_, 0:8], ss8[0:n_, 0:8])
                    P.ts("vector", ksct[0:n_, :], ss8[0:n_, 0:8], 96.0 ** -0.5, None, ALU.mult)
                    for c in range(2):
                        P.mm(ps[3][0:n_, 0:512], ckvb_[:, c, 0:n_], kvbb[:, c, 512:1024], c == 0, c == 1)
                    P.copy("vector", vtb[0:n_, :], ps[3][0:n_, 0:512])
                    for (slot, k0, lo, cnt) in slot_list:
                        P.dma("gpsimd", R(KM.t[slot, :, :, k0:k0 + cnt].rearrange("h p t -> p h t"), KM.b), knm[:, :, lo:lo + cnt])
                        P.dma("gpsimd", VM[slot, k0:k0 + cnt, :], vtb[lo:lo + cnt, :])
                        P.dma("gpsimd", ksc_all[0:cnt, slot, blk if k0 < 4096 else 32, :], ksct[lo:lo + cnt, :])

                for s_ in range(NS):
                    for g4 in range(8):
                        r0 = g4 * 512
                        P.dma("sync", lc32[:, :, :], R(clf.t[s_, r0:r0 + 512, :].rearrange("(b p) w -> p b w", p=128), clf.b))
                        for b in range(4):
                            P.tr(ps[5][0:8, b * 128:(b + 1) * 128], lc32[:, b, :], ident[:, :])
                        P.copy("vector", lf32[:, :], ps[5][0:8, 0:512])
                        P.scan(Fc[:, :], ones8[:, :], lf32[:, :], fst[:, 1 + s_:2 + s_])
                        P.copy("vector", fst[:, 1 + s_:2 + s_], Fc[:, 511:512])
                        f_split(512, [(1 + s_, r0, 0, 512)], [])
                        P.dma("sync", cc32[:, :, :], R(cckv.t[s_, r0:r0 + 512, :].rearrange("(b p) w -> p b w", p=128), cckv.b))
                        P.dma("sync", cp32[:, :, :], R(ckpe.t[s_, r0:r0 + 512, :].rearrange("(b p) w -> p b w", p=128), ckpe.b))
                        for b in range(4):
                            S2 = kp_sets[kpctr[0] % 2]
                            ckvb_p, kpe32_p, kpeb_p = S2["ckvb"], S2["kpe32"], S2["kpeb"]
                            for c in range(2):
                                P.tr(ps[6][:, c * 128:(c + 1) * 128], cc32[:, b, c * 128:(c + 1) * 128], ident[:, :])
                                P.copy("vector", ckvb_p[:, c, :], ps[6][:, c * 128:(c + 1) * 128])
                            P.tr(ps[7][0:32, 0:128], cp32[:, b, :], ident[:, :])
                            P.copy("vector", kpe32_p[:, :], ps[7][0:32, 0:128])
                            P.copy("gpsimd", kpeb_p[:, :], kpe32_p[:, :])
                            P.dma("gpsimd", KP[1 + s_, :, r0 + b * 128:r0 + (b + 1) * 128], kpeb_p[:, :])
                            mla_kprep(ckvb_p, cp32[:, b, :], 128, [(1 + s_, r0 + b * 128, 0, 128)], g4 * 4 + b)

                for (c0, n) in TILES_A:
                    P.dma("sync", xT[:, :, 0:n], xres[:, :, c0:c0 + n])
                    P.dma("sync", cos_t[:, 0:n], cos96[:, c0:c0 + n]); P.dma("sync", sin_t[:, 0:n], sin96[:, c0:c0 + n])
                    P.dma("sync", cos3[:, 0:n], cos32[:, c0:c0 + n]); P.dma("sync", sin3[:, 0:n], sin32[:, c0:c0 + n])
                    rms_h(xT, h, sqc, rstd, "mix_norm1", n)
                    for which in range(2):
                        for hh in range(4):
                            pq = ps[4 + hh % 2]
                            for c in range(8):
                                P.mm(pq[:, 0:n], wcd[:, c, which * 512 + hh * 128:which * 512 + (hh + 1) * 128], h[:, c, 0:n], c == 0, c == 7)
                            sqt_ = sqt2[hh % 2]; rs_ = rs2[hh % 2]; pst = ps[6 + hh % 2]
                            P.act(sqt_[:, 0:n], pq[:, 0:n], AF.Square)
                            P.mm(pst[:, 0:n], blk64[:, :], sqt_[:, 0:n])
                            P.act(rs_[:, 0:n], pst[:, 0:n], AF.Sqrt, bias=eps_c, scale=1.0 / 64)
                            P.recip(rs_[:, 0:n], rs_[:, 0:n])
                            if which == 0:
                                P.stt(qnf[:, hh, 0:n], pq[:, 0:n], pcol("fgq"), rs_[:, 0:n], ALU.mult, ALU.mult)
                            else:
                                P.stt(kn32[:, hh, 0:n], pq[:, 0:n], pcol("fgk"), rs_[:, 0:n], ALU.mult, ALU.mult)
                                P.copy("gpsimd", knb[:, hh, 0:n], kn32[:, hh, 0:n])
                    P.dma("gpsimd", R(QF.t[:, :, c0:c0 + n].rearrange("h p t -> p h t"), QF.b), qnf[:, :, 0:n])
                    store_tok(kn32, 4, c0, n, {"p": O["fkp"], "s": O["fks"]}, ktok, 512)
                    for slot, k0, lo, cnt in key_dst(c0, n):
                        P.dma("gpsimd", R(KF.t[slot, :, :, k0:k0 + cnt].rearrange("h p t -> p h t"), KF.b), knb[:, :, lo:lo + cnt])
                    for c in range(8):
                        P.mm(ps[6][0:n, 0:512], h[:, c, 0:n], wcd[:, c, 1024:1536], c == 0, c == 7)
                    P.copy("vector", vtok[0:n, :], ps[6][0:n, 0:512])
                    P.copy("gpsimd", vtb[0:n, :], vtok[0:n, :])
                    for grp, r0, lo, cnt in out_rows(c0, n):
                        P.dma("gpsimd", O["fv" + grp][r0:r0 + cnt, :], vtok[lo:lo + cnt, :])
                    for slot, k0, lo, cnt in key_dst(c0, n):
                        P.dma("gpsimd", VF[slot, k0:k0 + cnt, :], vtb[lo:lo + cnt, :])
                    for c in range(8):
                        P.mm(ps[4][0:8, 0:n], wcd[:, c, 1536:1544], h[:, c, 0:n], c == 0, c == 7)
                    P.act(fe[:, 0:n], ps[4][0:8, 0:n], AF.Exp, bias=fst[:, 5:6], scale=-1.0)
                    P.act(fe[:, 0:n], fe[:, 0:n], AF.Ln, bias=cst[0:8, 3:4])
                    P.ts("vector", lf32[:, 0:n], fe[:, 0:n], -1.0, None, ALU.mult)
                    P.tr(ps[5][0:n, 0:8], lf32[:, 0:n], ident[0:8, 0:8])
                    P.copy("vector", lftok[0:n, :], ps[5][0:n, 0:8])
                    for grp, r0, lo, cnt in out_rows(c0, n):
                        P.dma("gpsimd", O["lf" + grp][r0:r0 + cnt, :], lftok[lo:lo + cnt, :])
                    if c0 < 4096:
                        P.scan(Fc[:, 0:n], ones8[:, 0:n], lf32[:, 0:n], fst[:, 0:1])
                        P.copy("vector", fst[:, 0:1], Fc[:, n - 1:n])
                        f_split(n, [(0, c0, 0, n)], [(c0, 0, n)])
                    else:
                        P.scan(Fc[:, 0:16], ones8[:, 0:16], lf32[:, 0:16], cst[0:8, 2:3])
                        P.copy("vector", fst[:, 0:1], Fc[:, 15:16])
                        for s_ in range(NS):
                            o_ = 16 + 16 * s_
                            P.scan(Fc[:, o_:o_ + 16], ones8[:, 0:16], lf32[:, o_:o_ + 16], fst[:, 1 + s_:2 + s_])
                        f_split(80, key_dst(c0, n), [(c0, 0, n)])
                    for cc in range(3):
                        pq = ps[4 + cc % 2]
                        for c in range(8):
                            P.mm(pq[:, 0:n], wcd[:, c, 1544 + cc * 128:1544 + (cc + 1) * 128], h[:, c, 0:n], c == 0, c == 7)
                        P.copy("vector", qa32[:, cc, 0:n], pq[:, 0:n])
                        P.act(sqt[:, 0:n], qa32[:, cc, 0:n], AF.Square)
                        P.mm(ps[7][:, 0:n], ones[:, :], sqt[:, 0:n], cc == 0, cc == 2)
                    P.act(rs[:, 0:n], ps[7][:, 0:n], AF.Sqrt, bias=eps_c, scale=1.0 / 384)
                    P.recip(rs[:, 0:n], rs[:, 0:n])
                    for cc in range(3):
                        P.stt(qan[:, cc, 0:n], qa32[:, cc, 0:n], pcol("qan", cc), rs[:, 0:n], ALU.mult, ALU.mult)
                    for hh in range(8):
                        pa = ps[(hh % 2) * 2]; pb = ps[(hh % 2) * 2 + 1]
                        for c in range(3):
                            P.mm(pa[0:96, 0:n], qbb[:, c, hh * 96:(hh + 1) * 96], qan[:, c, 0:n], c == 0, c == 2)
                        for c in range(3):
                            P.mm(pb[0:96, 0:n], qbrb[:, c, hh * 96:(hh + 1) * 96], qan[:, c, 0:n], c == 0, c == 2)
                        qt1_ = qt1s[hh % 2]; qt2_ = qt2s[hh % 2]; qm_ = qms[hh % 2]; pst = ps[6 + hh % 2]
                        P.tt("vector", qt1_[:, 0:n], pa[0:96, 0:n], cos_t[:, 0:n], ALU.mult)
                        P.tt("vector", qt2_[:, 0:n], pb[0:96, 0:n], sin_t[:, 0:n], ALU.mult)
                        P.tt("vector", qt1_[:, 0:n], qt1_[:, 0:n], qt2_[:, 0:n], ALU.add)
                        P.act(qt2_[:, 0:n], qt1_[:, 0:n], AF.Square)
                        P.mm(pst[0:96, 0:n], ones[0:96, 0:96], qt2_[:, 0:n])
                        P.act(qt2_[:, 0:n], pst[0:96, 0:n], AF.Sqrt, bias=cst[0:96, 0:1], scale=1.0 / 96)
                        P.recip(qt2_[:, 0:n], qt2_[:, 0:n])
                        P.stt(qm_[:, 0:n], qt1_[:, 0:n], gmq[:, :], qt2_[:, 0:n], ALU.mult, ALU.mult)
                        P.dma("gpsimd", QM[hh, :, c0:c0 + n], qm_[:, 0:n])
                    for cc in range(2):
                        pq = ps[4 + cc % 2]
                        for c in range(8):
                            P.mm(pq[:, 0:n], wcd[:, c, 1928 + cc * 128:1928 + (cc + 1) * 128], h[:, c, 0:n], c == 0, c == 7)
                        P.copy("vector", ck32[:, cc, 0:n], pq[:, 0:n])
                        P.act(sqt[:, 0:n], ck32[:, cc, 0:n], AF.Square)
                        P.mm(ps[7][:, 0:n], ones[:, :], sqt[:, 0:n], cc == 0, cc == 1)
                    P.act(rs[:, 0:n], ps[7][:, 0:n], AF.Sqrt, bias=eps_c, scale=1.0 / 256)
                    P.recip(rs[:, 0:n], rs[:, 0:n])
                    for cc in range(2):
                        P.stt(ckn32[:, cc, 0:n], ck32[:, cc, 0:n], pcol("kvan", cc), rs[:, 0:n], ALU.mult, ALU.mult)
                        P.copy("gpsimd", ckvb[:, cc, 0:n], ckn32[:, cc, 0:n])
                    store_tok(ckn32, 2, c0, n, {"p": O["ckvp"], "s": O["ckvs"]}, cktok, 256)
                    for c in range(8):
                        P.mm(ps[0][0:32, 0:n], wcd[:, c, 2184:2216], h[:, c, 0:n], c == 0, c == 7)
                    for c in range(8):
                        P.mm(ps[1][0:32, 0:n], wkr[:, c, :], h[:, c, 0:n], c == 0, c == 7)
                    P.tt("vector", kpe32[:, 0:n], ps[0][0:32, 0:n], cos3[:, 0:n], ALU.mult)
                    P.tt("vector", sqp[:, 0:n], ps[1][0:32, 0:n], sin3[:, 0:n], ALU.mult)
                    P.tt("vector", kpe32[:, 0:n], kpe32[:, 0:n], sqp[:, 0:n], ALU.add)
                    P.copy("gpsimd", kpeb[:, 0:n], kpe32[:, 0:n])
                    P.tr(ps[5][0:n, 0:32], kpe32[:, 0:n], ident[0:32, 0:32])
                    P.copy("vector", kptok[0:n, :], ps[5][0:n, 0:32])
                    for grp, r0, lo, cnt in out_rows(c0, n):
                        P.dma("gpsimd", O["kpe" + grp][r0:r0 + cnt, :], kptok[lo:lo + cnt, :])
                    for slot, k0, lo, cnt in key_dst(c0, n):
                        P.dma("gpsimd", KP[slot, :, k0:k0 + cnt], kpeb[:, lo:lo + cnt])
                    mla_kprep(ckvb, kptok[0:n, :], n, key_dst(c0, n), c0 // 128)
                P.emit()


        def m1b_phase():
            with ExitStack() as st:
                def sb(name, shape, dt=F32):
                    return TT(st.enter_context(nc.sbuf_tensor(name + "_m1b", shape, dt)))
                conv_cache(sb, cfk, cfv, KF, VF, "f")
                KAs = {kd: [sb("KA%s%d" % (kd, k), [96, TK], BF16) for k in range(2)] for kd in ("f", "m")}
                QAs = {kd: [sb("QA%s%d" % (kd, k), [96, LP], BF16) for k in range(2)] for kd in ("f", "m")}
                Vh = [sb("Vh%d" % k, [128, 33 * 65], BF16) for k in range(2)]
                ptb = [sb("ptb%d" % k, [128, 512], BF16) for k in range(4)]
                tmpf = [sb("tmpf%d" % k, [128, 128]) for k in range(2)]
                os_ = sb("os", [65, 512]); rr = sb("rr", [64, 512]); ob = sb("ob", [64, 512], BF16)
                cfb = sb("cfb", [128, 128]); cfs = sb("cfs", [16, 16]); cmb = sb("cmb", [128, 128])
                P.dma("sync", cfb[:, :], cfbig[:, :]); P.dma("sync", cfs[:, :], cfsm[:, :]); P.dma("sync", cmb[:, :], cmbig[:, :])
                for k in range(2):
                    P.memset("vector", KAs["f"][k][64:96, :], 1.0)
                    P.memset("vector", QAs["f"][k][64:96, :], 1.0)
                    P.memset("vector", Vh[k][:, :], 1.0)
                it = 0
                sctr = [0]
                for kd in ("f", "m"):
                    RW = 70 if kd == "f" else 96
                    KSRC = KF if kd == "f" else KM
                    VSRC = VF if kd == "f" else VM
                    for slot in range(1 + NS):
                        for hh in range(8):
                            ka = KAs[kd][it % 2]; qa = QAs[kd][it % 2]; vh = Vh[it % 2]
                            vh3 = TT(vh.t[:, :].rearrange("p (b d) -> p b d", d=65), vh.b)
                            it += 1
                            hp, hl = hh // 2, 64 * (hh % 2)
                            P.dma("sync", ka[0:64, :], KSRC[slot, hp, hl:hl + 64, :])
                            if slot == 0:
                                nq_all, xcol0 = LP, 0
                                qtiles = [(512 * Q, 512, Q) for Q in range(8)] + [(4096, 16, "meta")]
                            else:
                                nq_all, xcol0 = 16, LP + 16 * (slot - 1)
                                qtiles = [(0, 16, "sample")]
                            if kd == "f":
                                P.dma("sync", ka[64:67, :], FK[slot, hh, :, :])
                                P.dma("sync", qa[0:64, 0:nq_all], QF[hp, hl:hl + 64, xcol0:xcol0 + nq_all])
                                P.dma("sync", qa[67:70, 0:nq_all], FQ[hh, :, xcol0:xcol0 + nq_all])
                            else:
                                P.dma("sync", ka[64:96, :], KP[slot, :, :])
                                P.dma("sync", qa[0:96, 0:nq_all], QM[hh, :, xcol0:xcol0 + nq_all])
                            P.dma("sync", vh3[:, 0:32, 0:64], R(VSRC.t[slot, 0:4096, hh * 64:(hh + 1) * 64].rearrange("(b p) d -> p b d", p=128), VSRC.b))
                            P.dma("sync", vh3[0:16, 32, 0:64], VSRC[slot, 4096:4112, hh * 64:(hh + 1) * 64])
                            for qi, (q0, nq, Q) in enumerate(qtiles):
                                blocks = attn_blocks(slot, Q)
                                if kd == "m":
                                    blocks = [(a, b, c, d, "full" if e == "diag16" else e) for (a, b, c, d, e) in blocks]
                                nb = len(blocks)
                                pO = ps[4 + qi % 2]
                                if Q == "sample" and kd == "f":
                                    groups = [blocks[g8 * 8:(g8 + 1) * 8] for g8 in range(4)] + [[blocks[32]]]
                                else:
                                    groups = [[blk_] for blk_ in blocks]
                                ng = len(groups)
                                ust = {}

                                def qk(gi):
                                    grp = groups[gi]
                                    pS_ = ps[sctr[0] % 4]; pt_ = ptb[sctr[0] % 4]; tf = tmpf[sctr[0] % 2]
                                    sctr[0] += 1
                                    ust[gi] = (pS_, pt_, tf)
                                    if len(grp) > 1:
                                        for bb, (k0, nk, jb, qs, kind) in enumerate(grp):
                                            P.mm(pS_[0:nk, bb * 16:(bb + 1) * 16], ka[0:RW, k0:k0 + nk], qa[0:RW, q0:q0 + 16])
                                    else:
                                        (k0, nk, jb, qs, kind) = grp[0]
                                        P.mm(pS_[0:nk, qs:nq], ka[0:RW, k0:k0 + nk], qa[0:RW, q0 + qs:q0 + nq])

                                def rest(gi):
                                    grp = groups[gi]
                                    pS_, pt_, tf = ust.pop(gi)
                                    first_, last_ = gi == 0, gi == ng - 1
                                    if len(grp) > 1:
                                        P.act(pt_[0:128, 0:128], pS_[0:128, 0:128], AF.Exp, scale=0.125)
                                        for bb, (k0, nk, jb, qs, kind) in enumerate(grp):
                                            P.mm(pO[0:65, 0:16], vh3[0:nk, jb, :], pt_[0:nk, bb * 16:(bb + 1) * 16], first_ and bb == 0, False)
                                        return
                                    (k0, nk, jb, qs, kind) = grp[0]
                                    if kd == "f":
                                        sc = 0.125
                                    else:
                                        sc = ksc_all[0:nk, slot, jb, hh:hh + 1]
                                    if kind == "diag128":
                                        P.stt(tf[0:128, 0:128], pS_[0:128, qs:qs + 128], sc, (cfb if kd == "f" else cmb)[:, :], ALU.mult, ALU.add)
                                        P.act(pt_[0:128, qs:qs + 128], tf[0:128, 0:128], AF.Exp)
                                        if qs + 128 < nq:
                                            P.act(pt_[0:128, qs + 128:nq], pS_[0:128, qs + 128:nq], AF.Exp, scale=sc)
                                    elif kind == "diag16":
                                        P.stt(tf[0:16, 0:16], pS_[0:16, 0:16], sc, cfs[:, :], ALU.mult, ALU.add)
                                        P.act(pt_[0:16, 0:16], tf[0:16, 0:16], AF.Exp)
                                    else:
                                        P.act(pt_[0:nk, qs:nq], pS_[0:nk, qs:nq], AF.Exp, scale=sc)
                                    P.mm(pO[0:65, qs:nq], vh3[0:nk, jb, :], pt_[0:nk, qs:nq], first_, last_)
                                qk(0)
                                if ng > 1:
                                    qk(1)
                                for gi in range(ng):
                                    if gi + 2 < ng:
                                        qk(gi + 2)
                                    rest(gi)
                                P.copy("vector", os_[:, 0:nq], pO[0:65, 0:nq])
                                P.mm(ps[6][0:64, 0:nq], esel[0:65, :], os_[0:65, 0:nq])
                                P.recip(rr[:, 0:nq], ps[6][0:64, 0:nq])
                                P.tt("vector", ob[:, 0:nq], os_[0:64, 0:nq], rr[:, 0:nq], ALU.mult)
                                row0 = (0 if kd == "f" else 512) + 64 * hh
                                xc0 = xcol0 + q0
                                P.dma("gpsimd", mixin[row0:row0 + 64, xc0:xc0 + nq], ob[:, 0:nq])
                P.emit()

        import os
        stop = os.environ.get("MK_STOP", "")
        ffn_phase(0, 0, True, False)
        m0a_phase()
        if stop == "M0A0":
            return nc
        if stop == "M0A":
            dump_x_phase()
            return nc
        m0b_phase()
        if stop == "M0B":
            dump_x_phase()
            return nc
        mixout_phase(abwo, "0")
        if stop == "L0":
            dump_x_phase()
            return nc
        ffn_phase(0, 1, False, False)
        ffn_phase(1, 0, False, False)
        m1a_phase()
        if stop == "M1A":
            ffn_phase(1, 1, False, True)
            return nc
        m1b_phase()
        mixout_phase(cdwo, "1")
        ffn_phase(1, 1, False, True)
    return nc


def _host_consts():
    c = {}
    c["ident"] = np.eye(128, dtype=np.float32)
    slopes = np.exp2(-8.0 * np.arange(1, 5, dtype=np.float32) / 4).astype(np.float32)
    kl = np.arange(128)[:, None]
    ql = np.arange(128)[None, :]
    cd = np.zeros((128, 4, 128), np.float32)
    for h in range(4):
        cd[:, h, :] = np.where((kl // 64) <= (ql // 64), -2.0 * slopes[h] * np.maximum(kl - ql, 0), NEG)
    c["cdbig"] = cd.reshape(128, 512)
    k16 = np.arange(16)[:, None]
    q16 = np.arange(16)[None, :]
    cs = np.zeros((16, 4, 16), np.float32)
    for h in range(4):
        cs[:, h, :] = -2.0 * slopes[h] * np.maximum(k16 - q16, 0)
    c["cdsm"] = cs.reshape(16, 64)
    c["cfbig"] = np.where(kl <= ql, 0.0, NEG).astype(np.float32)
    c["cfsm"] = np.where(k16 <= q16, 0.0, NEG).astype(np.float32)
    c["cmbig"] = np.where((kl // 64) <= (ql // 64), 0.0, NEG).astype(np.float32)
    pos_p = np.concatenate([16 + np.arange(SEQ), np.arange(NMETA)]).astype(np.float64)
    pos_s = np.arange(TK).astype(np.float64)
    augk = np.zeros((2, 4, 4, TK), np.float32)
    augq = np.zeros((2, 4, 4, TK), np.float32)
    for g, pos in enumerate((pos_p, pos_s)):
        a = np.floor(pos / 64)
        b = pos - 64 * a
        for h in range(4):
            augk[g, h, 0] = 8 * slopes[h] * 64 * a
            augk[g, h, 1] = 8 * slopes[h] * b
            augk[g, h, 2] = 1
            augk[g, h, 3] = 1
            augq[g, h, 0] = 1
            augq[g, h, 1] = 1
            augq[g, h, 2] = -8 * slopes[h] * 64 * a
            augq[g, h, 3] = -8 * slopes[h] * b
    c["augk"] = augk
    c["augq"] = augq
    posx = np.concatenate([16 + np.arange(SEQ), np.arange(NMETA), np.tile(PAST + np.arange(DSEQ), NS)]).astype(np.float32)
    inv = (np.float32(10000.0) ** (-np.arange(16, dtype=np.float32) / np.float32(16))).astype(np.float32)
    ang = (posx[None, :] * inv[:, None]).astype(np.float32)
    cs_, sn_ = np.cos(ang).astype(np.float32), np.sin(ang).astype(np.float32)
    c["cos32"] = np.concatenate([cs_, cs_], 0)
    c["sin32"] = np.concatenate([-sn_, sn_], 0)
    c["cos96"] = np.concatenate([np.ones((64, T), np.float32), cs_, cs_], 0)
    c["sin96"] = np.concatenate([np.zeros((64, T), np.float32), -sn_, sn_], 0)
    return c


def _pack_prm(inp):
    prm = np.zeros((128, NPRM), np.float32)

    def put(name, arr):
        arr = np.asarray(arr, np.float32)
        prm[:arr.shape[0], PRM_OFF[name]:PRM_OFF[name] + arr.shape[1]] = arr
    for l in range(2):
        for i in range(2):
            put("ffn_norm%d%d" % (l, i), _cols(inp["ffn_norm"][l, i], 8))
        put("mix_norm%d" % l, _cols(inp["mix_norm"][l], 8))
    put("a_re", _cols(inp["s5_a_re"][0].reshape(-1), 16))
    put("a_im", _cols(inp["s5_a_im"][0].reshape(-1), 16))
    put("lstep", _cols(np.repeat(inp["s5_log_step"][0], 64), 16))
    put("s5d", _cols(inp["s5_d"][0], 4))
    put("glub", _cols(inp["s5_glu_b"][0], 4))
    put("gq", np.tile(inp["diff_q_norm"][0], 2)[:, None])
    put("gk", np.tile(inp["diff_k_norm"][0], 2)[:, None])
    put("gsub", inp["diff_sub_norm"][0][:, None])
    put("lamv", np.ascontiguousarray(inp["diff_lam"][0].T))
    put("fgq", np.tile(inp["fox_q_norm"][0], 2)[:, None])
    put("fgk", np.tile(inp["fox_k_norm"][0], 2)[:, None])
    put("fbias", inp["fox_f_bias"][0][:, None])
    put("qan", _cols(inp["mla_q_a_norm"][0], 3))
    put("kvan", _cols(inp["mla_kv_a_norm"][0], 2))
    put("mqn", inp["mla_q_norm"][0][:, None])
    put("mkn", inp["mla_k_norm"][0][:, None])
    return prm


def _s5_mats(inp):
    b_re, b_im = np.asarray(inp["s5_b_re"][0]), np.asarray(inp["s5_b_im"][0])
    c_re, c_im = np.asarray(inp["s5_c_re"][0]), np.asarray(inp["s5_c_im"][0])
    out = {k: np.zeros((16, 128, 128), np.float32) for k in ("bre", "bim", "cre", "cim")}
    for j in range(16):
        for gl in range(2):
            g = 2 * j + gl
            f0 = 32 * (j % 4) + 16 * gl
            out["bre"][j, f0:f0 + 16, 64 * gl:64 * gl + 64] = b_re[g].T
            out["bim"][j, f0:f0 + 16, 64 * gl:64 * gl + 64] = b_im[g].T
            out["cre"][j, 64 * gl:64 * gl + 64, f0:f0 + 16] = c_re[g].T
            out["cim"][j, 64 * gl:64 * gl + 64, f0:f0 + 16] = c_im[g].T
    return out


_NC_CACHE = {}


def kernel(**inp):
    inp = {k: np.asarray(v) for k, v in inp.items()}
    if "nc" not in _NC_CACHE:
        _NC_CACHE["nc"] = build_program()
    nc = _NC_CACHE["nc"]
    consts = _host_consts()
    prm = _pack_prm(inp)
    s5m = _s5_mats(inp)
    qb = np.ascontiguousarray(inp["mla_q_b"][0])
    qbr = qb.reshape(384, 8, 96).copy()
    x1 = qbr[:, :, 64:80].copy()
    x2 = qbr[:, :, 80:96].copy()
    qbr[:, :, 64:80] = x2
    qbr[:, :, 80:96] = x1
    qbr = np.ascontiguousarray(qbr.reshape(384, 768))
    kvb = inp["mla_kv_b"][0].reshape(256, 8, 128)
    kvb = np.ascontiguousarray(np.concatenate([kvb[:, :, :64].reshape(256, 512), kvb[:, :, 64:].reshape(256, 512)], 1))
    shared = dict(
        meta=inp["meta_tokens"], wi=inp["ffn_w_in"], wo=inp["ffn_w_out"],
        abwi=inp["ab_w_in"][0], abwo=inp["ab_w_out"][0], cdwi=inp["cd_w_in"][0], cdwo=inp["cd_w_out"][0],
        gluw=inp["s5_glu_w"][0], qb=qb, qbr=qbr, kvb=kvb, prm=prm, **s5m, **consts)
    shared = {k: np.ascontiguousarray(v, dtype=np.float32) for k, v in shared.items()}
    in_maps = []
    for c in range(NCORES):
        sl = slice(NS * c, NS * (c + 1))
        st = np.stack([inp["state_s5_re"][0, sl], inp["state_s5_im"][0, sl]], 1)
        st = st.reshape(NS, 2, 16, 128).transpose(3, 0, 1, 2).reshape(128, NS * 2 * 16)
        m = dict(shared)
        m.update(
            xp=inp["x_prompt"][c], xs=inp["x_sample"][sl].reshape(NS * DSEQ, D), st0=np.ascontiguousarray(st),
            cdk=inp["cache_diff_k"][0, sl].reshape(NS, PAST, 512), cdv=inp["cache_diff_v"][0, sl].reshape(NS, PAST, 512),
            cfk=inp["cache_fox_k"][0, sl].reshape(NS, PAST, 512), cfv=inp["cache_fox_v"][0, sl].reshape(NS, PAST, 512),
            clf=inp["cache_fox_logf"][0, sl], cckv=inp["cache_mla_ckv"][0, sl], ckpe=inp["cache_mla_kpe"][0, sl])
        in_maps.append({k: np.ascontiguousarray(v, dtype=np.float32) for k, v in m.items()})
    res = run_bass_kernel_spmd(nc, in_maps, core_ids=list(range(NCORES)))
    rs = res.results

    def cat(name, shape):
        return np.stack([np.asarray(r[name]) for r in rs]).reshape(shape).astype(np.float32)

    def st_out(a):
        return np.ascontiguousarray(np.swapaxes(a, -1, -2)).reshape(a.shape[:-2] + (32, 64))
    y_prompt = cat("yp", (8, SEQ, D))
    y_sample = cat("ys", (32, DSEQ, D))
    s5pp = cat("s5p", (8, 2, 128, 16))
    s5ss = cat("s5s", (32, 2, 128, 16))
    outs = [y_prompt, y_sample,
            st_out(s5pp[:, 0])[None], st_out(s5pp[:, 1])[None],
            cat("dkp_o", (1, 8, LP, 4, 128)), cat("dvp_o", (1, 8, LP, 4, 128)),
            cat("fkp_o", (1, 8, LP, 8, 64)), cat("fvp_o", (1, 8, LP, 8, 64)),
            cat("lfp_o", (1, 8, LP, 8)), cat("ckvp_o", (1, 8, LP, 256)), cat("kpep_o", (1, 8, LP, 32)),
            st_out(s5ss[:, 0])[None], st_out(s5ss[:, 1])[None],
            cat("dks_o", (1, 32, DSEQ, 4, 128)), cat("dvs_o", (1, 32, DSEQ, 4, 128)),
            cat("fks_o", (1, 32, DSEQ, 8, 64)), cat("fvs_o", (1, 32, DSEQ, 8, 64)),
            cat("lfs_o", (1, 32, DSEQ, 8)), cat("ckvs_o", (1, 32, DSEQ, 256)), cat("kpes_o", (1, 32, DSEQ, 32))]
    return tuple(outs)
```
